# Optimizing a Trainium2 kernel written in Bass

```python
import jax, jax.numpy as jnp
from jax import lax
import numpy as np

D_MODEL = 1024
BATCH = 2
SEQ = 8192
DEPTH = 1

GRID_W = 64
CTX_LEN = 256
EPS = 1e-6
LRU_WIDTH = 1024
LRU_HEADS = 16
LRU_HEAD_DIM = LRU_WIDTH // LRU_HEADS
LRU_CONV = 4
LRU_C = 8.0
SSD_WIDTH = 1024
SSD_HEAD_DIM = 64
SSD_HEADS = SSD_WIDTH // SSD_HEAD_DIM
SSD_GROUPS = 2
SSD_STATE = 128
SSD_CONV = 4
SSD_CHUNK = 128
SSD_CONV_DIM = SSD_WIDTH + 2 * SSD_GROUPS * SSD_STATE
D_MIX = LRU_WIDTH + SSD_WIDTH
D_IN = 2 * LRU_WIDTH + SSD_WIDTH + SSD_CONV_DIM + 2 * SSD_HEADS
D_FF = 3 * D_MODEL
FFN_CONV = 3

kernel_name = "hybrid_rglru_ssd_convffn_prefix_ctx"


def rms_norm(x, g):
    xf = x.astype(jnp.float32)
    y = xf * lax.rsqrt(jnp.mean(xf * xf, axis=-1, keepdims=True) + EPS)
    return (y * g.astype(jnp.float32)).astype(x.dtype)


def modulate(x, shift, scale):
    return x * (1 + scale) + shift


def dw_conv(x, w, b):
    k = w.shape[0]
    pad_l = k // 2
    y = lax.conv_general_dilated(
        x, w[:, None, :].astype(x.dtype), window_strides=(1,),
        padding=[(pad_l, k - 1 - pad_l)], dimension_numbers=('NWC', 'WIO', 'NWC'),
        feature_group_count=x.shape[-1])
    return y + b.astype(x.dtype)


def _flip(t, rev):
    return jnp.flip(t, axis=1) if rev else t


def raster_to_column(x):
    b, l, ch = x.shape
    rows = l // GRID_W
    return x.reshape(b, rows, GRID_W, ch).transpose(0, 2, 1, 3).reshape(b, l, ch)


def column_to_raster(x):
    b, l, ch = x.shape
    rows = l // GRID_W
    return x.reshape(b, GRID_W, rows, ch).transpose(0, 2, 1, 3).reshape(b, l, ch)


def linear_scan(a, u, h0):
    u = u.at[:, 0].add(a[:, 0] * h0)

    def combine(left, right):
        a_l, u_l = left
        a_r, u_r = right
        return a_l * a_r, a_r * u_l + u_r

    return lax.associative_scan(combine, (a, u), axis=1)[1]


def rg_lru_scan(xc, wa, ba, wx, bx, lam, h0):
    b, l, w = xc.shape
    f32 = jnp.float32
    xf = xc.astype(f32)
    xh = xf.reshape(b, l, LRU_HEADS, LRU_HEAD_DIM)
    gate_r = jax.nn.sigmoid(jnp.einsum('blhi,hij->blhj', xh, wa.astype(f32)).reshape(b, l, w) + ba.astype(f32))
    gate_i = jax.nn.sigmoid(jnp.einsum('blhi,hij->blhj', xh, wx.astype(f32)).reshape(b, l, w) + bx.astype(f32))
    log_a = -LRU_C * gate_r * jax.nn.softplus(-lam.astype(f32))
    a = jnp.exp(log_a)
    u = jnp.sqrt(-jnp.expm1(2.0 * log_a)) * (gate_i * xf)
    return linear_scan(a, u, h0)


def rglru_mixer(u_lat, gate_lat, u_ctx, gate_ctx, conv_w, conv_b, wa, ba, wx, bx, lam, with_ctx_out):
    xc_lat = dw_conv(u_lat, conv_w, conv_b)
    xc_ctx = dw_conv(u_ctx, conv_w, conv_b)
    zero = jnp.zeros((u_lat.shape[0], LRU_WIDTH), jnp.float32)
    y_lat = 0.0
    h_ctx_dirs = []
    for d in (0, 1):
        rev = d == 1
        h_ctx = rg_lru_scan(_flip(xc_ctx, rev), wa[d], ba[d], wx[d], bx[d], lam[d], zero)
        h_lat = rg_lru_scan(_flip(xc_lat, rev), wa[d], ba[d], wx[d], bx[d], lam[d], h_ctx[:, -1])
        y_lat = y_lat + _flip(h_lat, rev)
        h_ctx_dirs.append(_flip(h_ctx, rev))
    out_lat = y_lat.astype(u_lat.dtype) * jax.nn.gelu(gate_lat)
    out_ctx = None
    if with_ctx_out:
        out_ctx = (h_ctx_dirs[0] + h_ctx_dirs[1]).astype(u_ctx.dtype) * jax.nn.gelu(gate_ctx)
    return out_lat, out_ctx


def segsum(x):
    t = x.shape[-1]
    x_rep = jnp.broadcast_to(x[..., :, None], x.shape + (t,))
    x_rep = jnp.where(jnp.tril(jnp.ones((t, t), dtype=bool), -1), x_rep, 0.0)
    ss = jnp.cumsum(x_rep, axis=-2)
    return jnp.where(jnp.tril(jnp.ones((t, t), dtype=bool)), ss, -jnp.inf)


def ssd_chunked(x, dt, a, bm, cm, h0):
    b, l, h, p = x.shape
    g, n = bm.shape[-2:]
    r = h // g
    t = SSD_CHUNK
    nc = l // t
    xs = (x * dt[..., None]).reshape(b, nc, t, g, r, p)
    da = (dt * a).reshape(b, nc, t, g, r).transpose(0, 3, 4, 1, 2)
    bm = bm.reshape(b, nc, t, g, n)
    cm = cm.reshape(b, nc, t, g, n)
    da_cs = jnp.cumsum(da, axis=-1)
    decay = jnp.exp(segsum(da))
    scores = jnp.einsum('bctgn,bcsgn->bgcts', cm, bm)
    y_diag = jnp.einsum('bgcts,bgrcts,bcsgrp->bctgrp', scores, decay, xs)
    decay_states = jnp.exp(da_cs[..., -1:] - da_cs)
    states = jnp.einsum('bcsgn,bgrcs,bcsgrp->bcgrpn', bm, decay_states, xs)
    chunk_decay = jnp.exp(da_cs[..., -1]).transpose(0, 3, 1, 2)
    h0g = h0.reshape(b, g, r, p, n)
    h_end = linear_scan(chunk_decay[..., None, None], states, h0g)
    h_start = jnp.concatenate([h0g[:, None], h_end[:, :-1]], axis=1)
    y_off = jnp.einsum('bctgn,bcgrpn,bgrct->bctgrp', cm, h_start, jnp.exp(da_cs))
    y = (y_diag + y_off).reshape(b, l, h, p)
    return y, h_end[:, -1].reshape(b, h, p, n)


def ssd_scan(xh, dt_raw, bm, cm, a_log, dt_bias, h0, rev):
    f32 = jnp.float32
    xh, dt_raw, bm, cm = (_flip(v, rev) for v in (xh, dt_raw, bm, cm))
    dt = jax.nn.softplus(dt_raw.astype(f32) + dt_bias.astype(f32))
    y, h_fin = ssd_chunked(xh.astype(f32), dt, -jnp.exp(a_log.astype(f32)), bm.astype(f32), cm.astype(f32), h0)
    return _flip(y, rev), h_fin


def _ssd_split(xbc):
    b, l, _ = xbc.shape
    xs, bm, cm = jnp.split(xbc, [SSD_WIDTH, SSD_WIDTH + SSD_GROUPS * SSD_STATE], axis=-1)
    return (xs.reshape(b, l, SSD_HEADS, SSD_HEAD_DIM),
            bm.reshape(b, l, SSD_GROUPS, SSD_STATE),
            cm.reshape(b, l, SSD_GROUPS, SSD_STATE))


def ssd_mixer(xbc_lat, dt_lat, z_lat, xbc_ctx, dt_ctx, z_ctx, conv_w, conv_b, a_log, dt_bias, d_skip, norm_g, with_ctx_out):
    b, l, _ = xbc_lat.shape
    xbc_lat = raster_to_column(xbc_lat)
    dt_lat = raster_to_column(dt_lat)
    xl, bl, cl = _ssd_split(jax.nn.silu(dw_conv(xbc_lat, conv_w, conv_b)))
    xc, bc, cc = _ssd_split(jax.nn.silu(dw_conv(xbc_ctx, conv_w, conv_b)))
    h0 = jnp.zeros((b, SSD_HEADS, SSD_HEAD_DIM, SSD_STATE), jnp.float32)
    dsk = d_skip.astype(jnp.float32)[:, None]
    y_lat = xl.astype(jnp.float32) * dsk
    y_ctx = xc.astype(jnp.float32) * dsk
    for d in (0, 1):
        rev = d == 1
        cols = slice(d * SSD_HEADS, (d + 1) * SSD_HEADS)
        yc, hc = ssd_scan(xc, dt_ctx[..., cols], bc, cc, a_log[d], dt_bias[d], h0, rev)
        yl, _ = ssd_scan(xl, dt_lat[..., cols], bl, cl, a_log[d], dt_bias[d], hc, rev)
        y_lat = y_lat + yl
        y_ctx = y_ctx + yc
    y_lat = column_to_raster(y_lat.reshape(b, l, SSD_WIDTH)).astype(z_lat.dtype)
    out_lat = rms_norm(y_lat * jax.nn.silu(z_lat), norm_g)
    out_ctx = None
    if with_ctx_out:
        y_ctx = y_ctx.reshape(b, xbc_ctx.shape[1], SSD_WIDTH).astype(z_ctx.dtype)
        out_ctx = rms_norm(y_ctx * jax.nn.silu(z_ctx), norm_g)
    return out_lat, out_ctx


def conv_ffn(h, w_up, conv_w, conv_b, w_down):
    u = dw_conv(h @ w_up, conv_w, conv_b)
    val, gate = jnp.split(u, 2, axis=-1)
    return (jax.nn.gelu(gate) * val) @ w_down


def _split_in_proj(p):
    i1 = LRU_WIDTH
    i2 = i1 + LRU_WIDTH
    i3 = i2 + SSD_WIDTH
    i4 = i3 + SSD_CONV_DIM
    return jnp.split(p, [i1, i2, i3, i4], axis=-1)


def hybrid_layer(x, xc, mod_lat, mod_ctx, lp, update_ctx):
    sh1, sc1, g1, sh2, sc2, g2 = jnp.split(mod_lat, 6, axis=-1)
    csh1, csc1, cg1, csh2, csc2, cg2 = jnp.split(mod_ctx, 6, axis=-1)
    h_lat = modulate(rms_norm(x, lp['norm1_g']), sh1, sc1)
    h_ctx = modulate(rms_norm(xc, lp['norm1_g']), csh1, csc1)
    lu_l, lg_l, z_l, xbc_l, dt_l = _split_in_proj(h_lat @ lp['w_in'])
    lu_c, lg_c, z_c, xbc_c, dt_c = _split_in_proj(h_ctx @ lp['w_in'])
    lru_l, lru_c = rglru_mixer(lu_l, lg_l, lu_c, lg_c, lp['lru_conv_w'], lp['lru_conv_b'],
                               lp['lru_wa'], lp['lru_ba'], lp['lru_wx'], lp['lru_bx'], lp['lru_lambda'], update_ctx)
    ssd_l, ssd_c = ssd_mixer(xbc_l, dt_l, z_l, xbc_c, dt_c, z_c, lp['ssd_conv_w'], lp['ssd_conv_b'],
                             lp['ssd_a_log'], lp['ssd_dt_bias'], lp['ssd_d'], lp['ssd_norm_g'], update_ctx)
    x = x + g1 * (jnp.concatenate([lru_l, ssd_l], axis=-1) @ lp['w_out'])
    f_lat = modulate(rms_norm(x, lp['norm2_g']), sh2, sc2)
    x = x + g2 * conv_ffn(f_lat, lp['ffn_w_up'], lp['ffn_conv_w'], lp['ffn_conv_b'], lp['ffn_w_down'])
    if update_ctx:
        xc = xc + cg1 * (jnp.concatenate([lru_c, ssd_c], axis=-1) @ lp['w_out'])
        f_ctx = modulate(rms_norm(xc, lp['norm2_g']), csh2, csc2)
        xc = xc + cg2 * conv_ffn(f_ctx, lp['ffn_w_up'], lp['ffn_conv_w'], lp['ffn_conv_b'], lp['ffn_w_down'])
    return x, xc


def setup_inputs(seed: int = 0) -> dict:
    key = jax.random.key(seed)
    ks = iter(jax.random.split(key, 40))
    f32 = jnp.float32
    nrm = lambda shape, s: jax.random.normal(next(ks), shape, f32) * s
    L = DEPTH
    a_c = jax.random.uniform(next(ks), (L, 2, LRU_WIDTH), f32, 0.9, 0.999)
    s = a_c ** (1.0 / LRU_C)
    lru_lambda = jnp.log(s) - jnp.log1p(-s)
    ssd_a_log = jnp.log(jax.random.uniform(next(ks), (L, 2, SSD_HEADS), f32, 1.0, 16.0))
    dt0 = jnp.exp(jax.random.uniform(next(ks), (L, 2, SSD_HEADS), f32, np.log(1e-3), np.log(0.1)))
    ssd_dt_bias = dt0 + jnp.log(-jnp.expm1(-dt0))
    return {
        'x': nrm((BATCH, SEQ, D_MODEL), 1.0),
        'c': nrm((BATCH, D_MODEL), 1.0),
        'ctx': nrm((BATCH, CTX_LEN, D_MODEL), 1.0),
        'c_ctx': nrm((D_MODEL,), 1.0),
        'ada_w': nrm((L, D_MODEL, 6 * D_MODEL), 0.5 * D_MODEL ** -0.5),
        'ada_b': nrm((L, 6 * D_MODEL), 0.01),
        'norm1_g': 1.0 + nrm((L, D_MODEL), 0.02),
        'w_in': nrm((L, D_MODEL, D_IN), D_MODEL ** -0.5),
        'lru_conv_w': nrm((L, LRU_CONV, LRU_WIDTH), LRU_CONV ** -0.5),
        'lru_conv_b': nrm((L, LRU_WIDTH), 0.01),
        'lru_wa': nrm((L, 2, LRU_HEADS, LRU_HEAD_DIM, LRU_HEAD_DIM), LRU_HEAD_DIM ** -0.5),
        'lru_ba': nrm((L, 2, LRU_WIDTH), 0.01),
        'lru_wx': nrm((L, 2, LRU_HEADS, LRU_HEAD_DIM, LRU_HEAD_DIM), LRU_HEAD_DIM ** -0.5),
        'lru_bx': nrm((L, 2, LRU_WIDTH), 0.01),
        'lru_lambda': lru_lambda,
        'ssd_conv_w': nrm((L, SSD_CONV, SSD_CONV_DIM), SSD_CONV ** -0.5),
        'ssd_conv_b': nrm((L, SSD_CONV_DIM), 0.01),
        'ssd_a_log': ssd_a_log,
        'ssd_dt_bias': ssd_dt_bias,
        'ssd_d': 1.0 + nrm((L, SSD_HEADS), 0.1),
        'ssd_norm_g': 1.0 + nrm((L, SSD_WIDTH), 0.02),
        'w_out': nrm((L, D_MIX, D_MODEL), D_MIX ** -0.5),
        'norm2_g': 1.0 + nrm((L, D_MODEL), 0.02),
        'ffn_w_up': nrm((L, D_MODEL, 2 * D_FF), D_MODEL ** -0.5),
        'ffn_conv_w': nrm((L, FFN_CONV, 2 * D_FF), FFN_CONV ** -0.5),
        'ffn_conv_b': nrm((L, 2 * D_FF), 0.01),
        'ffn_w_down': nrm((L, D_FF, D_MODEL), D_FF ** -0.5),
        'final_norm_g': 1.0 + nrm((D_MODEL,), 0.02),
    }


def reference(x, c, ctx, c_ctx, ada_w, ada_b, norm1_g, w_in, lru_conv_w, lru_conv_b, lru_wa, lru_ba,
              lru_wx, lru_bx, lru_lambda, ssd_conv_w, ssd_conv_b, ssd_a_log, ssd_dt_bias, ssd_d,
              ssd_norm_g, w_out, norm2_g, ffn_w_up, ffn_conv_w, ffn_conv_b, ffn_w_down, final_norm_g):
    silu_c = jax.nn.silu(c)
    silu_cc = jax.nn.silu(c_ctx)
    xc = ctx
    for i in range(DEPTH):
        mod_lat = (silu_c @ ada_w[i] + ada_b[i])[:, None, :]
        mod_ctx = silu_cc @ ada_w[i] + ada_b[i]
        lp = dict(norm1_g=norm1_g[i], w_in=w_in[i], lru_conv_w=lru_conv_w[i], lru_conv_b=lru_conv_b[i],
                  lru_wa=lru_wa[i], lru_ba=lru_ba[i], lru_wx=lru_wx[i], lru_bx=lru_bx[i],
                  lru_lambda=lru_lambda[i], ssd_conv_w=ssd_conv_w[i], ssd_conv_b=ssd_conv_b[i],
                  ssd_a_log=ssd_a_log[i], ssd_dt_bias=ssd_dt_bias[i], ssd_d=ssd_d[i],
                  ssd_norm_g=ssd_norm_g[i], w_out=w_out[i], norm2_g=norm2_g[i], ffn_w_up=ffn_w_up[i],
                  ffn_conv_w=ffn_conv_w[i], ffn_conv_b=ffn_conv_b[i], ffn_w_down=ffn_w_down[i])
        x, xc = hybrid_layer(x, xc, mod_lat, mod_ctx, lp, i < DEPTH - 1)
    return rms_norm(x, final_norm_g)
```

```python
import numpy as np
from contextlib import ExitStack, contextmanager
import concourse.bass as bass
import concourse.mybir as mybir
from concourse.bass_utils import run_bass_kernel_spmd

F32 = mybir.dt.float32
BF16 = mybir.dt.bfloat16
AF = mybir.ActivationFunctionType
ALU = mybir.AluOpType
EPS = 1e-6
GC = 0.7978845608028654
DBG_KIND = "Internal"
SSD_LIMIT = (9, 66)


class Buf:
    __slots__ = ("t", "w", "r", "wx")

    def __init__(self, t):
        self.t = t
        self.w = None
        self.r = {}
        self.wx = []

    def __getitem__(self, idx):
        return self.t[idx]


class KB:
    NS = 8

    def __init__(self, nc, es):
        self.nc = nc
        self.es = es
        self.eng = {'pe': nc.tensor, 'act': nc.scalar, 'dve': nc.vector, 'pool': nc.gpsimd, 'sp': nc.sync}
        self.sem = {}
        self.cnt = {}
        self.known = {}
        for e in self.eng:
            self.sem[e] = es.enter_context(nc.semaphore("s_" + e))
            self.cnt[e] = 0
            self.known[e] = {}
        self.dsem = {}
        self.dcnt = {}
        for q in ('sp', 'pool', 'act'):
            self.dsem[q] = [es.enter_context(nc.semaphore(f"d_{q}{i}")) for i in range(self.NS)]
            self.dcnt[q] = 0
        self.nbuf = 0
        self.sfx = ""
        self.free_tok = {}
        self._scopes = {}

    @contextmanager
    def scope(self):
        with ExitStack() as s:
            self._scopes[id(s)] = []
            try:
                yield s
            finally:
                for b in self._scopes.pop(id(s)):
                    toks = list(b.r.items()) + list(b.wx)
                    if b.w is not None:
                        toks.append(b.w)
                    for key, val in toks:
                        if val > self.free_tok.get(key, 0):
                            self.free_tok[key] = val

    def _new(self, t, es):
        b = Buf(t)
        b.r = dict(self.free_tok)
        if es is not None and id(es) in self._scopes:
            self._scopes[id(es)].append(b)
        return b

    def sb(self, shape, dt, es=None, name=None):
        self.nbuf += 1
        t = (es or self.es).enter_context(self.nc.sbuf_tensor("S_" + (name or f"sb{self.nbuf}") + self.sfx, list(shape), dt))
        return self._new(t, es)

    def ps(self, shape, dt, es=None, name=None):
        self.nbuf += 1
        t = (es or self.es).enter_context(self.nc.psum_tensor("P_" + (name or f"ps{self.nbuf}") + self.sfx, list(shape), dt))
        return self._new(t, es)

    def _semh(self, key):
        return self.sem[key] if isinstance(key, str) else self.dsem[key[1]][key[2]]

    def _wait(self, e, deps):
        kn = self.known[e]
        best = {}
        for key, val in deps:
            if val > best.get(key, 0):
                best[key] = val
        for key, val in best.items():
            if key == e and e == 'pe':
                continue
            if kn.get(key, 0) >= val:
                continue
            self.eng[e].wait_ge(self._semh(key), val)
            kn[key] = val

    @staticmethod
    def _deps(reads, writes):
        deps = []
        for b in reads:
            if b.w is not None:
                deps.append(b.w)
            deps.extend(b.wx)
        for b in writes:
            if b.w is not None:
                deps.append(b.w)
            deps.extend(b.wx)
            deps.extend(b.r.items())
        return deps

    def op(self, e, fn, reads=(), writes=(), inc=True):
        self._wait(e, self._deps(reads, writes))
        ins = fn(self.eng[e])
        n = self.cnt[e] + 1
        if inc:
            ins.then_inc(self.sem[e], 1)
            self.cnt[e] = n
        for b in reads:
            b.r[e] = n
        for b in writes:
            b.w = (e, n)
            b.wx = []
            b.r = {}
        return ins

    def dma(self, q, out, in_, reads=(), writes=(), merge=False):
        if merge:
            deps = self._deps(reads, [])
            for b in writes:
                deps.extend(b.r.items())
            self._wait(q, deps)
        else:
            self._wait(q, self._deps(reads, writes))
        i = self.dcnt[q]
        idx = i % self.NS
        rnd = i // self.NS
        self.dcnt[q] = i + 1
        key = ('d', q, idx)
        if rnd > 0:
            self._wait(q, [(key, 16 * rnd)])
        ins = self.eng[q].dma_start(out=out, in_=in_)
        ins.then_inc(self.dsem[q][idx], 16)
        val = 16 * (rnd + 1)
        for b in reads:
            b.r[key] = val
        for b in writes:
            if merge and b.w is not None:
                b.wx.append((key, val))
            else:
                b.w = (key, val)
                b.wx = []
                b.r = {}
        return ins

    def finish(self, bufs):
        deps = []
        for b in bufs:
            if b.w is not None:
                deps.append(b.w)
            deps.extend(b.wx)
        self._wait('sp', deps)

    def act(self, out, in_, func, reads, writes, bias=None, scale=None, accum_out=None):
        kw = {}
        if bias is not None:
            kw['bias'] = bias
        if scale is not None:
            kw['scale'] = scale
        if accum_out is not None:
            kw['accum_out'] = accum_out
        return self.op('act', lambda e: e.activation(out=out, in_=in_, func=func, **kw), reads, writes)

    def mm(self, out, lhsT, rhs, start, stop, reads, writes, inc=None):
        if inc is None:
            inc = stop
        return self.op('pe', lambda e: e.matmul(out, lhsT, rhs, start=start, stop=stop), reads, writes, inc=inc)

    def tr(self, out, in_, ident, reads, writes, inc=True):
        return self.op('pe', lambda e: e.transpose(out, in_, ident), reads, writes, inc=inc)


def bcast_rows(ap1d, n, parts=128):
    return bass.AP(ap1d.tensor, ap1d.offset, [[0, parts], [1, n]])


def dram(nc, name, shape, dt, kind="ExternalInput"):
    return nc.dram_tensor(name, list(shape), dt, kind=kind).ap()


def matvec_part(k, es, w_ap, ncols, vec, nv, out, bias=None, tag="mv"):
    with k.scope() as s:
        st = [k.sb([128, 8, 512], F32, s, f"{tag}_st{i}") for i in range(2)]
        pp = [k.ps([128, 4, nv], F32, s, f"{tag}_ps{i}") for i in range(2)]
        nch = (ncols + 511) // 512
        for c in range(nch):
            c0 = c * 512
            cw = min(512, ncols - c0)
            sb = st[c % 2]
            ps = pp[c % 2]
            k.dma('sp', sb[:, :, 0:cw], w_ap[:, c0:c0 + cw].rearrange("(j p) c -> p j c", p=128), writes=[sb])
            nb = cw // 128
            for m in range(nb):
                for j in range(8):
                    k.mm(ps[:, m, :], sb[:, j, m * 128:(m + 1) * 128], vec[:, j, :], start=(j == 0), stop=(j == 7),
                         reads=[sb, vec], writes=[ps])
            mb0 = c0 // 128
            if bias is None:
                k.op('dve', lambda e: e.tensor_copy(out=out[:, mb0:mb0 + nb, :], in_=ps[:, 0:nb, :]), [ps], [out])
            else:
                for v in range(nv):
                    k.op('dve', lambda e: e.tensor_tensor(out=out[:, mb0:mb0 + nb, v], in0=ps[:, 0:nb, v],
                                                          in1=bias[:, mb0:mb0 + nb], op=ALU.add), [ps, bias], [out])


def matvec_bcast(k, es, w_ap, ncols, vrep, out, bias_ap=None, tag="mb"):
    vb, vfn = vrep
    with k.scope() as s:
        st = [k.sb([128, 8, 512], F32, s, f"{tag}_st{i}") for i in range(2)]
        pp = [k.ps([128, 512], F32, s, f"{tag}_ps{i}") for i in range(2)]
        bb = None
        if bias_ap is not None:
            bb = k.sb([128, ncols], F32, s, f"{tag}_bias")
            k.dma('sp', bb[:, :], bcast_rows(bias_ap, ncols), writes=[bb])
        for c in range((ncols + 511) // 512):
            c0 = c * 512
            cw = min(512, ncols - c0)
            sb = st[c % 2]
            ps = pp[c % 2]
            k.dma('sp', sb[:, :, 0:cw], w_ap[:, c0:c0 + cw].rearrange("(j p) c -> p j c", p=128), writes=[sb])
            for j in range(8):
                k.mm(ps[:, 0:cw], vfn(j), sb[:, j, 0:cw], start=(j == 0), stop=(j == 7), reads=[sb, vb], writes=[ps])
            if bb is None:
                k.op('dve', lambda e: e.tensor_copy(out=out[:, c0:c0 + cw], in_=ps[:, 0:cw]), [ps], [out])
            else:
                k.op('dve', lambda e: e.tensor_tensor(out=out[:, c0:c0 + cw], in0=ps[:, 0:cw],
                                                      in1=bb[:, c0:c0 + cw], op=ALU.add), [ps, bb], [out])


def make_rep(k, es, identf, vec, v, tag):
    rep = k.sb([128, 8, 128], F32, es, tag)
    for j in range(8):
        k.op('dve', lambda e: e.tensor_scalar(out=rep[:, j, :], in0=identf[:, :], scalar1=0.0, scalar2=None,
                                              op0=ALU.mult), [identf], [rep])
        k.op('dve', lambda e: e.tensor_scalar(out=rep[:, j, :], in0=rep[:, j, :], scalar1=vec[:, j, v:v + 1], scalar2=None,
                                              op0=ALU.add), [rep, vec], [rep])
    return rep


def silu_vec(k, es, c_ap, nv, tag):
    raw = k.sb([128, 8, nv], F32, es, f"{tag}_raw")
    th = k.sb([128, 8, nv], F32, es, f"{tag}_th")
    out = k.sb([128, 8, nv], F32, es, f"{tag}_silu")
    for v, ap in enumerate(c_ap):
        k.dma('sp', raw[:, :, v], ap.rearrange("(j p) -> p j", p=128), writes=[raw])
    k.act(th[:, :, :], raw[:, :, :], AF.Tanh, [raw], [th], scale=0.5)
    k.op('dve', lambda e: e.scalar_tensor_tensor(out=out[:, :, :], in0=th[:, :, :], scalar=1.0, in1=raw[:, :, :],
                                                 op0=ALU.add, op1=ALU.mult), [th, raw], [out])
    k.op('dve', lambda e: e.tensor_scalar(out=out[:, :, :], in0=out[:, :, :], scalar1=0.5, scalar2=None,
                                          op0=ALU.mult), [out], [out])
    return out


def rstd_from_ss(k, ss, tmp, rstd, n):
    k.op('dve', lambda e: e.tensor_scalar(out=tmp[:, :], in0=ss[:, :], scalar1=1.0 / n, scalar2=EPS,
                                          op0=ALU.mult, op1=ALU.add), [ss], [tmp])
    k.act(tmp[:, :], tmp[:, :], AF.Sqrt, [tmp], [tmp])
    k.op('dve', lambda e: e.reciprocal(out=rstd[:, :], in_=tmp[:, :]), [tmp], [rstd])


NT2 = 2048


def build_phase2(nc, k, es, io, fz=None):
    mixl, mixs, x2, out2 = io.get('mixl'), io.get('mixs'), io['x2'], io['out2']
    xmid_d = [Buf(io['xmid'][t * 128:(t + 1) * 128, :]) for t in range(16)]

    identf = k.sb([128, 128], F32, es, "identf")
    identb = k.sb([128, 128], BF16, es, "identb")
    k.dma('sp', identf[:, :], io['ident'], writes=[identf])
    k.op('dve', lambda e: e.tensor_copy(out=identb[:, :], in_=identf[:, :]), [identf], [identb])
    hmask = k.sb([128, 2], F32, es, "hmask")
    k.dma('sp', hmask[:, :], io['hmask'], writes=[hmask])

    sc = silu_vec(k, es, [io['c_b']], 1, "c2")
    screp = make_rep(k, es, identf, sc, 0, "screp")
    srep = (screp, lambda j: screp[:, j, :])
    adaw, adab = io['ada_w'], io['ada_b']
    g1bc = k.sb([128, 1024], F32, es, "g1bc")
    g2bc = k.sb([128, 1024], F32, es, "g2bc")
    fgbc = k.sb([128, 1024], F32, es, "fgbc")
    matvec_bcast(k, es, adaw[:, 2048:3072], 1024, srep, g1bc, adab[2048:3072], "g1")
    matvec_bcast(k, es, adaw[:, 5120:6144], 1024, srep, g2bc, adab[5120:6144], "g2")
    k.op('dve', lambda e: e.tensor_scalar(out=g2bc[:, :], in0=g2bc[:, :], scalar1=0.5, scalar2=None, op0=ALU.mult),
         [g2bc], [g2bc])
    k.dma('sp', fgbc[:, :], bcast_rows(io['final_g'], 1024), writes=[fgbc])
    adab_p = k.sb([128, 16], F32, es, "adab_p")
    k.dma('sp', adab_p[:, :], adab[3072:5120].rearrange("(m p) -> p m", p=128), writes=[adab_p])
    shsc = k.sb([128, 16, 1], F32, es, "shsc")
    matvec_part(k, es, adaw[:, 3072:5120], 2048, sc, 1, shsc, adab_p, "shsc")
    n2g = k.sb([128, 8], F32, es, "n2g")
    k.dma('sp', n2g[:, :], io['norm2_g'].rearrange("(m p) -> p m", p=128), writes=[n2g])
    gs2 = k.sb([128, 8], F32, es, "gs2")
    k.op('dve', lambda e: e.scalar_tensor_tensor(out=gs2[:, :], in0=shsc[:, 8:16, 0], scalar=1.0, in1=n2g[:, :],
                                                 op0=ALU.add, op1=ALU.mult), [shsc, n2g], [gs2])
    sh2 = k.sb([128, 8, 1], F32, es, "sh2")
    k.op('dve', lambda e: e.tensor_copy(out=sh2[:, :, :], in_=shsc[:, 0:8, :]), [shsc], [sh2])
    bias2 = k.sb([128, 48, 1], F32, es, "bias2")
    matvec_part(k, es, io['w_up'], 6144, sh2, 1, bias2, None, "b2")
    cw = k.sb([128, 48, 3], F32, es, "cw")
    cb = k.sb([128, 48], F32, es, "cb")
    for tp in range(3):
        k.dma('sp', cw[:, :, tp], io['ffn_cw'][tp].rearrange("(m p) -> p m", p=128), writes=[cw])
    k.dma('sp', cb[:, :], io['ffn_cb'].rearrange("(m p) -> p m", p=128), writes=[cb])
    sng = k.sb([128, 8], F32, es, "sng")
    k.dma('sp', sng[:, :], io['ssd_norm_g'].rearrange("(m p) -> p m", p=128), writes=[sng])

    fT = k.sb([128, 8, 2050], BF16, es, "fT")

    with k.scope() as s2:
        wo = k.sb([128, 16, 1024], BF16, s2, "wo")
        wst = [k.sb([128, 1024], F32, s2, f"wst{i}") for i in range(2)]
        for kb in range(16):
            st = wst[kb % 2]
            k.dma('sp', st[:, :], io['w_out'][kb * 128:(kb + 1) * 128, :], writes=[st])
            if kb < 8:
                k.op('dve', lambda e: e.tensor_tensor(out=wo[:, kb, :], in0=st[:, :], in1=g1bc[:, :], op=ALU.mult),
                     [st, g1bc], [wo])
            else:
                k.op('dve', lambda e: e.scalar_tensor_tensor(out=wo[:, kb, :], in0=st[:, :], scalar=sng[:, kb - 8:kb - 7],
                                                             in1=g1bc[:, :], op0=ALU.mult, op1=ALU.mult),
                     [st, sng, g1bc], [wo])
        ml = [k.sb([128, 1024], BF16, s2, f"ml{i}") for i in range(2)]
        if fz is not None:
            cl = [[k.sb([128, 4, 256], BF16, s2, f"cl{i}_{kk}") for kk in range(4)] for i in range(2)]
            cs_ = [[k.sb([128, 4, 256], BF16, s2, f"cs{i}_{kk}") for kk in range(4)] for i in range(2)]
        ms = [k.sb([128, 1024], BF16, s2, f"ms{i}") for i in range(2)]
        xt = [k.sb([128, 1024], F32, s2, f"xt{i}") for i in range(2)]
        vs = [k.sb([128, 1024], BF16, s2, f"vs{i}") for i in range(2)]
        junk = [k.sb([128, 1024], BF16, s2, f"junk{i}") for i in range(2)]
        mixT = [k.sb([128, 16, 128], BF16, s2, f"mixT{i}") for i in range(2)]
        xm = [k.sb([128, 1024], F32, s2, f"xm{i}") for i in range(2)]
        xn = [k.sb([128, 1024], BF16, s2, f"xn{i}") for i in range(2)]
        st1 = [k.sb([128, 4], F32, s2, f"st1_{i}") for i in range(2)]
        st2 = [k.sb([128, 4], F32, s2, f"st2_{i}") for i in range(2)]
        psT = k.ps([128, 16, 128], BF16, s2, "psT")
        psO = k.ps([128, 1024], F32, s2, "psO")
        psF = k.ps([128, 8, 128], BF16, s2, "psF")
        for t in range(17):
            i = t % 2
            P = 128 if t < 16 else 2
            r0 = t * 128
            if fz is None:
                k.dma('sp', ml[i][0:P, :], mixl[r0:r0 + P, :], writes=[ml[i]])
                k.dma('sp', ms[i][0:P, :], mixs[r0:r0 + P, :], writes=[ms[i]])
            else:
                ol_all, os_all, pjoin, sel = fz
                for (dstb, srcall, cbufs) in ((ml[i], ol_all, cl[i]), (ms[i], os_all, cs_[i])):
                    for kk in range(4):
                        cb_ = cbufs[kk]
                        if t < 16:
                            rr = kk * 2048 + t * 128
                            k.dma('sp', cb_[:, :, :], srcall[:, rr:rr + 128, :].rearrange("q p c -> p q c"),
                                  reads=[pjoin], writes=[cb_])
                        else:
                            lrow = max(kk * 2048 - 1, 0)
                            rrow = min((kk + 1) * 2048, 8191)
                            k.dma('sp', cb_[0:1, :, :], srcall[:, lrow:lrow + 1, :].rearrange("q p c -> p q c"),
                                  reads=[pjoin], writes=[cb_])
                            k.dma('sp', cb_[1:2, :, :], srcall[:, rrow:rrow + 1, :].rearrange("q p c -> p q c"),
                                  reads=[pjoin], writes=[cb_], merge=True)
                    d2 = dstb[0:P, :]
                    k.op('dve', lambda e: e.tensor_scalar(out=d2, in0=cbufs[0][0:P, :, :].rearrange("p q c -> p (q c)"),
                                                          scalar1=sel[0:P, 0:1], scalar2=None, op0=ALU.mult),
                         [cbufs[0], sel], [dstb])
                    for kk in range(1, 4):
                        k.op('dve', lambda e: e.scalar_tensor_tensor(
                            out=d2, in0=cbufs[kk][0:P, :, :].rearrange("p q c -> p (q c)"), scalar=sel[0:P, kk:kk + 1],
                            in1=d2, op0=ALU.mult, op1=ALU.add), [cbufs[kk], sel, dstb], [dstb])
            k.dma('sp', xt[i][0:P, :], x2[r0:r0 + P, :], writes=[xt[i]])
            a = st1[i]
            k.act(junk[i][0:P, :], ms[i][0:P, :], AF.Square, [ms[i]], [junk[i], a], accum_out=a[0:P, 0:1])
            k.op('dve', lambda e: e.tensor_scalar(out=a[0:P, 1:2], in0=a[0:P, 0:1], scalar1=1.0 / 1024, scalar2=EPS,
                                                  op0=ALU.mult, op1=ALU.add), [a], [a])
            k.act(a[0:P, 1:2], a[0:P, 1:2], AF.Sqrt, [a], [a])
            k.op('dve', lambda e: e.reciprocal(out=a[0:P, 2:3], in_=a[0:P, 1:2]), [a], [a])
            k.op('dve', lambda e: e.tensor_scalar(out=vs[i][0:P, :], in0=ms[i][0:P, :], scalar1=a[0:P, 2:3], scalar2=None,
                                                  op0=ALU.mult), [ms[i], a], [vs[i]])
            for kb in range(16):
                src = ml[i] if kb < 8 else vs[i]
                c0 = (kb % 8) * 128
                k.tr(psT[:, kb, 0:P], src[0:P, c0:c0 + 128], identb[0:P, 0:P], [src, identb], [psT], inc=(kb == 15))
            k.act(mixT[i][:, 0:8, 0:P], psT[:, 0:8, 0:P], AF.Copy, [psT], [mixT[i]])
            k.op('dve', lambda e: e.tensor_copy(out=mixT[i][:, 8:16, 0:P], in_=psT[:, 8:16, 0:P]), [psT], [mixT[i]])
            for h in range(2):
                for kb in range(16):
                    k.mm(psO[0:P, h * 512:(h + 1) * 512], mixT[i][:, kb, 0:P], wo[:, kb, h * 512:(h + 1) * 512],
                         start=(kb == 0), stop=(kb == 15), reads=[mixT[i], wo], writes=[psO])
            k.op('dve', lambda e: e.tensor_tensor(out=xm[i][0:P, :], in0=psO[0:P, :], in1=xt[i][0:P, :], op=ALU.add),
                 [psO, xt[i]], [xm[i]])
            if t < 16:
                k.dma('pool', xmid_d[t][:, :], xm[i][:, :], reads=[xm[i]], writes=[xmid_d[t]])
            b = st2[i]
            k.act(junk[i][0:P, :], xm[i][0:P, :], AF.Square, [xm[i]], [junk[i], b], accum_out=b[0:P, 0:1])
            k.op('dve', lambda e: e.tensor_scalar(out=b[0:P, 1:2], in0=b[0:P, 0:1], scalar1=1.0 / 1024, scalar2=EPS,
                                                  op0=ALU.mult, op1=ALU.add), [b], [b])
            k.act(b[0:P, 1:2], b[0:P, 1:2], AF.Sqrt, [b], [b])
            k.op('dve', lambda e: e.reciprocal(out=b[0:P, 2:3], in_=b[0:P, 1:2]), [b], [b])
            k.op('dve', lambda e: e.tensor_scalar(out=xn[i][0:P, :], in0=xm[i][0:P, :], scalar1=b[0:P, 2:3], scalar2=None,
                                                  op0=ALU.mult), [xm[i], b], [xn[i]])
            for j in range(8):
                k.tr(psF[:, j, 0:P], xn[i][0:P, j * 128:(j + 1) * 128], identb[0:P, 0:P], [xn[i], identb], [psF],
                     inc=(j == 7))
            if t < 16:
                k.act(fT[:, :, 1 + r0:1 + r0 + 128], psF[:, :, :], AF.Copy, [psF], [fT])
            else:
                k.act(fT[:, :, 0:1], psF[:, :, 0:1], AF.Copy, [psF], [fT])
                k.act(fT[:, :, 2049:2050], psF[:, :, 1:2], AF.Copy, [psF], [fT])

    with k.scope() as s3:
        wup = k.sb([128, 8, 6144], BF16, s3, "wup")
        wdn_d = [Buf(io['wdn_bf'][m * 128:(m + 1) * 128, :]) for m in range(24)]
        with k.scope() as sp:
            stg = [k.sb([128, 2048], F32, sp, f"stg{i}") for i in range(2)]
            n = 0
            for j in range(8):
                for c in range(3):
                    st = stg[n % 2]
                    n += 1
                    k.dma('sp', st[:, :], io['w_up'][j * 128:(j + 1) * 128, c * 2048:(c + 1) * 2048], writes=[st])
                    k.op('dve', lambda e: e.tensor_scalar(out=wup[:, j, c * 2048:(c + 1) * 2048], in0=st[:, :],
                                                          scalar1=gs2[:, j:j + 1], scalar2=None, op0=ALU.mult),
                         [st, gs2], [wup])
            dst = [k.sb([128, 1024], F32, sp, f"dst{i}") for i in range(2)]
            dbf = [k.sb([128, 1024], BF16, sp, f"dbf{i}") for i in range(2)]
            for m in range(24):
                st = dst[m % 2]
                ob = dbf[m % 2]
                k.dma('sp', st[:, :], io['w_down'][m * 128:(m + 1) * 128, :], writes=[st])
                k.op('pool', lambda e: e.tensor_tensor(out=ob[:, :], in0=st[:, :], in1=g2bc[:, :], op=ALU.mult),
                     [st, g2bc], [ob])
                k.dma('pool', wdn_d[m][:, :], ob[:, :], reads=[ob], writes=[wdn_d[m]])
        NB = 3
        wd = [k.sb([128, 1024], BF16, s3, f"wd{i}") for i in range(NB)]
        uv = [k.sb([128, 258], F32, s3, f"uv{i}") for i in range(2)]
        ug = [k.sb([128, 258], F32, s3, f"ug{i}") for i in range(2)]
        cv = [k.sb([128, 256], F32, s3, f"cv{i}") for i in range(2)]
        cg = [k.sb([128, 256], F32, s3, f"cg{i}") for i in range(2)]
        t1 = [k.sb([128, 256], F32, s3, f"t1{i}") for i in range(2)]
        t2 = [k.sb([128, 256], F32, s3, f"t2{i}") for i in range(2)]
        t3 = [k.sb([128, 256], F32, s3, f"t3{i}") for i in range(2)]
        aT = [k.sb([128, 256], BF16, s3, f"aT{i}") for i in range(2)]
        xmt = [k.sb([128, 1024], F32, s3, f"xmt{i}") for i in range(2)]
        xo = [k.sb([128, 1024], F32, s3, f"xo{i}") for i in range(2)]
        jk = [k.sb([128, 1024], F32, s3, f"jk{i}") for i in range(2)]
        st3 = [k.sb([128, 4], F32, s3, f"st3_{i}") for i in range(2)]
        psV = [k.ps([128, 512], F32, s3, f"psV{i}") for i in range(2)]
        psG = [k.ps([128, 512], F32, s3, f"psG{i}") for i in range(2)]
        psD = [k.ps([128, 1024], F32, s3, f"psD{i}") for i in range(2)]
        out_bufs = []

        def stage_a(it):
            g, m = divmod(it, 24)
            c0 = 256 * g
            i = it % 2
            w = wd[it % NB]
            k.dma('sp', w[:, :], wdn_d[m][:, :], reads=[wdn_d[m]], writes=[w])
            for (ps, mb) in ((psV[i], m), (psG[i], 24 + m)):
                for j in range(8):
                    k.mm(ps[:, 0:258], wup[:, j, mb * 128:(mb + 1) * 128], fT[:, j, c0:c0 + 258],
                         start=(j == 0), stop=(j == 7), reads=[wup, fT], writes=[ps])

        def stage_b(it):
            g, m = divmod(it, 24)
            i = it % 2
            k.act(uv[i][:, :], psV[i][:, 0:258], AF.Identity, [psV[i], bias2], [uv[i]], bias=bias2[:, m, :])
            k.act(ug[i][:, :], psG[i][:, 0:258], AF.Identity, [psG[i], bias2], [ug[i]], bias=bias2[:, 24 + m, :])
            if g == 0:
                for u in (uv[i], ug[i]):
                    k.op('dve', lambda e: e.tensor_scalar(out=u[:, 0:1], in0=u[:, 0:1], scalar1=hmask[:, 0:1],
                                                          scalar2=None, op0=ALU.mult), [u, hmask], [u])
            if g == 7:
                for u in (uv[i], ug[i]):
                    k.op('dve', lambda e: e.tensor_scalar(out=u[:, 257:258], in0=u[:, 257:258], scalar1=hmask[:, 1:2],
                                                          scalar2=None, op0=ALU.mult), [u, hmask], [u])
            for (u, c, mb) in ((ug[i], cg[i], 24 + m), (uv[i], cv[i], m)):
                k.op('dve', lambda e: e.tensor_scalar(out=c[:, :], in0=u[:, 0:256], scalar1=cw[:, mb, 0:1],
                                                      scalar2=cb[:, mb:mb + 1], op0=ALU.mult, op1=ALU.add),
                     [u, cw, cb], [c])
                for tp in (1, 2):
                    k.op('dve', lambda e: e.scalar_tensor_tensor(out=c[:, :], in0=u[:, tp:tp + 256],
                                                                 scalar=cw[:, mb, tp:tp + 1], in1=c[:, :],
                                                                 op0=ALU.mult, op1=ALU.add), [u, cw, c], [c])
                if mb >= 24:
                    k.act(t1[i][:, :], cg[i][:, :], AF.Square, [cg[i]], [t1[i]])
                    k.op('dve', lambda e: e.tensor_scalar(out=t1[i][:, :], in0=t1[i][:, :], scalar1=0.044715 * GC,
                                                          scalar2=GC, op0=ALU.mult, op1=ALU.add), [t1[i]], [t1[i]])
                    k.op('pool', lambda e: e.tensor_tensor(out=t1[i][:, :], in0=t1[i][:, :], in1=cg[i][:, :], op=ALU.mult),
                         [t1[i], cg[i]], [t1[i]])
                    k.act(t2[i][:, :], t1[i][:, :], AF.Tanh, [t1[i]], [t2[i]])
            k.op('pool', lambda e: e.tensor_tensor(out=t3[i][:, :], in0=cg[i][:, :], in1=cv[i][:, :], op=ALU.mult),
                 [cg[i], cv[i]], [t3[i]])
            k.op('dve', lambda e: e.scalar_tensor_tensor(out=aT[i][:, :], in0=t2[i][:, :], scalar=1.0, in1=t3[i][:, :],
                                                         op0=ALU.add, op1=ALU.mult), [t2[i], t3[i]], [aT[i]])

        def stage_c(it):
            g, m = divmod(it, 24)
            i = it % 2
            w = wd[it % NB]
            for tt in range(2):
                for h in range(2):
                    k.mm(psD[tt][:, h * 512:(h + 1) * 512], aT[i][:, tt * 128:(tt + 1) * 128],
                         w[:, h * 512:(h + 1) * 512], start=(m == 0), stop=(m == 23),
                         reads=[aT[i], w], writes=[psD[tt]], inc=True)
            if m != 23:
                return
            for tt in range(2):
                t = 2 * g + tt
                xi = xmt[tt]
                k.dma('sp', xi[:, :], xmid_d[t][:, :], reads=[xmid_d[t]], writes=[xi])
                o = xo[tt]
                k.op('dve', lambda e: e.tensor_tensor(out=o[:, :], in0=psD[tt][:, :], in1=xi[:, :], op=ALU.add),
                     [psD[tt], xi], [o])
                a = st3[tt]
                k.act(jk[tt][:, :], o[:, :], AF.Square, [o], [jk[tt], a], accum_out=a[:, 0:1])
                k.op('dve', lambda e: e.tensor_scalar(out=a[:, 1:2], in0=a[:, 0:1], scalar1=1.0 / 1024, scalar2=EPS,
                                                      op0=ALU.mult, op1=ALU.add), [a], [a])
                k.act(a[:, 1:2], a[:, 1:2], AF.Sqrt, [a], [a])
                k.op('dve', lambda e: e.reciprocal(out=a[:, 2:3], in_=a[:, 1:2]), [a], [a])
                k.op('dve', lambda e: e.scalar_tensor_tensor(out=o[:, :], in0=o[:, :], scalar=a[:, 2:3], in1=fgbc[:, :],
                                                             op0=ALU.mult, op1=ALU.mult), [o, a, fgbc], [o])
                ob = Buf(out2[t * 128:(t + 1) * 128, :])
                k.dma('pool', ob[:, :], o[:, :], reads=[o], writes=[ob])
                out_bufs.append(ob)

        NIT = 8 * 24
        stage_a(0)
        for it in range(NIT):
            if it + 1 < NIT:
                stage_a(it + 1)
            stage_b(it)
            stage_c(it)
        return out_bufs


def p2_decl(nc, io1=None):
    io = {}
    if io1 is None:
        io['mixl'] = dram(nc, "mixl", [2050, 1024], BF16)
        io['mixs'] = dram(nc, "mixs", [2050, 1024], BF16)
    io['x2'] = dram(nc, "x2", [2050, 1024], F32)
    io['ident'] = dram(nc, "ident", [128, 128], F32) if io1 is None else io1['ident']
    io['hmask'] = dram(nc, "hmask", [128, 2], F32)
    io['c_b'] = dram(nc, "c_b", [1024], F32) if io1 is None else io1['c_b']
    io['ada_w'] = dram(nc, "ada_w", [1024, 6144], F32) if io1 is None else io1['ada_w1']
    io['ada_b'] = dram(nc, "ada_b", [6144], F32) if io1 is None else io1['ada_b1']
    io['final_g'] = dram(nc, "final_g", [1024], F32)
    io['norm2_g'] = dram(nc, "norm2_g", [1024], F32)
    io['ssd_norm_g'] = dram(nc, "ssd_norm_g", [1024], F32)
    io['w_out'] = dram(nc, "w_out", [2048, 1024], F32)
    io['w_up'] = dram(nc, "w_up", [1024, 6144], F32)
    io['w_down'] = dram(nc, "w_down", [3072, 1024], F32)
    io['ffn_cw'] = dram(nc, "ffn_cw", [3, 6144], F32)
    io['ffn_cb'] = dram(nc, "ffn_cb", [6144], F32)
    io['out2'] = dram(nc, "out2", [2048, 1024], F32, kind="ExternalOutput")
    io['xmid'] = dram(nc, "xmid", [2048, 1024], F32, kind=DBG_KIND)
    io['wdn_bf'] = dram(nc, "wdn_bf", [3072, 1024], BF16, kind="Internal")
    if io1 is not None:
        io['sel'] = dram(nc, "sel", [128, 4], F32)
    return io


def make_phase2_nc():
    nc = bass.Bass("TRN2", target_bir_lowering=False)
    io = p2_decl(nc)
    with ExitStack() as es:
        es.enter_context(nc.allow_non_contiguous_dma("small strided parameter loads"))
        k = KB(nc, es)
        outs = build_phase2(nc, k, es, io)
        k.finish(outs)
    return nc


def p2_inputs(inp, mixl_full, mixs_full):
    maps = []
    ident = np.eye(128, dtype=np.float32)
    for b in range(2):
        for kq in range(4):
            t0 = kq * NT2
            rows = list(range(t0, t0 + NT2)) + [max(t0 - 1, 0), min(t0 + NT2, 8191)]
            hm = np.ones((128, 2), np.float32)
            if kq == 0:
                hm[:, 0] = 0.0
            if kq == 3:
                hm[:, 1] = 0.0
            maps.append({
                'mixl': np.ascontiguousarray(mixl_full[b][rows]),
                'mixs': np.ascontiguousarray(mixs_full[b][rows]),
                'x2': np.ascontiguousarray(inp['x'][b][rows]),
                'ident': ident, 'hmask': hm,
                'c_b': np.ascontiguousarray(inp['c'][b]),
                'ada_w': np.ascontiguousarray(inp['ada_w'][0]),
                'ada_b': np.ascontiguousarray(inp['ada_b'][0]),
                'final_g': np.ascontiguousarray(inp['final_norm_g']),
                'norm2_g': np.ascontiguousarray(inp['norm2_g'][0]),
                'ssd_norm_g': np.ascontiguousarray(inp['ssd_norm_g'][0]),
                'w_out': np.ascontiguousarray(inp['w_out'][0]),
                'w_up': np.ascontiguousarray(inp['ffn_w_up'][0]),
                'w_down': np.ascontiguousarray(inp['ffn_w_down'][0]),
                'ffn_cw': np.ascontiguousarray(inp['ffn_conv_w'][0]),
                'ffn_cb': np.ascontiguousarray(inp['ffn_conv_b'][0]),
            })
    return maps


TT = 256 + 8192
TILES = [('c', 0, 256)] + [('l', i * 512, 512) for i in range(16)]


def round_robin(gens):
    gens = list(gens)
    while gens:
        for g_ in list(gens):
            try:
                next(g_)
            except StopIteration:
                gens.remove(g_)


def ring(k, s, shape, dt, n, name):
    return [k.sb(shape, dt, s, f"{name}{i}") for i in range(n)]


def ap3(base, mid):
    d = [list(x) for x in base.ap]
    return bass.AP(base.tensor, base.offset, [d[0], [0, mid], d[1]])


def p1_decl(nc, G=1, fused=False):
    io = {}
    io['x1'] = dram(nc, "x1", [8192, 1024], F32)
    io['ctx1'] = dram(nc, "ctx1", [256, 1024], F32)
    io['c_b'] = dram(nc, "c_b", [1024], F32)
    io['c_ctx'] = dram(nc, "c_ctx", [1024], F32)
    if fused:
        io['ada_w1'] = dram(nc, "ada_w", [1024, 6144], F32)
        io['ada_b1'] = dram(nc, "ada_b", [6144], F32)
    else:
        io['ada_w1'] = dram(nc, "ada_w1", [1024, 2048], F32)
        io['ada_b1'] = dram(nc, "ada_b1", [2048], F32)
    io['norm1_g'] = dram(nc, "norm1_g", [1024], F32)
    io['w_in1'] = dram(nc, "w_in1", [G, 1024, 1288], F32)
    io['l_cw'] = dram(nc, "l_cw", [G, 4, 256], F32)
    io['l_cb'] = dram(nc, "l_cb", [G, 256], F32)
    io['l_wbd'] = dram(nc, "l_wbd", [G, 8, 128, 128], F32)
    io['l_ba'] = dram(nc, "l_ba", [G, 2, 256], F32)
    io['l_bx'] = dram(nc, "l_bx", [G, 2, 256], F32)
    io['l_lam'] = dram(nc, "l_lam", [G, 2, 256], F32)
    io['s_cw'] = dram(nc, "s_cw", [G, 4, 512], F32)
    io['s_cb'] = dram(nc, "s_cb", [G, 512], F32)
    io['s_alog'] = dram(nc, "s_alog", [G, 8], F32)
    io['s_dtb'] = dram(nc, "s_dtb", [G, 8], F32)
    io['s_d'] = dram(nc, "s_d", [G, 4], F32)
    io['ident'] = dram(nc, "ident", [128, 128], F32)
    io['triu'] = dram(nc, "triu", [128, 128], F32)
    io['tril'] = dram(nc, "tril", [128, 128], F32)
    io['ones'] = dram(nc, "ones", [128, 128], F32)
    io['identzf'] = dram(nc, "identzf", [128, 128], F32)
    io['identzb'] = dram(nc, "identzb", [128, 128], F32)
    okind = "Internal" if fused else "ExternalOutput"
    io['ol'] = dram(nc, "ol", [G, 8192, 256], BF16, kind=okind)
    io['os'] = dram(nc, "os", [G, 8192, 256], BF16, kind=okind)
    io['ht_d'] = dram(nc, "ht_d", [len(TILES), 128, 8, 512], BF16, kind="Internal")
    io['lug'] = dram(nc, "lug", [G, 4, 128, TT], F32, kind=DBG_KIND)
    io['zx'] = dram(nc, "zx", [G, TT, 776], F32, kind=DBG_KIND)
    io['xc_d'] = dram(nc, "xc_d", [G, 2, 128, TT], F32, kind="Internal")
    io['hf_d'] = dram(nc, "hf_d", [G, 2, 128, 8192], F32, kind="Internal")
    io['sb_d'] = dram(nc, "sb_d", [G, 66, 128, 256], F32, kind="Internal")
    io['yp_d'] = dram(nc, "yp_d", [G, 64, 128, 256], F32, kind="Internal")
    io['ct_d'] = dram(nc, "ct_d", [G, 64, 128, 128], BF16, kind="Internal")
    io['ecs_d'] = dram(nc, "ecs_d", [G, 66, 128, 16], F32, kind="Internal")
    return io


def io_group(io, q):
    v = dict(io)
    for n in ('w_in1', 'l_cw', 'l_cb', 'l_wbd', 'l_ba', 'l_bx', 'l_lam', 's_cw', 's_cb', 's_alog', 's_dtb', 's_d',
              'ol', 'os', 'lug', 'zx', 'xc_d', 'hf_d', 'sb_d', 'yp_d', 'ct_d', 'ecs_d'):
        v[n] = io[n][q]
    return v


def p1_shared(k, es, io):
    W = {}
    identf = k.sb([128, 128], F32, es, "identf")
    identb = k.sb([128, 128], BF16, es, "identb")
    triu = k.sb([128, 128], F32, es, "triu")
    tril = k.sb([128, 128], F32, es, "tril")
    ones = k.sb([128, 128], F32, es, "ones")
    identzf = k.sb([128, 128], F32, es, "identzf")
    identzb = k.sb([128, 128], F32, es, "identzb")
    ident4 = k.sb([128, 512], F32, es, "ident4")
    for (b, n) in ((identf, 'ident'), (triu, 'triu'), (tril, 'tril'), (ones, 'ones'), (identzf, 'identzf'),
                   (identzb, 'identzb')):
        k.dma('sp', b[:, :], io[n], writes=[b])
    for h in range(4):
        k.dma('sp', ident4[:, h * 128:(h + 1) * 128], io['ident'], writes=[ident4])
    W.update(identzf=identzf, identzb=identzb, ident4=ident4)
    k.op('dve', lambda e: e.tensor_copy(out=identb[:, :], in_=identf[:, :]), [identf], [identb])
    W.update(identf=identf, identb=identb, triu=triu, tril=tril, ones=ones)

    sc = silu_vec(k, es, [io['c_b'], io['c_ctx']], 2, "c1")
    adab_p = k.sb([128, 16], F32, es, "adab_p")
    k.dma('sp', adab_p[:, :], io['ada_b1'][0:2048].rearrange("(m p) -> p m", p=128), writes=[adab_p])
    shsc = k.sb([128, 16, 2], F32, es, "shsc")
    matvec_part(k, es, io['ada_w1'][:, 0:2048], 2048, sc, 2, shsc, adab_p, "shsc")
    n1g = k.sb([128, 8], F32, es, "n1g")
    k.dma('sp', n1g[:, :], io['norm1_g'].rearrange("(m p) -> p m", p=128), writes=[n1g])
    gs = k.sb([128, 8, 2], F32, es, "gs1")
    sh = k.sb([128, 8, 2], F32, es, "sh1")
    for v in range(2):
        k.op('dve', lambda e: e.scalar_tensor_tensor(out=gs[:, :, v], in0=shsc[:, 8:16, v], scalar=1.0, in1=n1g[:, :],
                                                     op0=ALU.add, op1=ALU.mult), [shsc, n1g], [gs])
    k.op('dve', lambda e: e.tensor_copy(out=sh[:, :, :], in_=shsc[:, 0:8, :]), [shsc], [sh])
    W.update(sh=sh, gs=gs)
    return W


def p1_group(k, es, io, W0):
    W = dict(W0)
    identf, sh, gs = W['identf'], W['sh'], W['gs']
    bl = k.sb([128, 4, 2], F32, es, "bl")
    matvec_part(k, es, io['w_in1'][:, 0:512], 512, sh, 2, bl, None, "bl")
    bz = []
    for v in range(2):
        rep = make_rep(k, es, identf, sh, v, f"shrep{v}")
        o = k.sb([128, 776], F32, es, f"bz{v}")
        matvec_bcast(k, es, io['w_in1'][:, 512:1288], 776, (rep, lambda j, rep=rep: rep[:, j, :]), o, None, f"bz{v}")
        bz.append(o)
    W.update(bl=bl, bz=bz)
    Wl = k.sb([128, 8, 1288], BF16, es, "Wl")
    Wc = k.sb([128, 8, 1288], BF16, es, "Wc")
    with k.scope() as s:
        stg = ring(k, s, [128, 1288], F32, 2, "wstg")
        for j in range(8):
            st = stg[j % 2]
            k.dma('sp', st[:, :], io['w_in1'][j * 128:(j + 1) * 128, :], writes=[st])
            k.op('dve', lambda e: e.tensor_scalar(out=Wl[:, j, :], in0=st[:, :], scalar1=gs[:, j, 0:1], scalar2=None,
                                                  op0=ALU.mult), [st, gs], [Wl])
            k.act(Wc[:, j, :], st[:, :], AF.Copy, [st, gs], [Wc], scale=gs[:, j, 1:2])
    W.update(Wl=Wl, Wc=Wc)
    lcw = k.sb([128, 2, 4], F32, es, "lcw")
    lcb = k.sb([128, 2], F32, es, "lcb")
    for tp in range(4):
        k.dma('sp', lcw[:, :, tp], io['l_cw'][tp].rearrange("(m p) -> p m", p=128), writes=[lcw])
    k.dma('sp', lcb[:, :], io['l_cb'].rearrange("(m p) -> p m", p=128), writes=[lcb])
    wbd = k.sb([128, 8, 128], BF16, es, "wbd")
    with k.scope() as s:
        wf = k.sb([128, 8, 128], F32, s, "wbdf")
        k.dma('sp', wf[:, :, :], io['l_wbd'].rearrange("i p m -> p i m"), writes=[wf])
        k.op('dve', lambda e: e.tensor_copy(out=wbd[:, :, :], in_=wf[:, :, :]), [wf], [wbd])
    hba = k.sb([128, 2, 2], F32, es, "hba")
    hbx = k.sb([128, 2, 2], F32, es, "hbx")
    lam = k.sb([128, 2, 2], F32, es, "lam")
    for d in range(2):
        k.dma('sp', hba[:, d, :], io['l_ba'][d].rearrange("(m p) -> p m", p=128), writes=[hba])
        k.dma('sp', hbx[:, d, :], io['l_bx'][d].rearrange("(m p) -> p m", p=128), writes=[hbx])
        k.dma('sp', lam[:, d, :], io['l_lam'][d].rearrange("(m p) -> p m", p=128), writes=[lam])
    for b in (hba, hbx):
        k.op('dve', lambda e: e.tensor_scalar(out=b[:, :, :], in0=b[:, :, :], scalar1=0.5, scalar2=None, op0=ALU.mult),
             [b], [b])
    cr = k.sb([128, 2, 2], F32, es, "cr")
    hcr = k.sb([128, 2, 2], F32, es, "hcr")
    k.act(cr[:, :, :], lam[:, :, :], AF.Exp, [lam], [cr], scale=-1.0)
    k.act(cr[:, :, :], cr[:, :, :], AF.Ln, [cr], [cr], bias=1.0)
    k.op('dve', lambda e: e.tensor_scalar(out=hcr[:, :, :], in0=cr[:, :, :], scalar1=-4.0, scalar2=None, op0=ALU.mult),
         [cr], [hcr])
    k.op('dve', lambda e: e.tensor_scalar(out=cr[:, :, :], in0=cr[:, :, :], scalar1=-8.0, scalar2=None, op0=ALU.mult),
         [cr], [cr])
    W.update(lcw=lcw, lcb=lcb, wbd=wbd, hba=hba, hbx=hbx, cr=cr, hcr=hcr)
    scw = k.sb([128, 4, 512], F32, es, "scw")
    scb = k.sb([128, 512], F32, es, "scb")
    for tp in range(4):
        k.dma('sp', scw[:, tp, :], bcast_rows(io['s_cw'][tp], 512), writes=[scw])
    k.dma('sp', scb[:, :], bcast_rows(io['s_cb'], 512), writes=[scb])
    negA = k.sb([128, 8], F32, es, "negA")
    dtb = k.sb([128, 8], F32, es, "dtb")
    hD = k.sb([128, 4], F32, es, "hD")
    k.dma('sp', negA[:, :], bcast_rows(io['s_alog'], 8), writes=[negA])
    k.dma('sp', dtb[:, :], bcast_rows(io['s_dtb'], 8), writes=[dtb])
    k.dma('sp', hD[:, :], bcast_rows(io['s_d'], 4), writes=[hD])
    k.act(negA[:, :], negA[:, :], AF.Exp, [negA], [negA])
    k.op('dve', lambda e: e.tensor_scalar(out=negA[:, :], in0=negA[:, :], scalar1=-1.0, scalar2=None, op0=ALU.mult),
         [negA], [negA])
    k.op('dve', lambda e: e.tensor_scalar(out=hD[:, :], in0=hD[:, :], scalar1=0.5, scalar2=None, op0=ALU.mult),
         [hD], [hD])
    W.update(scw=scw, scb=scb, negA=negA, dtb=dtb, hD=hD)
    return W


def p1_prework(k, es, io, W0):
    identb = W0['identb']
    htd = [Buf(io['ht_d'][ti]) for ti in range(len(TILES))]
    with k.scope() as s:
        xt = ring(k, s, [128, 1024], F32, 4, "xt")
        xn = ring(k, s, [128, 1024], BF16, 3, "xn")
        junk = ring(k, s, [128, 1024], BF16, 2, "junk")
        st = ring(k, s, [128, 4], F32, 4, "st")
        hT = ring(k, s, [128, 8, 512], BF16, 3, "hTp")
        psH = [k.ps([128, 8, 128], BF16, s, f"psH{i}") for i in range(4)]
        nx = 0
        for ti, (kd, t0, n) in enumerate(TILES):
            src = io['ctx1'] if kd == 'c' else io['x1']
            h = hT[ti % 3]
            for sub in range(n // 128):
                x = xt[nx % 4]
                xb = xn[nx % 3]
                jk = junk[nx % 2]
                a = st[nx % 4]
                ph = psH[nx % 4]
                nx += 1
                r0 = t0 + sub * 128
                k.dma('sp', x[:, :], src[r0:r0 + 128, :], writes=[x])
                k.act(jk[:, :], x[:, :], AF.Square, [x], [jk, a], accum_out=a[:, 0:1])
                k.op('dve', lambda e: e.tensor_scalar(out=a[:, 1:2], in0=a[:, 0:1], scalar1=1.0 / 1024, scalar2=EPS,
                                                      op0=ALU.mult, op1=ALU.add), [a], [a])
                k.act(a[:, 1:2], a[:, 1:2], AF.Sqrt, [a], [a])
                k.op('dve', lambda e: e.reciprocal(out=a[:, 2:3], in_=a[:, 1:2]), [a], [a])
                k.op('dve', lambda e: e.tensor_scalar(out=xb[:, :], in0=x[:, :], scalar1=a[:, 2:3], scalar2=None,
                                                      op0=ALU.mult), [x, a], [xb])
                for j in range(8):
                    k.tr(ph[:, j, :], xb[:, j * 128:(j + 1) * 128], identb[:, :], [xb, identb], [ph], inc=(j == 7))
                k.op('dve' if sub % 2 else 'act',
                     (lambda e: e.tensor_copy(out=h[:, :, sub * 128:(sub + 1) * 128], in_=ph[:, :, :])) if sub % 2 else
                     (lambda e: e.activation(out=h[:, :, sub * 128:(sub + 1) * 128], in_=ph[:, :, :], func=AF.Copy)),
                     [ph], [h])
            k.dma('pool', htd[ti][:, :, 0:n], h[:, :, 0:n], reads=[h], writes=[htd[ti]])
    return htd


def p1_inproj(k, es, io, W, htd):
    lug = [[Buf(io['lug'][mb, :, (0 if ti == 0 else 256 + t0):(0 if ti == 0 else 256 + t0) + n])
            for ti, (kd, t0, n) in enumerate(TILES)] for mb in range(4)]
    zxb = {}
    with k.scope() as s:
        hT = ring(k, s, [128, 8, 512], BF16, 3, "hT")
        lo = ring(k, s, [128, 512], F32, 3, "lo")
        g1 = ring(k, s, [128, 512], F32, 2, "g1")
        g2 = ring(k, s, [128, 512], F32, 2, "g2")
        zo = ring(k, s, [128, 776], F32, 3, "zo")
        psL = [k.ps([128, 512], F32, s, f"psL{i}") for i in range(4)]
        psZ = [k.ps([128, 1024], F32, s, f"psZ{i}") for i in range(2)]
        nl = 0
        nz = 0

        def load(ti):
            kd, t0, n = TILES[ti]
            h = hT[ti % 3]
            k.dma('sp', h[:, :, 0:n], htd[ti][:, :, 0:n], reads=[htd[ti]], writes=[h])

        load(0)
        load(1)
        for ti, (kd, t0, n) in enumerate(TILES):
            if ti + 2 < len(TILES):
                load(ti + 2)
            v = 1 if kd == 'c' else 0
            Wt = W['Wc'] if kd == 'c' else W['Wl']
            base = 0 if kd == 'c' else 256
            h = hT[ti % 3]
            nsub = n // 128
            for mb in range(4):
                ps = psL[nl % 4]
                o = lo[nl % 3]
                ga = g1[nl % 2]
                gb = g2[nl % 2]
                nl += 1
                for j in range(8):
                    k.mm(ps[:, 0:n], Wt[:, j, mb * 128:(mb + 1) * 128], h[:, j, 0:n], start=(j == 0), stop=(j == 7),
                         reads=[Wt, h], writes=[ps])
                k.act(o[:, 0:n], ps[:, 0:n], AF.Identity, [ps, W['bl']], [o], bias=W['bl'][:, mb, v:v + 1])
                if mb >= 2:
                    k.act(ga[:, 0:n], o[:, 0:n], AF.Square, [o], [ga])
                    k.op('dve', lambda e: e.tensor_scalar(out=ga[:, 0:n], in0=ga[:, 0:n], scalar1=0.044715 * GC, scalar2=GC,
                                                          op0=ALU.mult, op1=ALU.add), [ga], [ga])
                    k.op('pool', lambda e: e.tensor_tensor(out=ga[:, 0:n], in0=ga[:, 0:n], in1=o[:, 0:n], op=ALU.mult),
                         [ga, o], [ga])
                    k.act(gb[:, 0:n], ga[:, 0:n], AF.Tanh, [ga], [gb])
                    k.act(ga[:, 0:n], o[:, 0:n], AF.Copy, [o], [ga], scale=0.25)
                    k.op('dve', lambda e: e.scalar_tensor_tensor(out=o[:, 0:n], in0=gb[:, 0:n], scalar=1.0, in1=ga[:, 0:n],
                                                                 op0=ALU.add, op1=ALU.mult), [gb, ga], [o])
                k.dma('pool', lug[mb][ti][:, :], o[:, 0:n], reads=[o], writes=[lug[mb][ti]])
            for sub in range(nsub):
                pz = psZ[nz % 2]
                z = zo[nz % 3]
                nz += 1
                for (c0, cw_) in ((0, 512), (512, 264)):
                    for j in range(8):
                        k.mm(pz[:, c0:c0 + cw_], h[:, j, sub * 128:(sub + 1) * 128], Wt[:, j, 512 + c0:512 + c0 + cw_],
                             start=(j == 0), stop=(j == 7), reads=[h, Wt], writes=[pz])
                k.op('dve', lambda e: e.tensor_tensor(out=z[:, :], in0=pz[:, 0:776], in1=W['bz'][v][:, :], op=ALU.add),
                     [pz, W['bz'][v]], [z])
                r0 = base + t0 + sub * 128
                zb = Buf(io['zx'][r0:r0 + 128, :])
                zxb[r0 // 128] = zb
                k.dma('pool', zb[:, :], z[:, :], reads=[z], writes=[zb])
    return lug, zxb


def lru_gates(k, W, d, blk, n, xcb, psR, psI, thr, thi, a, m, u, xc):
    wbd = W['wbd']
    k.mm(psR[:, 0:n], wbd[:, d * 4 + 0 + blk, :], xcb[:, 0:n], True, True, [wbd, xcb], [psR])
    k.mm(psI[:, 0:n], wbd[:, d * 4 + 2 + blk, :], xcb[:, 0:n], True, True, [wbd, xcb], [psI])
    yield
    k.act(thr[:, 0:n], psR[:, 0:n], AF.Tanh, [psR, W['hba']], [thr], scale=0.5, bias=W['hba'][:, d, blk:blk + 1])
    k.act(thi[:, 0:n], psI[:, 0:n], AF.Tanh, [psI, W['hbx']], [thi], scale=0.5, bias=W['hbx'][:, d, blk:blk + 1])
    yield
    k.act(a[:, 0:n], thr[:, 0:n], AF.Exp, [thr, W['hcr']], [a], scale=W['hcr'][:, d, blk:blk + 1],
          bias=W['hcr'][:, d, blk:blk + 1])
    k.act(m[:, 0:n], thr[:, 0:n], AF.Exp, [thr, W['cr']], [m], scale=W['cr'][:, d, blk:blk + 1],
          bias=W['cr'][:, d, blk:blk + 1])
    yield
    k.act(m[:, 0:n], m[:, 0:n], AF.Sqrt, [m], [m], scale=-1.0, bias=1.0)
    yield
    k.op('dve', lambda e: e.scalar_tensor_tensor(out=u[:, 0:n], in0=thi[:, 0:n], scalar=1.0, in1=xc[:, 0:n],
                                                 op0=ALU.add, op1=ALU.mult), [thi, xc], [u])
    yield
    k.op('dve', lambda e: e.tensor_tensor(out=u[:, 0:n], in0=u[:, 0:n], in1=m[:, 0:n], op=ALU.mult), [u, m], [u])


def p1_lru(k, es, io, W, lug):
    xcd = [[Buf(io['xc_d'][blk, :, (0 if ti == 0 else 256 + t0):(0 if ti == 0 else 256 + t0) + n])
            for ti, (kd, t0, n) in enumerate(TILES)] for blk in range(2)]
    hfd = [[None] + [Buf(io['hf_d'][blk, :, t0:t0 + n]) for (kd, t0, n) in TILES[1:]] for blk in range(2)]
    outs = []
    with k.scope() as s:
        lw = ring(k, s, [128, 515], F32, 4, "lw")
        xc = ring(k, s, [128, 512], F32, 4, "xc")
        xcb = ring(k, s, [128, 512], BF16, 4, "xcb")
        thr = ring(k, s, [128, 512], F32, 2, "thr")
        thi = ring(k, s, [128, 512], F32, 2, "thi")
        aa = ring(k, s, [128, 512], F32, 2, "aa")
        mm_ = ring(k, s, [128, 512], F32, 2, "mm")
        uu = ring(k, s, [128, 512], F32, 2, "uu")
        hf = [ring(k, s, [128, 512], F32, 2, f"hf{blk}") for blk in range(2)]
        psR = [k.ps([128, 512], F32, s, f"psR{i}") for i in range(2)]
        psI = [k.ps([128, 512], F32, s, f"psI{i}") for i in range(2)]
        prev = [None, None]

        def fwd_early(ti, blk, it):
            kd, t0, n = TILES[ti]
            L = 256 if kd == 'c' else 8192
            w = lw[it % 4]
            c = xc[it % 4]
            xb_ = xcb[it % 4]
            lo_, hi_ = t0 - 2, t0 + n + 1
            clo, chi = max(lo_, 0), min(hi_, L)
            if clo > lo_ or chi < hi_:
                k.op('pool', lambda e: e.memset(w[:, :], 0.0), [], [w])
            base = 0 if kd == 'c' else 256
            srcs = [lug[blk][tj] for tj, (kd2, t02, n2) in enumerate(TILES)
                    if kd2 == kd and t02 < chi and t02 + n2 > clo]
            k.dma('sp', w[:, clo - lo_:chi - lo_], io['lug'][blk, :, base + clo:base + chi], reads=srcs, writes=[w])
            yield
            yield
            lcw, lcb = W['lcw'], W['lcb']
            k.op('dve', lambda e: e.tensor_scalar(out=c[:, 0:n], in0=w[:, 0:n], scalar1=lcw[:, blk, 0:1],
                                                  scalar2=lcb[:, blk:blk + 1], op0=ALU.mult, op1=ALU.add),
                 [w, lcw, lcb], [c])
            yield
            for tp in (1, 2, 3):
                k.op('dve', lambda e: e.scalar_tensor_tensor(out=c[:, 0:n], in0=w[:, tp:tp + n],
                                                             scalar=lcw[:, blk, tp:tp + 1], in1=c[:, 0:n],
                                                             op0=ALU.mult, op1=ALU.add), [w, lcw, c], [c])
                yield
            k.act(xb_[:, 0:n], c[:, 0:n], AF.Copy, [c], [xb_])
            k.dma('pool', xcd[blk][ti][:, :], c[:, 0:n], reads=[c], writes=[xcd[blk][ti]])
            yield

        def fwd_late(ti, blk, it):
            kd, t0, n = TILES[ti]
            c = xc[it % 4]
            xb_ = xcb[it % 4]
            i2 = it % 2
            yield from lru_gates(k, W, 0, blk, n, xb_, psR[i2], psI[i2], thr[i2], thi[i2], aa[i2], mm_[i2], uu[i2], c)
            h = hf[blk][ti % 2]
            if prev[blk] is None:
                k.op('dve', lambda e: e.tensor_tensor_scan(out=h[:, 0:n], data0=aa[i2][:, 0:n], data1=uu[i2][:, 0:n],
                                                           initial=0.0, op0=ALU.mult, op1=ALU.add),
                     [aa[i2], uu[i2]], [h])
            else:
                pb, pn = prev[blk]
                k.op('dve', lambda e: e.tensor_tensor_scan(out=h[:, 0:n], data0=aa[i2][:, 0:n], data1=uu[i2][:, 0:n],
                                                           initial=pb[:, pn - 1:pn], op0=ALU.mult, op1=ALU.add),
                     [aa[i2], uu[i2], pb], [h])
            prev[blk] = (h, n)
            if kd == 'l':
                k.dma('pool', hfd[blk][ti][:, :], h[:, 0:n], reads=[h], writes=[hfd[blk][ti]])
            yield

        round_robin([fwd_early(0, blk, blk) for blk in range(2)])
        for ti in range(len(TILES)):
            gens = [fwd_late(ti, blk, 2 * ti + blk) for blk in range(2)]
            if ti + 1 < len(TILES):
                gens += [fwd_early(ti + 1, blk, 2 * (ti + 1) + blk) for blk in range(2)]
            round_robin(gens)
    with k.scope() as s:
        xc = ring(k, s, [128, 512], F32, 4, "bxc")
        xcb = ring(k, s, [128, 512], BF16, 4, "bxcb")
        thr = ring(k, s, [128, 512], F32, 2, "bthr")
        thi = ring(k, s, [128, 512], F32, 2, "bthi")
        aa = ring(k, s, [128, 512], F32, 2, "baa")
        mm_ = ring(k, s, [128, 512], F32, 2, "bmm")
        uu = ring(k, s, [128, 512], F32, 2, "buu")
        hb = [ring(k, s, [128, 512], F32, 2, f"hb{blk}") for blk in range(2)]
        hfl = ring(k, s, [128, 512], F32, 4, "hfl")
        gl = ring(k, s, [128, 512], F32, 4, "gl")
        ob = ring(k, s, [128, 512], BF16, 2, "ob")
        otm = ring(k, s, [128, 4, 256], BF16, 2, "otm")
        psR = [k.ps([128, 512], F32, s, f"bpsR{i}") for i in range(2)]
        psI = [k.ps([128, 512], F32, s, f"bpsI{i}") for i in range(2)]
        psT = [k.ps([128, 4, 256], BF16, s, f"bpsT{i}") for i in range(2)]
        order = [0] + list(range(16, 0, -1))
        prev = [None, None]

        def bwd_early(oi, ti, blk, it):
            kd, t0, n = TILES[ti]
            c = xc[it % 4]
            hl = hfl[it % 4]
            g = gl[it % 4]
            k.dma('sp', c[:, 0:n], xcd[blk][ti][:, :], reads=[xcd[blk][ti]], writes=[c])
            if kd == 'l':
                k.dma('sp', hl[:, 0:n], hfd[blk][ti][:, :], reads=[hfd[blk][ti]], writes=[hl])
                k.dma('sp', g[:, 0:n], lug[2 + blk][ti][:, :], reads=[lug[2 + blk][ti]], writes=[g])
            yield
            yield
            yield
            k.act(xcb[it % 4][:, 0:n], c[:, 0:n], AF.Copy, [c], [xcb[it % 4]])
            yield

        def bwd_late(oi, ti, blk, it):
            kd, t0, n = TILES[ti]
            pt = psT[oi % 2]
            c = xc[it % 4]
            hl = hfl[it % 4]
            g = gl[it % 4]
            xb_ = xcb[it % 4]
            i2 = it % 2
            yield from lru_gates(k, W, 1, blk, n, xb_, psR[i2], psI[i2], thr[i2], thi[i2], aa[i2], mm_[i2], uu[i2], c)
            h = hb[blk][oi % 2]
            if prev[blk] is None:
                k.op('dve', lambda e: e.tensor_tensor_scan(out=h[:, 0:n][:, ::-1],
                                                           data0=aa[i2][:, 0:n][:, ::-1], data1=uu[i2][:, 0:n][:, ::-1],
                                                           initial=0.0, op0=ALU.mult, op1=ALU.add),
                     [aa[i2], uu[i2]], [h])
            else:
                pb = prev[blk]
                k.op('dve', lambda e: e.tensor_tensor_scan(out=h[:, 0:n][:, ::-1], data0=aa[i2][:, 0:n][:, ::-1],
                                                           data1=uu[i2][:, 0:n][:, ::-1], initial=pb[:, 0:1],
                                                           op0=ALU.mult, op1=ALU.add), [aa[i2], uu[i2], pb], [h])
            prev[blk] = h
            yield
            if kd == 'l':
                k.op('pool', lambda e: e.tensor_tensor(out=hl[:, 0:n], in0=hl[:, 0:n], in1=h[:, 0:n], op=ALU.add),
                     [hl, h], [hl])
                yield
                k.op('pool', lambda e: e.tensor_tensor(out=ob[i2][:, 0:n], in0=hl[:, 0:n], in1=g[:, 0:n], op=ALU.mult),
                     [hl, g], [ob[i2]])
                yield
                for sub in range(4):
                    k.tr(pt[:, sub, blk * 128:(blk + 1) * 128], ob[i2][:, sub * 128:(sub + 1) * 128], W['identb'][:, :],
                         [ob[i2], W['identb']], [pt], inc=(sub == 3))

        round_robin([bwd_early(0, order[0], blk, blk) for blk in range(2)])
        for oi, ti in enumerate(order):
            kd, t0, n = TILES[ti]
            pt = psT[oi % 2]
            gens = [bwd_late(oi, ti, blk, 2 * oi + blk) for blk in range(2)]
            if oi + 1 < len(order):
                gens += [bwd_early(oi + 1, order[oi + 1], blk, 2 * (oi + 1) + blk) for blk in range(2)]
            round_robin(gens)
            if kd == 'l':
                o = otm[oi % 2]
                k.act(o[:, :, :], pt[:, :, :], AF.Copy, [pt], [o])
                dst = Buf(io['ol'][t0:t0 + n, :])
                k.dma('pool', dst.t.rearrange("(s p) c -> p s c", p=128), o[:, :, :], reads=[o], writes=[dst])
                outs.append(dst)
    return outs


def zrows(io, ci, r_lo, r_hi, c0, c1):
    zx = io['zx']
    if ci < 2:
        b0 = ci * 128 + r_lo
        return zx[b0:b0 + (r_hi - r_lo), c0:c1]
    c = ci - 2
    start = 256 + r_lo * 64 + c
    return bass.AP(zx.tensor, zx.offset + start * 776 + c0, [[64 * 776, r_hi - r_lo], [1, c1 - c0]])


def p1_ssd(k, es, io, W, zxb):
    identf, identb, triu, tril, ones = W['identf'], W['identb'], W['triu'], W['tril'], W['ones']
    outs = []
    join = k.sb([128, 1], F32, es, "zjoin")
    k.op('pool', lambda e: e.memset(join[:, :], 0.0), list(zxb.values()), [join])
    NCH = 66
    sbd = [Buf(io['sb_d'][ci]) for ci in range(NCH)]
    ecd = [Buf(io['ecs_d'][ci]) for ci in range(NCH)]
    ypd = [Buf(io['yp_d'][c]) for c in range(64)]
    ctd = [Buf(io['ct_d'][c]) for c in range(64)]
    DTV = k.sb([128, NCH, 8], F32, es, "DTV")
    DA = k.sb([128, NCH, 8], F32, es, "DA")
    EA = k.sb([128, NCH, 8], F32, es, "EA")
    HDT = k.sb([128, NCH, 8], F32, es, "HDT")
    for ci in range(2):
        k.dma('sp', DTV[:, ci, :], zrows(io, ci, 0, 128, 768, 776), reads=[join], writes=[DTV])
    zx = io['zx']
    for g in range(4):
        src = bass.AP(zx.tensor, zx.offset + (256 + 16 * g) * 776 + 768, [[64 * 776, 128], [776, 16], [1, 8]])
        k.dma('sp', DTV[:, 2 + 16 * g:2 + 16 * (g + 1), :], src, reads=[join], writes=[DTV])

    def bc66(buf):
        b = buf[:, :]
        d = [list(x) for x in b.ap]
        return bass.AP(b.tensor, b.offset, [d[0], [0, NCH], d[1]])
    k.op('dve', lambda e: e.tensor_tensor(out=DTV[:, :, :], in0=DTV[:, :, :], in1=bc66(W['dtb']), op=ALU.add),
         [DTV, W['dtb']], [DTV])
    k.act(DTV[:, :, :], DTV[:, :, :], AF.Exp, [DTV], [DTV])
    k.act(DTV[:, :, :], DTV[:, :, :], AF.Ln, [DTV], [DTV], bias=1.0)
    k.op('dve', lambda e: e.tensor_tensor(out=DA[:, :, :], in0=DTV[:, :, :], in1=bc66(W['negA']), op=ALU.mult),
         [DTV, W['negA']], [DA])
    k.act(EA[:, :, :], DA[:, :, :], AF.Exp, [DA], [EA])
    k.op('dve', lambda e: e.tensor_scalar(out=HDT[:, :, :], in0=DTV[:, :, :], scalar1=0.5, scalar2=None, op0=ALU.mult),
         [DTV], [HDT])

    if SSD_LIMIT[0] < 1:
        return outs
    with k.scope() as s:
        tk = [ring(k, s, [128, 512], F32, 4, f"tk{kk}_") for kk in range(4)]
        pre = ring(k, s, [128, 512], F32, 2, "pre")
        th = ring(k, s, [128, 512], F32, 2, "sth")
        act_ = ring(k, s, [128, 512], F32, 2, "sact")
        xbf = ring(k, s, [128, 512], BF16, 2, "xbf")
        bct = ring(k, s, [128, 2, 128], BF16, 2, "bct")
        scT = ring(k, s, [128, 128], F32, 2, "scT")
        ecs = ring(k, s, [128, 16], F32, 2, "ecs")
        identz = [W['identzf'], W['identzb']]
        ident4 = W['ident4']
        LT8 = ring(k, s, [128, 8, 128], F32, 2, "LT8")
        MT = ring(k, s, [128, 8, 128], BF16, 2, "MT")
        xs = ring(k, s, [128, 2, 256], BF16, 2, "xs")
        xd = ring(k, s, [128, 2, 256], BF16, 2, "xd")
        yp = ring(k, s, [128, 256], F32, 2, "yp")
        tmp = ring(k, s, [128, 256], F32, 2, "ytmp")
        sbo = ring(k, s, [128, 256], F32, 2, "sbo")
        dst8 = ring(k, s, [128, 8], F32, 2, "dst8")
        hd8 = ring(k, s, [128, 8], F32, 2, "hd8")
        sfo = ring(k, s, [128, 256], F32, 2, "sfo")
        Hf = k.sb([128, 256], F32, s, "Hf")
        Hfb = k.sb([128, 256], BF16, s, "Hfb")
        Hs = k.sb([128, 256], F32, s, "Hs")
        k.op('pool', lambda e: e.memset(Hf[:, :], 0.0), [], [Hf])
        k.op('pool', lambda e: e.memset(Hfb[:, :], 0.0), [], [Hfb])
        psBC = k.ps([128, 2, 128], BF16, s, "psBC")
        psSc = k.ps([128, 128], F32, s, "psSc")
        psCS = k.ps([128, 16], F32, s, "psCS")
        psA = [k.ps([128, 512], F32, s, f"psA{i}") for i in range(2)]
        psY = k.ps([128, 256], F32, s, "psY")
        psYo = k.ps([128, 256], F32, s, "psYo")
        psS = k.ps([128, 512], F32, s, "psS")
        na_box = [0]

        pend = []

        def ssd_loads(cj):
            r3 = cj % 4
            has_prev = cj in (1,) or cj > 2
            has_next = cj == 0 or (2 <= cj < NCH - 1)
            for kk in range(4):
                sh = kk - 2
                t = tk[kk][r3]
                need_zero = (sh < 0 and not has_prev) or (sh > 0 and not has_next)
                if need_zero:
                    k.op('pool', lambda e: e.memset(t[:, :], 0.0), [], [t])
                a_, b_ = max(0, -sh), min(128, 128 - sh)
                k.dma('sp', t[a_:b_, :], zrows(io, cj, a_ + sh, b_ + sh, 256, 768), reads=[join], writes=[t])
                if sh < 0 and has_prev:
                    k.dma('sp', t[0:-sh, :], zrows(io, cj - 1, 128 + sh, 128, 256, 768), reads=[join], writes=[t], merge=True)
                if sh > 0 and has_next:
                    k.dma('sp', t[128 - sh:128, :], zrows(io, cj + 1, 0, sh, 256, 768), reads=[join], writes=[t], merge=True)

        def stage_a(ci):
            i2 = ci % 2
            lat = ci >= 2
            has_prev = ci in (1,) or ci > 2
            has_next = ci == 0 or (2 <= ci < NCH - 1)
            if ci + 2 < min(NCH, SSD_LIMIT[1]):
                ssd_loads(ci + 2)
            r3 = ci % 4
            yield
            scw, scb = W['scw'], W['scb']
            for kk in range(4):
                t = tk[kk][r3]
                k.op('pool', lambda e: e.tensor_tensor(out=t[:, :], in0=t[:, :], in1=scw[:, kk, :], op=ALU.mult),
                     [t, scw], [t])
                yield
            p = pre[i2]
            k.op('dve', lambda e: e.tensor_tensor(out=p[:, :], in0=tk[0][r3][:, :], in1=tk[1][r3][:, :], op=ALU.add),
                 [tk[0][r3], tk[1][r3]], [p])
            yield
            k.op('dve', lambda e: e.tensor_tensor(out=p[:, :], in0=p[:, :], in1=tk[2][r3][:, :], op=ALU.add),
                 [p, tk[2][r3]], [p])
            yield
            k.op('dve', lambda e: e.tensor_tensor(out=p[:, :], in0=p[:, :], in1=tk[3][r3][:, :], op=ALU.add),
                 [p, tk[3][r3]], [p])
            k.op('dve', lambda e: e.tensor_tensor(out=p[:, :], in0=p[:, :], in1=scb[:, :], op=ALU.add), [p, scb], [p])
            yield
            k.act(th[i2][:, :], p[:, :], AF.Tanh, [p], [th[i2]], scale=0.5)
            yield
            ac = act_[i2]
            k.op('dve', lambda e: e.scalar_tensor_tensor(out=ac[:, :], in0=th[i2][:, :], scalar=1.0, in1=p[:, :],
                                                         op0=ALU.add, op1=ALU.mult), [th[i2], p], [ac])
            yield
            xb = xbf[i2]
            k.act(xb[:, :], ac[:, :], AF.Copy, [ac], [xb], scale=0.5)
            yield
            k.tr(psBC[:, 0, :], xb[:, 256:384], identb[:, :], [xb, identb], [psBC], inc=False)
            k.tr(psBC[:, 1, :], xb[:, 384:512], identb[:, :], [xb, identb], [psBC], inc=True)
            yield
            bc = bct[i2]
            k.act(bc[:, :, :], psBC[:, :, :], AF.Copy, [psBC], [bc])
            yield
            if lat:
                k.mm(psSc[:, :], bc[:, 0, :], bc[:, 1, :], True, True, [bc], [psSc])
                k.act(scT[i2][:, :], psSc[:, :], AF.Copy, [psSc], [scT[i2]])
                pend.append(lambda ci=ci: k.dma('pool', ctd[ci - 2][:, :], bc[:, 1, :], reads=[bc], writes=[ctd[ci - 2]]))
            k.mm(psCS[:, 0:4], triu[:, :], DA[:, ci, 0:4], True, True, [triu, DA], [psCS])
            k.mm(psCS[:, 4:8], tril[:, :], DA[:, ci, 4:8], True, True, [tril, DA], [psCS])
            k.mm(psCS[:, 8:16], ones[:, :], DA[:, ci, 0:8], True, True, [ones, DA], [psCS])
            yield
            ec = ecs[i2]
            k.act(ec[:, :], psCS[:, :], AF.Exp, [psCS], [ec])
            pend.append(lambda ci=ci: k.dma('pool', ecd[ci][:, :], ec[:, :], reads=[ec], writes=[ecd[ci]]))

        gen_box = [None]

        def advance(n=1):
            for _ in range(n):
                if gen_box[0] is None:
                    return
                try:
                    next(gen_box[0])
                except StopIteration:
                    gen_box[0] = None

        def stage_b(ci):
            i2 = ci % 2
            lat = ci >= 2
            ac = act_[i2]
            mt = MT[i2]
            ltb = LT8[i2]
            if lat:
                for d in range(2):
                    for h in range(4):
                        col = d * 4 + h
                        k.act(xs[i2][:, d, h * 64:(h + 1) * 64], ac[:, h * 64:(h + 1) * 64], AF.Copy, [ac, HDT], [xs[i2]],
                              scale=HDT[:, ci, col:col + 1])
                    advance()
            for d in range(2):
                pa = psA[d]
                idz = identz[d]
                for h in range(4):
                    col = d * 4 + h
                    ecol = EA[:, ci, col:col + 1]
                    dd = [list(x) for x in ecol.ap]
                    lhsT = bass.AP(ecol.tensor, ecol.offset, [dd[0], [0, 128]])
                    k.mm(pa[:, h * 128:(h + 1) * 128], lhsT, idz[:, :], True, True, [EA, idz], [pa], inc=(h == 3))
                advance()
                lt4 = ltb[:, d * 4:(d + 1) * 4, :].rearrange("p h t -> p (h t)")
                if d == 0:
                    k.op('dve', lambda e: e.tensor_tensor_scan(out=lt4, data0=pa[:, :], data1=ident4[:, :],
                                                               initial=0.0, op0=ALU.mult, op1=ALU.add),
                         [pa, ident4], [ltb])
                else:
                    k.op('dve', lambda e: e.tensor_tensor_scan(out=lt4[:, ::-1], data0=pa[:, ::-1], data1=ident4[:, ::-1],
                                                               initial=0.0, op0=ALU.mult, op1=ALU.add),
                         [pa, ident4], [ltb])
                advance()
                if lat:
                    sc_ = scT[i2][:, :]
                    sd = [list(x) for x in sc_.ap]
                    scb4 = bass.AP(sc_.tensor, sc_.offset, [sd[0], [0, 4], sd[1]])
                    k.op('dve', lambda e: e.tensor_tensor(out=mt[:, d * 4:(d + 1) * 4, :], in0=ltb[:, d * 4:(d + 1) * 4, :],
                                                          in1=scb4, op=ALU.mult), [ltb, scT[i2]], [mt])
                k.op('pool', lambda e: e.tensor_tensor(out=hd8[i2][:, d * 4:d * 4 + 4], in0=HDT[:, ci, d * 4:d * 4 + 4],
                                                       in1=ltb[:, d * 4:d * 4 + 4, (127 if d == 0 else 0)], op=ALU.mult),
                     [HDT, ltb], [hd8[i2]])
                advance()
                for h in range(4):
                    col = d * 4 + h
                    k.act(xd[i2][:, d, h * 64:(h + 1) * 64], ac[:, h * 64:(h + 1) * 64], AF.Copy, [ac, hd8[i2]], [xd[i2]],
                          scale=hd8[i2][:, col:col + 1])
                advance()

        def stage_c(ci):
            i2 = ci % 2
            lat = ci >= 2
            ac = act_[i2]
            xb = xbf[i2]
            bc = bct[i2]
            ec = ecs[i2]
            mt = MT[i2]
            k.mm(psS[:, 0:256], xb[:, 256:384], xd[i2][:, 0, :], True, True, [xb, xd[i2]], [psS])
            k.mm(psS[:, 256:512], xb[:, 256:384], xd[i2][:, 1, :], True, True, [xb, xd[i2]], [psS])
            advance()
            so = sbo[i2]
            k.act(so[:, :], psS[:, 256:512], AF.Copy, [psS], [so])
            pend.append(lambda ci=ci: k.dma('pool', sbd[ci][:, :], so[:, :], reads=[so], writes=[sbd[ci]]))
            if lat:
                for h in range(4):
                    k.mm(psY[:, h * 64:(h + 1) * 64], mt[:, h, :], xs[i2][:, 0, h * 64:(h + 1) * 64], True, False,
                         [mt, xs[i2]], [psY], inc=False)
                    k.mm(psY[:, h * 64:(h + 1) * 64], mt[:, 4 + h, :], xs[i2][:, 1, h * 64:(h + 1) * 64], False, True,
                         [mt, xs[i2]], [psY], inc=(h == 3))
                k.mm(psYo[:, :], bc[:, 1, :], Hfb[:, :], True, True, [bc, Hfb], [psYo])
                y = yp[i2]
                tm = tmp[i2]
                k.act(y[:, :], psY[:, :], AF.Copy, [psY], [y])
                k.op('dve', lambda e: e.tensor_tensor(out=tm[:, :].rearrange("p (h c) -> p c h", c=64),
                                                      in0=psYo[:, :].rearrange("p (h c) -> p c h", c=64),
                                                      in1=ap3(ec[:, 0:4], 64), op=ALU.mult), [psYo, ec], [tm])
                k.op('pool', lambda e: e.tensor_tensor(out=y[:, :], in0=y[:, :], in1=tm[:, :], op=ALU.add), [y, tm], [y])
                k.op('dve', lambda e: e.tensor_tensor(out=tm[:, :].rearrange("p (h c) -> p c h", c=64),
                                                      in0=ac[:, 0:256].rearrange("p (h c) -> p c h", c=64),
                                                      in1=ap3(W['hD'][:, 0:4], 64), op=ALU.mult), [ac, W['hD']], [tm])
                k.op('pool', lambda e: e.tensor_tensor(out=y[:, :], in0=y[:, :], in1=tm[:, :], op=ALU.add), [y, tm], [y])
                pend.append(lambda ci=ci: k.dma('pool', ypd[ci - 2][:, :], y[:, :], reads=[y], writes=[ypd[ci - 2]]))
            advance()
            k.op('dve', lambda e: e.tensor_tensor(out=Hs[:, :].rearrange("p (h c) -> p c h", c=64),
                                                  in0=Hf[:, :].rearrange("p (h c) -> p c h", c=64),
                                                  in1=ap3(ec[:, 8:12], 64), op=ALU.mult), [Hf, ec], [Hs])
            k.act(sfo[i2][:, :], psS[:, 0:256], AF.Copy, [psS], [sfo[i2]])
            k.op('dve', lambda e: e.tensor_tensor(out=Hf[:, :], in0=sfo[i2][:, :], in1=Hs[:, :], op=ALU.add),
                 [Hs, sfo[i2]], [Hf])
            k.act(Hfb[:, :], Hf[:, :], AF.Copy, [Hf], [Hfb])


        NC1 = min(NCH, SSD_LIMIT[1])
        ssd_loads(0)
        if NC1 > 1:
            ssd_loads(1)
        for _ in stage_a(0):
            pass
        for f in pend:
            f()
        pend.clear()
        for ci in range(NC1):
            gen_box[0] = stage_a(ci + 1) if ci + 1 < NC1 else None
            stage_b(ci)
            stage_c(ci)
            while gen_box[0] is not None:
                advance()
            for f in pend:
                f()
            pend.clear()
    if SSD_LIMIT[0] < 2:
        return outs
    with k.scope() as s:
        Hb = k.sb([128, 256], F32, s, "Hb")
        Hbb = k.sb([128, 256], BF16, s, "Hbb")
        k.op('pool', lambda e: e.memset(Hb[:, :], 0.0), [], [Hb])
        k.op('pool', lambda e: e.memset(Hbb[:, :], 0.0), [], [Hbb])
        NR = 4
        sbi = ring(k, s, [128, 256], F32, NR, "sbi")
        eci = ring(k, s, [128, 16], F32, NR, "eci")
        ypi = ring(k, s, [128, 256], F32, NR, "ypi")
        cti = ring(k, s, [128, 128], BF16, NR, "cti")
        zi = ring(k, s, [128, 256], F32, NR, "zi")
        zth = ring(k, s, [128, 256], F32, 2, "zth")
        tm2 = ring(k, s, [128, 256], F32, 2, "tm2")
        vo = ring(k, s, [128, 256], BF16, 2, "vo")
        Hs2 = k.sb([128, 256], F32, s, "Hs2")
        psB = [k.ps([128, 256], F32, s, f"psB{i}") for i in range(2)]
        order = [1, 0] + list(range(NCH - 1, 1, -1))

        def d2_load(oi):
            ci = order[oi]
            r = oi % NR
            k.dma('sp', sbi[r][:, :], sbd[ci][:, :], reads=[sbd[ci]], writes=[sbi[r]])
            k.dma('sp', eci[r][:, :], ecd[ci][:, :], reads=[ecd[ci]], writes=[eci[r]])
            if ci >= 2:
                c = ci - 2
                k.dma('sp', ypi[r][:, :], ypd[c][:, :], reads=[ypd[c]], writes=[ypi[r]])
                k.dma('sp', cti[r][:, :], ctd[c][:, :], reads=[ctd[c]], writes=[cti[r]])
                k.dma('sp', zi[r][:, :], zrows(io, ci, 0, 128, 0, 256), reads=[join], writes=[zi[r]])

        d2_load(0)
        d2_load(1)
        for oi, ci in enumerate(order):
            if oi + 2 < len(order):
                d2_load(oi + 2)
            i2 = oi % 2
            r = oi % NR
            lat = ci >= 2
            sb_ = sbi[r]
            ec = eci[r]
            if lat:
                c = ci - 2
                y = ypi[r]
                ct = cti[r]
                z = zi[r]
                pb = psB[i2]
                k.mm(pb[:, :], ct[:, :], Hbb[:, :], True, True, [ct, Hbb], [pb])
            k.op('dve', lambda e: e.tensor_tensor(out=Hs2[:, :].rearrange("p (h c) -> p c h", c=64),
                                                  in0=Hb[:, :].rearrange("p (h c) -> p c h", c=64),
                                                  in1=ap3(ec[:, 12:16], 64), op=ALU.mult), [Hb, ec], [Hs2])
            k.op('dve', lambda e: e.tensor_tensor(out=Hbb[:, :], in0=Hs2[:, :], in1=sb_[:, :], op=ALU.add),
                 [Hs2, sb_], [Hbb])
            k.op('pool', lambda e: e.tensor_tensor(out=Hb[:, :], in0=Hs2[:, :], in1=sb_[:, :], op=ALU.add),
                 [Hs2, sb_], [Hb])
            if lat:
                tm = tm2[i2]
                k.op('dve', lambda e: e.tensor_tensor(out=tm[:, :].rearrange("p (h c) -> p c h", c=64),
                                                      in0=pb[:, :].rearrange("p (h c) -> p c h", c=64),
                                                      in1=ap3(ec[:, 4:8], 64), op=ALU.mult), [pb, ec], [tm])
                k.op('pool', lambda e: e.tensor_tensor(out=y[:, :], in0=y[:, :], in1=tm[:, :], op=ALU.add), [y, tm], [y])
                k.act(zth[i2][:, :], z[:, :], AF.Tanh, [z], [zth[i2]], scale=0.5)
                k.op('dve', lambda e: e.scalar_tensor_tensor(out=z[:, :], in0=zth[i2][:, :], scalar=1.0, in1=z[:, :],
                                                             op0=ALU.add, op1=ALU.mult), [zth[i2], z], [z])
                v = vo[i2]
                k.op('dve', lambda e: e.scalar_tensor_tensor(out=v[:, :], in0=y[:, :], scalar=0.5, in1=z[:, :],
                                                             op0=ALU.mult, op1=ALU.mult), [y, z], [v])
                dst = Buf(bass.AP(io['os'].tensor, io['os'].offset + c * 256, [[64 * 256, 128], [1, 256]]))
                k.dma('pool', dst.t, v[:, :], reads=[v], writes=[dst])
                outs.append(dst)
    return outs


def build_phase1(nc, k, es, io, G, stages=3):
    W0 = p1_shared(k, es, io)
    htd = p1_prework(k, es, io, W0)
    outs = []
    for q in range(G):
        k.sfx = f"_g{q}"
        iog = io_group(io, q)
        with k.scope() as sg:
            W = p1_group(k, sg, iog, W0)
            lug, zxb = p1_inproj(k, sg, iog, W, htd)
            if stages < 3:
                outs += [b_ for r in lug for b_ in r] + list(zxb.values())
            if stages >= 2:
                outs += p1_lru(k, sg, iog, W, lug)
            if stages >= 3:
                outs += p1_ssd(k, sg, iog, W, zxb)
    k.sfx = ""
    return outs


def make_phase1_nc(stages=3):
    nc = bass.Bass("TRN2", target_bir_lowering=False)
    io = p1_decl(nc, 1, False)
    with ExitStack() as es:
        es.enter_context(nc.allow_non_contiguous_dma("small strided parameter loads"))
        k = KB(nc, es)
        outs = build_phase1(nc, k, es, io, 1, stages)
        k.finish(outs)
    return nc


def p1_group_arrays(inp, q):
    w_in = inp['w_in'][0]
    g = q // 2
    cs = slice(q * 256, (q + 1) * 256)
    cols = np.concatenate([np.arange(q * 256, (q + 1) * 256), 1024 + np.arange(q * 256, (q + 1) * 256),
                           2048 + np.arange(q * 256, (q + 1) * 256), 3072 + np.arange(q * 256, (q + 1) * 256),
                           4096 + g * 128 + np.arange(128), 4352 + g * 128 + np.arange(128),
                           4608 + 4 * q + np.arange(4), 4624 + 4 * q + np.arange(4)])
    wbd = np.zeros((8, 128, 128), np.float32)
    for d in range(2):
        for gi, nm in enumerate(('lru_wa', 'lru_wx')):
            for blk in range(2):
                for hh in range(2):
                    head = 4 * q + 2 * blk + hh
                    wbd[d * 4 + gi * 2 + blk, hh * 64:(hh + 1) * 64, hh * 64:(hh + 1) * 64] = inp[nm][0, d, head]
    ccols = np.concatenate([np.arange(q * 256, (q + 1) * 256), 1024 + g * 128 + np.arange(128),
                            1280 + g * 128 + np.arange(128)])
    hs = slice(4 * q, 4 * q + 4)
    return {
        'w_in1': w_in[:, cols],
        'l_cw': inp['lru_conv_w'][0][:, cs], 'l_cb': inp['lru_conv_b'][0][cs],
        'l_wbd': wbd,
        'l_ba': inp['lru_ba'][0][:, cs], 'l_bx': inp['lru_bx'][0][:, cs], 'l_lam': inp['lru_lambda'][0][:, cs],
        's_cw': inp['ssd_conv_w'][0][:, ccols], 's_cb': inp['ssd_conv_b'][0][ccols],
        's_alog': inp['ssd_a_log'][0][:, hs].reshape(8), 's_dtb': inp['ssd_dt_bias'][0][:, hs].reshape(8),
        's_d': inp['ssd_d'][0][hs],
    }


def p1_common_arrays(inp, b, fused):
    d = {
        'x1': np.ascontiguousarray(inp['x'][b]), 'ctx1': np.ascontiguousarray(inp['ctx'][b]),
        'c_b': np.ascontiguousarray(inp['c'][b]), 'c_ctx': np.ascontiguousarray(inp['c_ctx']),
        'norm1_g': np.ascontiguousarray(inp['norm1_g'][0]),
        'ident': np.eye(128, dtype=np.float32),
        'triu': np.triu(np.ones((128, 128), np.float32)),
        'tril': np.tril(np.ones((128, 128), np.float32)),
        'ones': np.ones((128, 128), np.float32),
        'identzf': np.diag(np.r_[0.0, np.ones(127)]).astype(np.float32),
        'identzb': np.diag(np.r_[np.ones(127), 0.0]).astype(np.float32),
    }
    if fused:
        d['ada_w'] = np.ascontiguousarray(inp['ada_w'][0])
        d['ada_b'] = np.ascontiguousarray(inp['ada_b'][0])
    else:
        d['ada_w1'] = np.ascontiguousarray(inp['ada_w'][0][:, 0:2048])
        d['ada_b1'] = np.ascontiguousarray(inp['ada_b'][0][0:2048])
    return d


def p1_inputs(inp):
    maps = []
    for b in range(2):
        for q in range(4):
            m = p1_common_arrays(inp, b, False)
            for n, v in p1_group_arrays(inp, q).items():
                m[n] = np.ascontiguousarray(v[None])
            maps.append(m)
    return maps


def fused_inputs(inp):
    groups = [p1_group_arrays(inp, q) for q in range(4)]
    stacked = {n: np.ascontiguousarray(np.stack([g[n] for g in groups], 0)) for n in groups[0]}
    maps = []
    for b in range(2):
        for kq in range(4):
            m = p1_common_arrays(inp, b, True)
            m.update(stacked)
            t0 = kq * NT2
            rows = list(range(t0, t0 + NT2)) + [max(t0 - 1, 0), min(t0 + NT2, 8191)]
            hm = np.ones((128, 2), np.float32)
            if kq == 0:
                hm[:, 0] = 0.0
            if kq == 3:
                hm[:, 1] = 0.0
            sel = np.zeros((128, 4), np.float32)
            sel[:, kq] = 1.0
            m.update({
                'x2': np.ascontiguousarray(inp['x'][b][rows]), 'hmask': hm, 'sel': sel,
                'final_g': np.ascontiguousarray(inp['final_norm_g']),
                'norm2_g': np.ascontiguousarray(inp['norm2_g'][0]),
                'ssd_norm_g': np.ascontiguousarray(inp['ssd_norm_g'][0]),
                'w_out': np.ascontiguousarray(inp['w_out'][0]),
                'w_up': np.ascontiguousarray(inp['ffn_w_up'][0]),
                'w_down': np.ascontiguousarray(inp['ffn_w_down'][0]),
                'ffn_cw': np.ascontiguousarray(inp['ffn_conv_w'][0]),
                'ffn_cb': np.ascontiguousarray(inp['ffn_conv_b'][0]),
            })
            maps.append(m)
    return maps


def make_fused_nc():
    nc = bass.Bass("TRN2", target_bir_lowering=False)
    io1 = p1_decl(nc, 4, True)
    io2 = p2_decl(nc, io1)
    with ExitStack() as es:
        es.enter_context(nc.allow_non_contiguous_dma("small strided parameter loads"))
        k = KB(nc, es)
        with k.scope() as s1:
            outs1 = build_phase1(nc, k, s1, io1, 4)
        k.sfx = "_p2"
        pjoin = k.sb([128, 1], F32, es, "pjoin")
        k.op('pool', lambda e: e.memset(pjoin[:, :], 0.0), outs1, [pjoin])
        sel = k.sb([128, 4], F32, es, "sel")
        k.dma('sp', sel[:, :], io2['sel'], writes=[sel])
        outs = build_phase2(nc, k, es, io2, fz=(io1['ol'], io1['os'], pjoin, sel))
        k.finish(outs)
    return nc


def kernel_2launch(inp):
    nc1 = make_phase1_nc()
    res1 = run_bass_kernel_spmd(nc1, p1_inputs(inp), core_ids=list(range(8)))
    r0 = res1.results[0]['ol']
    mixl = np.zeros((2, 8192, 1024), r0.dtype)
    mixs = np.zeros((2, 8192, 1024), r0.dtype)
    for b in range(2):
        for q in range(4):
            r = res1.results[b * 4 + q]
            mixl[b][:, q * 256:(q + 1) * 256] = r['ol'][0]
            mixs[b][:, q * 256:(q + 1) * 256] = r['os'][0]
    nc2 = make_phase2_nc()
    res2 = run_bass_kernel_spmd(nc2, p2_inputs(inp, mixl, mixs), core_ids=list(range(8)))
    return np.stack([np.concatenate([res2.results[b * 4 + kq]['out2'] for kq in range(4)], 0) for b in range(2)])


def kernel(**inputs):
    inp = {k_: np.asarray(v) for k_, v in inputs.items()}
    nc = make_fused_nc()
    res = run_bass_kernel_spmd(nc, fused_inputs(inp), core_ids=list(range(8)))
    out = np.stack([np.concatenate([res.results[b * 4 + kq]['out2'] for kq in range(4)], 0) for b in range(2)])
    return out.astype(np.float32)
```

```python
import numpy as np
from contextlib import ExitStack, contextmanager
import concourse.bass as bass
import concourse.mybir as mybir
from concourse.bass_utils import run_bass_kernel_spmd

F32 = mybir.dt.float32
BF16 = mybir.dt.bfloat16
AF = mybir.ActivationFunctionType
ALU = mybir.AluOpType
EPS = 1e-6
GC = 0.7978845608028654
DBG_KIND = "Internal"
SSD_LIMIT = (9, 66)


class Buf:
    __slots__ = ("t", "w", "r", "wx")

    def __init__(self, t):
        self.t = t
        self.w = None
        self.r = {}
        self.wx = []

    def __getitem__(self, idx):
        return self.t[idx]


class KB:
    NS = 8

    def __init__(self, nc, es):
        self.nc = nc
        self.es = es
        self.eng = {'pe': nc.tensor, 'act': nc.scalar, 'dve': nc.vector, 'pool': nc.gpsimd, 'sp': nc.sync}
        self.sem = {}
        self.cnt = {}
        self.known = {}
        for e in self.eng:
            self.sem[e] = es.enter_context(nc.semaphore("s_" + e))
            self.cnt[e] = 0
            self.known[e] = {}
        self.dsem = {}
        self.dcnt = {}
        for q in ('sp', 'pool', 'act'):
            self.dsem[q] = [es.enter_context(nc.semaphore(f"d_{q}{i}")) for i in range(self.NS)]
            self.dcnt[q] = 0
        self.nbuf = 0
        self.sfx = ""
        self.free_tok = {}
        self._scopes = {}

    @contextmanager
    def scope(self):
        with ExitStack() as s:
            self._scopes[id(s)] = []
            try:
                yield s
            finally:
                for b in self._scopes.pop(id(s)):
                    toks = list(b.r.items()) + list(b.wx)
                    if b.w is not None:
                        toks.append(b.w)
                    for key, val in toks:
                        if val > self.free_tok.get(key, 0):
                            self.free_tok[key] = val

    def _new(self, t, es):
        b = Buf(t)
        b.r = dict(self.free_tok)
        if es is not None and id(es) in self._scopes:
            self._scopes[id(es)].append(b)
        return b

    def sb(self, shape, dt, es=None, name=None):
        self.nbuf += 1
        t = (es or self.es).enter_context(self.nc.sbuf_tensor("S_" + (name or f"sb{self.nbuf}") + self.sfx, list(shape), dt))
        return self._new(t, es)

    def ps(self, shape, dt, es=None, name=None):
        self.nbuf += 1
        t = (es or self.es).enter_context(self.nc.psum_tensor("P_" + (name or f"ps{self.nbuf}") + self.sfx, list(shape), dt))
        return self._new(t, es)

    def _semh(self, key):
        return self.sem[key] if isinstance(key, str) else self.dsem[key[1]][key[2]]

    def _wait(self, e, deps):
        kn = self.known[e]
        best = {}
        for key, val in deps:
            if val > best.get(key, 0):
                best[key] = val
        for key, val in best.items():
            if key == e and e == 'pe':
                continue
            if kn.get(key, 0) >= val:
                continue
            self.eng[e].wait_ge(self._semh(key), val)
            kn[key] = val

    @staticmethod
    def _deps(reads, writes):
        deps = []
        for b in reads:
            if b.w is not None:
                deps.append(b.w)
            deps.extend(b.wx)
        for b in writes:
            if b.w is not None:
                deps.append(b.w)
            deps.extend(b.wx)
            deps.extend(b.r.items())
        return deps

    def op(self, e, fn, reads=(), writes=(), inc=True):
        self._wait(e, self._deps(reads, writes))
        ins = fn(self.eng[e])
        n = self.cnt[e] + 1
        if inc:
            ins.then_inc(self.sem[e], 1)
            self.cnt[e] = n
        for b in reads:
            b.r[e] = n
        for b in writes:
            b.w = (e, n)
            b.wx = []
            b.r = {}
        return ins

    def dma(self, q, out, in_, reads=(), writes=(), merge=False):
        if merge:
            deps = self._deps(reads, [])
            for b in writes:
                deps.extend(b.r.items())
            self._wait(q, deps)
        else:
            self._wait(q, self._deps(reads, writes))
        i = self.dcnt[q]
        idx = i % self.NS
        rnd = i // self.NS
        self.dcnt[q] = i + 1
        key = ('d', q, idx)
        if rnd > 0:
            self._wait(q, [(key, 16 * rnd)])
        ins = self.eng[q].dma_start(out=out, in_=in_)
        ins.then_inc(self.dsem[q][idx], 16)
        val = 16 * (rnd + 1)
        for b in reads:
            b.r[key] = val
        for b in writes:
            if merge and b.w is not None:
                b.wx.append((key, val))
            else:
                b.w = (key, val)
                b.wx = []
                b.r = {}
        return ins

    def finish(self, bufs):
        deps = []
        for b in bufs:
            if b.w is not None:
                deps.append(b.w)
            deps.extend(b.wx)
        self._wait('sp', deps)

    def act(self, out, in_, func, reads, writes, bias=None, scale=None, accum_out=None):
        kw = {}
        if bias is not None:
            kw['bias'] = bias
        if scale is not None:
            kw['scale'] = scale
        if accum_out is not None:
            kw['accum_out'] = accum_out
        return self.op('act', lambda e: e.activation(out=out, in_=in_, func=func, **kw), reads, writes)

    def mm(self, out, lhsT, rhs, start, stop, reads, writes, inc=None):
        if inc is None:
            inc = stop
        return self.op('pe', lambda e: e.matmul(out, lhsT, rhs, start=start, stop=stop), reads, writes, inc=inc)

    def tr(self, out, in_, ident, reads, writes, inc=True):
        return self.op('pe', lambda e: e.transpose(out, in_, ident), reads, writes, inc=inc)


def bcast_rows(ap1d, n, parts=128):
    return bass.AP(ap1d.tensor, ap1d.offset, [[0, parts], [1, n]])


def dram(nc, name, shape, dt, kind="ExternalInput"):
    return nc.dram_tensor(name, list(shape), dt, kind=kind).ap()


def matvec_part(k, es, w_ap, ncols, vec, nv, out, bias=None, tag="mv"):
    with k.scope() as s:
        st = [k.sb([128, 8, 512], F32, s, f"{tag}_st{i}") for i in range(2)]
        pp = [k.ps([128, 4, nv], F32, s, f"{tag}_ps{i}") for i in range(2)]
        nch = (ncols + 511) // 512
        for c in range(nch):
            c0 = c * 512
            cw = min(512, ncols - c0)
            sb = st[c % 2]
            ps = pp[c % 2]
            k.dma('sp', sb[:, :, 0:cw], w_ap[:, c0:c0 + cw].rearrange("(j p) c -> p j c", p=128), writes=[sb])
            nb = cw // 128
            for m in range(nb):
                for j in range(8):
                    k.mm(ps[:, m, :], sb[:, j, m * 128:(m + 1) * 128], vec[:, j, :], start=(j == 0), stop=(j == 7),
                         reads=[sb, vec], writes=[ps])
            mb0 = c0 // 128
            if bias is None:
                k.op('dve', lambda e: e.tensor_copy(out=out[:, mb0:mb0 + nb, :], in_=ps[:, 0:nb, :]), [ps], [out])
            else:
                for v in range(nv):
                    k.op('dve', lambda e: e.tensor_tensor(out=out[:, mb0:mb0 + nb, v], in0=ps[:, 0:nb, v],
                                                          in1=bias[:, mb0:mb0 + nb], op=ALU.add), [ps, bias], [out])


def matvec_bcast(k, es, w_ap, ncols, vrep, out, bias_ap=None, tag="mb"):
    vb, vfn = vrep
    with k.scope() as s:
        st = [k.sb([128, 8, 512], F32, s, f"{tag}_st{i}") for i in range(2)]
        pp = [k.ps([128, 512], F32, s, f"{tag}_ps{i}") for i in range(2)]
        bb = None
        if bias_ap is not None:
            bb = k.sb([128, ncols], F32, s, f"{tag}_bias")
            k.dma('sp', bb[:, :], bcast_rows(bias_ap, ncols), writes=[bb])
        for c in range((ncols + 511) // 512):
            c0 = c * 512
            cw = min(512, ncols - c0)
            sb = st[c % 2]
            ps = pp[c % 2]
            k.dma('sp', sb[:, :, 0:cw], w_ap[:, c0:c0 + cw].rearrange("(j p) c -> p j c", p=128), writes=[sb])
            for j in range(8):
                k.mm(ps[:, 0:cw], vfn(j), sb[:, j, 0:cw], start=(j == 0), stop=(j == 7), reads=[sb, vb], writes=[ps])
            if bb is None:
                k.op('dve', lambda e: e.tensor_copy(out=out[:, c0:c0 + cw], in_=ps[:, 0:cw]), [ps], [out])
            else:
                k.op('dve', lambda e: e.tensor_tensor(out=out[:, c0:c0 + cw], in0=ps[:, 0:cw],
                                                      in1=bb[:, c0:c0 + cw], op=ALU.add), [ps, bb], [out])


def make_rep(k, es, identf, vec, v, tag):
    rep = k.sb([128, 8, 128], F32, es, tag)
    for j in range(8):
        k.op('dve', lambda e: e.tensor_scalar(out=rep[:, j, :], in0=identf[:, :], scalar1=0.0, scalar2=None,
                                              op0=ALU.mult), [identf], [rep])
        k.op('dve', lambda e: e.tensor_scalar(out=rep[:, j, :], in0=rep[:, j, :], scalar1=vec[:, j, v:v + 1], scalar2=None,
                                              op0=ALU.add), [rep, vec], [rep])
    return rep


def silu_vec(k, es, c_ap, nv, tag):
    raw = k.sb([128, 8, nv], F32, es, f"{tag}_raw")
    th = k.sb([128, 8, nv], F32, es, f"{tag}_th")
    out = k.sb([128, 8, nv], F32, es, f"{tag}_silu")
    for v, ap in enumerate(c_ap):
        k.dma('sp', raw[:, :, v], ap.rearrange("(j p) -> p j", p=128), writes=[raw])
    k.act(th[:, :, :], raw[:, :, :], AF.Tanh, [raw], [th], scale=0.5)
    k.op('dve', lambda e: e.scalar_tensor_tensor(out=out[:, :, :], in0=th[:, :, :], scalar=1.0, in1=raw[:, :, :],
                                                 op0=ALU.add, op1=ALU.mult), [th, raw], [out])
    k.op('dve', lambda e: e.tensor_scalar(out=out[:, :, :], in0=out[:, :, :], scalar1=0.5, scalar2=None,
                                          op0=ALU.mult), [out], [out])
    return out


def rstd_from_ss(k, ss, tmp, rstd, n):
    k.op('dve', lambda e: e.tensor_scalar(out=tmp[:, :], in0=ss[:, :], scalar1=1.0 / n, scalar2=EPS,
                                          op0=ALU.mult, op1=ALU.add), [ss], [tmp])
    k.act(tmp[:, :], tmp[:, :], AF.Sqrt, [tmp], [tmp])
    k.op('dve', lambda e: e.reciprocal(out=rstd[:, :], in_=tmp[:, :]), [tmp], [rstd])


NT2 = 2048


def build_phase2(nc, k, es, io, fz=None):
    mixl, mixs, x2, out2 = io.get('mixl'), io.get('mixs'), io['x2'], io['out2']
    xmid_d = [Buf(io['xmid'][t * 128:(t + 1) * 128, :]) for t in range(16)]

    identf = k.sb([128, 128], F32, es, "identf")
    identb = k.sb([128, 128], BF16, es, "identb")
    k.dma('sp', identf[:, :], io['ident'], writes=[identf])
    k.op('dve', lambda e: e.tensor_copy(out=identb[:, :], in_=identf[:, :]), [identf], [identb])
    hmask = k.sb([128, 2], F32, es, "hmask")
    k.dma('sp', hmask[:, :], io['hmask'], writes=[hmask])

    sc = silu_vec(k, es, [io['c_b']], 1, "c2")
    screp = make_rep(k, es, identf, sc, 0, "screp")
    srep = (screp, lambda j: screp[:, j, :])
    adaw, adab = io['ada_w'], io['ada_b']
    g1bc = k.sb([128, 1024], F32, es, "g1bc")
    g2bc = k.sb([128, 1024], F32, es, "g2bc")
    fgbc = k.sb([128, 1024], F32, es, "fgbc")
    matvec_bcast(k, es, adaw[:, 2048:3072], 1024, srep, g1bc, adab[2048:3072], "g1")
    matvec_bcast(k, es, adaw[:, 5120:6144], 1024, srep, g2bc, adab[5120:6144], "g2")
    k.op('dve', lambda e: e.tensor_scalar(out=g2bc[:, :], in0=g2bc[:, :], scalar1=0.5, scalar2=None, op0=ALU.mult),
         [g2bc], [g2bc])
    k.dma('sp', fgbc[:, :], bcast_rows(io['final_g'], 1024), writes=[fgbc])
    adab_p = k.sb([128, 16], F32, es, "adab_p")
    k.dma('sp', adab_p[:, :], adab[3072:5120].rearrange("(m p) -> p m", p=128), writes=[adab_p])
    shsc = k.sb([128, 16, 1], F32, es, "shsc")
    matvec_part(k, es, adaw[:, 3072:5120], 2048, sc, 1, shsc, adab_p, "shsc")
    n2g = k.sb([128, 8], F32, es, "n2g")
    k.dma('sp', n2g[:, :], io['norm2_g'].rearrange("(m p) -> p m", p=128), writes=[n2g])
    gs2 = k.sb([128, 8], F32, es, "gs2")
    k.op('dve', lambda e: e.scalar_tensor_tensor(out=gs2[:, :], in0=shsc[:, 8:16, 0], scalar=1.0, in1=n2g[:, :],
                                                 op0=ALU.add, op1=ALU.mult), [shsc, n2g], [gs2])
    sh2 = k.sb([128, 8, 1], F32, es, "sh2")
    k.op('dve', lambda e: e.tensor_copy(out=sh2[:, :, :], in_=shsc[:, 0:8, :]), [shsc], [sh2])
    bias2 = k.sb([128, 48, 1], F32, es, "bias2")
    matvec_part(k, es, io['w_up'], 6144, sh2, 1, bias2, None, "b2")
    cw = k.sb([128, 48, 3], F32, es, "cw")
    cb = k.sb([128, 48], F32, es, "cb")
    for tp in range(3):
        k.dma('sp', cw[:, :, tp], io['ffn_cw'][tp].rearrange("(m p) -> p m", p=128), writes=[cw])
    k.dma('sp', cb[:, :], io['ffn_cb'].rearrange("(m p) -> p m", p=128), writes=[cb])
    sng = k.sb([128, 8], F32, es, "sng")
    k.dma('sp', sng[:, :], io['ssd_norm_g'].rearrange("(m p) -> p m", p=128), writes=[sng])

    fT = k.sb([128, 8, 2050], BF16, es, "fT")

    with k.scope() as s2:
        wo = k.sb([128, 16, 1024], BF16, s2, "wo")
        wst = [k.sb([128, 1024], F32, s2, f"wst{i}") for i in range(2)]
        for kb in range(16):
            st = wst[kb % 2]
            k.dma('sp', st[:, :], io['w_out'][kb * 128:(kb + 1) * 128, :], writes=[st])
            if kb < 8:
                k.op('dve', lambda e: e.tensor_tensor(out=wo[:, kb, :], in0=st[:, :], in1=g1bc[:, :], op=ALU.mult),
                     [st, g1bc], [wo])
            else:
                k.op('dve', lambda e: e.scalar_tensor_tensor(out=wo[:, kb, :], in0=st[:, :], scalar=sng[:, kb - 8:kb - 7],
                                                             in1=g1bc[:, :], op0=ALU.mult, op1=ALU.mult),
                     [st, sng, g1bc], [wo])
        ml = [k.sb([128, 1024], BF16, s2, f"ml{i}") for i in range(2)]
        if fz is not None:
            cl = [[k.sb([128, 4, 256], BF16, s2, f"cl{i}_{kk}") for kk in range(4)] for i in range(2)]
            cs_ = [[k.sb([128, 4, 256], BF16, s2, f"cs{i}_{kk}") for kk in range(4)] for i in range(2)]
        ms = [k.sb([128, 1024], BF16, s2, f"ms{i}") for i in range(2)]
        xt = [k.sb([128, 1024], F32, s2, f"xt{i}") for i in range(2)]
        vs = [k.sb([128, 1024], BF16, s2, f"vs{i}") for i in range(2)]
        junk = [k.sb([128, 1024], BF16, s2, f"junk{i}") for i in range(2)]
        mixT = [k.sb([128, 16, 128], BF16, s2, f"mixT{i}") for i in range(2)]
        xm = [k.sb([128, 1024], F32, s2, f"xm{i}") for i in range(2)]
        xn = [k.sb([128, 1024], BF16, s2, f"xn{i}") for i in range(2)]
        st1 = [k.sb([128, 4], F32, s2, f"st1_{i}") for i in range(2)]
        st2 = [k.sb([128, 4], F32, s2, f"st2_{i}") for i in range(2)]
        psT = k.ps([128, 16, 128], BF16, s2, "psT")
        psO = k.ps([128, 1024], F32, s2, "psO")
        psF = k.ps([128, 8, 128], BF16, s2, "psF")
        for t in range(17):
            i = t % 2
            P = 128 if t < 16 else 2
            r0 = t * 128
            if fz is None:
                k.dma('sp', ml[i][0:P, :], mixl[r0:r0 + P, :], writes=[ml[i]])
                k.dma('sp', ms[i][0:P, :], mixs[r0:r0 + P, :], writes=[ms[i]])
            else:
                ol_all, os_all, pjoin, sel = fz
                for (dstb, srcall, cbufs) in ((ml[i], ol_all, cl[i]), (ms[i], os_all, cs_[i])):
                    for kk in range(4):
                        cb_ = cbufs[kk]
                        if t < 16:
                            rr = kk * 2048 + t * 128
                            k.dma('sp', cb_[:, :, :], srcall[:, rr:rr + 128, :].rearrange("q p c -> p q c"),
                                  reads=[pjoin], writes=[cb_])
                        else:
                            lrow = max(kk * 2048 - 1, 0)
                            rrow = min((kk + 1) * 2048, 8191)
                            k.dma('sp', cb_[0:1, :, :], srcall[:, lrow:lrow + 1, :].rearrange("q p c -> p q c"),
                                  reads=[pjoin], writes=[cb_])
                            k.dma('sp', cb_[1:2, :, :], srcall[:, rrow:rrow + 1, :].rearrange("q p c -> p q c"),
                                  reads=[pjoin], writes=[cb_], merge=True)
                    d2 = dstb[0:P, :]
                    k.op('dve', lambda e: e.tensor_scalar(out=d2, in0=cbufs[0][0:P, :, :].rearrange("p q c -> p (q c)"),
                                                          scalar1=sel[0:P, 0:1], scalar2=None, op0=ALU.mult),
                         [cbufs[0], sel], [dstb])
                    for kk in range(1, 4):
                        k.op('dve', lambda e: e.scalar_tensor_tensor(
                            out=d2, in0=cbufs[kk][0:P, :, :].rearrange("p q c -> p (q c)"), scalar=sel[0:P, kk:kk + 1],
                            in1=d2, op0=ALU.mult, op1=ALU.add), [cbufs[kk], sel, dstb], [dstb])
            k.dma('sp', xt[i][0:P, :], x2[r0:r0 + P, :], writes=[xt[i]])
            a = st1[i]
            k.act(junk[i][0:P, :], ms[i][0:P, :], AF.Square, [ms[i]], [junk[i], a], accum_out=a[0:P, 0:1])
            k.op('dve', lambda e: e.tensor_scalar(out=a[0:P, 1:2], in0=a[0:P, 0:1], scalar1=1.0 / 1024, scalar2=EPS,
                                                  op0=ALU.mult, op1=ALU.add), [a], [a])
            k.act(a[0:P, 1:2], a[0:P, 1:2], AF.Sqrt, [a], [a])
            k.op('dve', lambda e: e.reciprocal(out=a[0:P, 2:3], in_=a[0:P, 1:2]), [a], [a])
            k.op('dve', lambda e: e.tensor_scalar(out=vs[i][0:P, :], in0=ms[i][0:P, :], scalar1=a[0:P, 2:3], scalar2=None,
                                                  op0=ALU.mult), [ms[i], a], [vs[i]])
            for kb in range(16):
                src = ml[i] if kb < 8 else vs[i]
                c0 = (kb % 8) * 128
                k.tr(psT[:, kb, 0:P], src[0:P, c0:c0 + 128], identb[0:P, 0:P], [src, identb], [psT], inc=(kb == 15))
            k.act(mixT[i][:, 0:8, 0:P], psT[:, 0:8, 0:P], AF.Copy, [psT], [mixT[i]])
            k.op('dve', lambda e: e.tensor_copy(out=mixT[i][:, 8:16, 0:P], in_=psT[:, 8:16, 0:P]), [psT], [mixT[i]])
            for h in range(2):
                for kb in range(16):
                    k.mm(psO[0:P, h * 512:(h + 1) * 512], mixT[i][:, kb, 0:P], wo[:, kb, h * 512:(h + 1) * 512],
                         start=(kb == 0), stop=(kb == 15), reads=[mixT[i], wo], writes=[psO])
            k.op('dve', lambda e: e.tensor_tensor(out=xm[i][0:P, :], in0=psO[0:P, :], in1=xt[i][0:P, :], op=ALU.add),
                 [psO, xt[i]], [xm[i]])
            if t < 16:
                k.dma('pool', xmid_d[t][:, :], xm[i][:, :], reads=[xm[i]], writes=[xmid_d[t]])
            b = st2[i]
            k.act(junk[i][0:P, :], xm[i][0:P, :], AF.Square, [xm[i]], [junk[i], b], accum_out=b[0:P, 0:1])
            k.op('dve', lambda e: e.tensor_scalar(out=b[0:P, 1:2], in0=b[0:P, 0:1], scalar1=1.0 / 1024, scalar2=EPS,
                                                  op0=ALU.mult, op1=ALU.add), [b], [b])
            k.act(b[0:P, 1:2], b[0:P, 1:2], AF.Sqrt, [b], [b])
            k.op('dve', lambda e: e.reciprocal(out=b[0:P, 2:3], in_=b[0:P, 1:2]), [b], [b])
            k.op('dve', lambda e: e.tensor_scalar(out=xn[i][0:P, :], in0=xm[i][0:P, :], scalar1=b[0:P, 2:3], scalar2=None,
                                                  op0=ALU.mult), [xm[i], b], [xn[i]])
            for j in range(8):
                k.tr(psF[:, j, 0:P], xn[i][0:P, j * 128:(j + 1) * 128], identb[0:P, 0:P], [xn[i], identb], [psF],
                     inc=(j == 7))
            if t < 16:
                k.act(fT[:, :, 1 + r0:1 + r0 + 128], psF[:, :, :], AF.Copy, [psF], [fT])
            else:
                k.act(fT[:, :, 0:1], psF[:, :, 0:1], AF.Copy, [psF], [fT])
                k.act(fT[:, :, 2049:2050], psF[:, :, 1:2], AF.Copy, [psF], [fT])

    with k.scope() as s3:
        wup = k.sb([128, 8, 6144], BF16, s3, "wup")
        wdn_d = [Buf(io['wdn_bf'][m * 128:(m + 1) * 128, :]) for m in range(24)]
        with k.scope() as sp:
            stg = [k.sb([128, 2048], F32, sp, f"stg{i}") for i in range(2)]
            n = 0
            for j in range(8):
                for c in range(3):
                    st = stg[n % 2]
                    n += 1
                    k.dma('sp', st[:, :], io['w_up'][j * 128:(j + 1) * 128, c * 2048:(c + 1) * 2048], writes=[st])
                    k.op('dve', lambda e: e.tensor_scalar(out=wup[:, j, c * 2048:(c + 1) * 2048], in0=st[:, :],
                                                          scalar1=gs2[:, j:j + 1], scalar2=None, op0=ALU.mult),
                         [st, gs2], [wup])
            dst = [k.sb([128, 1024], F32, sp, f"dst{i}") for i in range(2)]
            dbf = [k.sb([128, 1024], BF16, sp, f"dbf{i}") for i in range(2)]
            for m in range(24):
                st = dst[m % 2]
                ob = dbf[m % 2]
                k.dma('sp', st[:, :], io['w_down'][m * 128:(m + 1) * 128, :], writes=[st])
                k.op('pool', lambda e: e.tensor_tensor(out=ob[:, :], in0=st[:, :], in1=g2bc[:, :], op=ALU.mult),
                     [st, g2bc], [ob])
                k.dma('pool', wdn_d[m][:, :], ob[:, :], reads=[ob], writes=[wdn_d[m]])
        NB = 3
        wd = [k.sb([128, 1024], BF16, s3, f"wd{i}") for i in range(NB)]
        uv = [k.sb([128, 258], F32, s3, f"uv{i}") for i in range(2)]
        ug = [k.sb([128, 258], F32, s3, f"ug{i}") for i in range(2)]
        cv = [k.sb([128, 256], F32, s3, f"cv{i}") for i in range(2)]
        cg = [k.sb([128, 256], F32, s3, f"cg{i}") for i in range(2)]
        t1 = [k.sb([128, 256], F32, s3, f"t1{i}") for i in range(2)]
        t2 = [k.sb([128, 256], F32, s3, f"t2{i}") for i in range(2)]
        t3 = [k.sb([128, 256], F32, s3, f"t3{i}") for i in range(2)]
        aT = [k.sb([128, 256], BF16, s3, f"aT{i}") for i in range(2)]
        xmt = [k.sb([128, 1024], F32, s3, f"xmt{i}") for i in range(2)]
        xo = [k.sb([128, 1024], F32, s3, f"xo{i}") for i in range(2)]
        jk = [k.sb([128, 1024], F32, s3, f"jk{i}") for i in range(2)]
        st3 = [k.sb([128, 4], F32, s3, f"st3_{i}") for i in range(2)]
        psV = [k.ps([128, 512], F32, s3, f"psV{i}") for i in range(2)]
        psG = [k.ps([128, 512], F32, s3, f"psG{i}") for i in range(2)]
        psD = [k.ps([128, 1024], F32, s3, f"psD{i}") for i in range(2)]
        out_bufs = []

        def stage_a(it):
            g, m = divmod(it, 24)
            c0 = 256 * g
            i = it % 2
            w = wd[it % NB]
            k.dma('sp', w[:, :], wdn_d[m][:, :], reads=[wdn_d[m]], writes=[w])
            for (ps, mb) in ((psV[i], m), (psG[i], 24 + m)):
                for j in range(8):
                    k.mm(ps[:, 0:258], wup[:, j, mb * 128:(mb + 1) * 128], fT[:, j, c0:c0 + 258],
                         start=(j == 0), stop=(j == 7), reads=[wup, fT], writes=[ps])

        def stage_b(it):
            g, m = divmod(it, 24)
            i = it % 2
            k.act(uv[i][:, :], psV[i][:, 0:258], AF.Identity, [psV[i], bias2], [uv[i]], bias=bias2[:, m, :])
            k.act(ug[i][:, :], psG[i][:, 0:258], AF.Identity, [psG[i], bias2], [ug[i]], bias=bias2[:, 24 + m, :])
            if g == 0:
                for u in (uv[i], ug[i]):
                    k.op('dve', lambda e: e.tensor_scalar(out=u[:, 0:1], in0=u[:, 0:1], scalar1=hmask[:, 0:1],
                                                          scalar2=None, op0=ALU.mult), [u, hmask], [u])
            if g == 7:
                for u in (uv[i], ug[i]):
                    k.op('dve', lambda e: e.tensor_scalar(out=u[:, 257:258], in0=u[:, 257:258], scalar1=hmask[:, 1:2],
                                                          scalar2=None, op0=ALU.mult), [u, hmask], [u])
            for (u, c, mb) in ((ug[i], cg[i], 24 + m), (uv[i], cv[i], m)):
                k.op('dve', lambda e: e.tensor_scalar(out=c[:, :], in0=u[:, 0:256], scalar1=cw[:, mb, 0:1],
                                                      scalar2=cb[:, mb:mb + 1], op0=ALU.mult, op1=ALU.add),
                     [u, cw, cb], [c])
                for tp in (1, 2):
                    k.op('dve', lambda e: e.scalar_tensor_tensor(out=c[:, :], in0=u[:, tp:tp + 256],
                                                                 scalar=cw[:, mb, tp:tp + 1], in1=c[:, :],
                                                                 op0=ALU.mult, op1=ALU.add), [u, cw, c], [c])
                if mb >= 24:
                    k.act(t1[i][:, :], cg[i][:, :], AF.Square, [cg[i]], [t1[i]])
                    k.op('dve', lambda e: e.tensor_scalar(out=t1[i][:, :], in0=t1[i][:, :], scalar1=0.044715 * GC,
                                                          scalar2=GC, op0=ALU.mult, op1=ALU.add), [t1[i]], [t1[i]])
                    k.op('pool', lambda e: e.tensor_tensor(out=t1[i][:, :], in0=t1[i][:, :], in1=cg[i][:, :], op=ALU.mult),
                         [t1[i], cg[i]], [t1[i]])
                    k.act(t2[i][:, :], t1[i][:, :], AF.Tanh, [t1[i]], [t2[i]])
            k.op('pool', lambda e: e.tensor_tensor(out=t3[i][:, :], in0=cg[i][:, :], in1=cv[i][:, :], op=ALU.mult),
                 [cg[i], cv[i]], [t3[i]])
            k.op('dve', lambda e: e.scalar_tensor_tensor(out=aT[i][:, :], in0=t2[i][:, :], scalar=1.0, in1=t3[i][:, :],
                                                         op0=ALU.add, op1=ALU.mult), [t2[i], t3[i]], [aT[i]])

        def stage_c(it):
            g, m = divmod(it, 24)
            i = it % 2
            w = wd[it % NB]
            for tt in range(2):
                for h in range(2):
                    k.mm(psD[tt][:, h * 512:(h + 1) * 512], aT[i][:, tt * 128:(tt + 1) * 128],
                         w[:, h * 512:(h + 1) * 512], start=(m == 0), stop=(m == 23),
                         reads=[aT[i], w], writes=[psD[tt]], inc=True)
            if m != 23:
                return
            for tt in range(2):
                t = 2 * g + tt
                xi = xmt[tt]
                k.dma('sp', xi[:, :], xmid_d[t][:, :], reads=[xmid_d[t]], writes=[xi])
                o = xo[tt]
                k.op('dve', lambda e: e.tensor_tensor(out=o[:, :], in0=psD[tt][:, :], in1=xi[:, :], op=ALU.add),
                     [psD[tt], xi], [o])
                a = st3[tt]
                k.act(jk[tt][:, :], o[:, :], AF.Square, [o], [jk[tt], a], accum_out=a[:, 0:1])
                k.op('dve', lambda e: e.tensor_scalar(out=a[:, 1:2], in0=a[:, 0:1], scalar1=1.0 / 1024, scalar2=EPS,
                                                      op0=ALU.mult, op1=ALU.add), [a], [a])
                k.act(a[:, 1:2], a[:, 1:2], AF.Sqrt, [a], [a])
                k.op('dve', lambda e: e.reciprocal(out=a[:, 2:3], in_=a[:, 1:2]), [a], [a])
                k.op('dve', lambda e: e.scalar_tensor_tensor(out=o[:, :], in0=o[:, :], scalar=a[:, 2:3], in1=fgbc[:, :],
                                                             op0=ALU.mult, op1=ALU.mult), [o, a, fgbc], [o])
                ob = Buf(out2[t * 128:(t + 1) * 128, :])
                k.dma('pool', ob[:, :], o[:, :], reads=[o], writes=[ob])
                out_bufs.append(ob)

        NIT = 8 * 24
        stage_a(0)
        for it in range(NIT):
            if it + 1 < NIT:
                stage_a(it + 1)
            stage_b(it)
            stage_c(it)
        return out_bufs


def p2_decl(nc, io1=None):
    io = {}
    if io1 is None:
        io['mixl'] = dram(nc, "mixl", [2050, 1024], BF16)
        io['mixs'] = dram(nc, "mixs", [2050, 1024], BF16)
    io['x2'] = dram(nc, "x2", [2050, 1024], F32)
    io['ident'] = dram(nc, "ident", [128, 128], F32) if io1 is None else io1['ident']
    io['hmask'] = dram(nc, "hmask", [128, 2], F32)
    io['c_b'] = dram(nc, "c_b", [1024], F32) if io1 is None else io1['c_b']
    io['ada_w'] = dram(nc, "ada_w", [1024, 6144], F32) if io1 is None else io1['ada_w1']
    io['ada_b'] = dram(nc, "ada_b", [6144], F32) if io1 is None else io1['ada_b1']
    io['final_g'] = dram(nc, "final_g", [1024], F32)
    io['norm2_g'] = dram(nc, "norm2_g", [1024], F32)
    io['ssd_norm_g'] = dram(nc, "ssd_norm_g", [1024], F32)
    io['w_out'] = dram(nc, "w_out", [2048, 1024], F32)
    io['w_up'] = dram(nc, "w_up", [1024, 6144], F32)
    io['w_down'] = dram(nc, "w_down", [3072, 1024], F32)
    io['ffn_cw'] = dram(nc, "ffn_cw", [3, 6144], F32)
    io['ffn_cb'] = dram(nc, "ffn_cb", [6144], F32)
    io['out2'] = dram(nc, "out2", [2048, 1024], F32, kind="ExternalOutput")
    io['xmid'] = dram(nc, "xmid", [2048, 1024], F32, kind=DBG_KIND)
    io['wdn_bf'] = dram(nc, "wdn_bf", [3072, 1024], BF16, kind="Internal")
    if io1 is not None:
        io['sel'] = dram(nc, "sel", [128, 4], F32)
    return io


def make_phase2_nc():
    nc = bass.Bass("TRN2", target_bir_lowering=False)
    io = p2_decl(nc)
    with ExitStack() as es:
        es.enter_context(nc.allow_non_contiguous_dma("small strided parameter loads"))
        k = KB(nc, es)
        outs = build_phase2(nc, k, es, io)
        k.finish(outs)
    return nc


def p2_inputs(inp, mixl_full, mixs_full):
    maps = []
    ident = np.eye(128, dtype=np.float32)
    for b in range(2):
        for kq in range(4):
            t0 = kq * NT2
            rows = list(range(t0, t0 + NT2)) + [max(t0 - 1, 0), min(t0 + NT2, 8191)]
            hm = np.ones((128, 2), np.float32)
            if kq == 0:
                hm[:, 0] = 0.0
            if kq == 3:
                hm[:, 1] = 0.0
            maps.append({
                'mixl': np.ascontiguousarray(mixl_full[b][rows]),
                'mixs': np.ascontiguousarray(mixs_full[b][rows]),
                'x2': np.ascontiguousarray(inp['x'][b][rows]),
                'ident': ident, 'hmask': hm,
                'c_b': np.ascontiguousarray(inp['c'][b]),
                'ada_w': np.ascontiguousarray(inp['ada_w'][0]),
                'ada_b': np.ascontiguousarray(inp['ada_b'][0]),
                'final_g': np.ascontiguousarray(inp['final_norm_g']),
                'norm2_g': np.ascontiguousarray(inp['norm2_g'][0]),
                'ssd_norm_g': np.ascontiguousarray(inp['ssd_norm_g'][0]),
                'w_out': np.ascontiguousarray(inp['w_out'][0]),
                'w_up': np.ascontiguousarray(inp['ffn_w_up'][0]),
                'w_down': np.ascontiguousarray(inp['ffn_w_down'][0]),
                'ffn_cw': np.ascontiguousarray(inp['ffn_conv_w'][0]),
                'ffn_cb': np.ascontiguousarray(inp['ffn_conv_b'][0]),
            })
    return maps


TT = 256 + 8192
TILES = [('c', 0, 256)] + [('l', i * 512, 512) for i in range(16)]


def round_robin(gens):
    gens = list(gens)
    while gens:
        for g_ in list(gens):
            try:
                next(g_)
            except StopIteration:
                gens.remove(g_)


def ring(k, s, shape, dt, n, name):
    return [k.sb(shape, dt, s, f"{name}{i}") for i in range(n)]


def ap3(base, mid):
    d = [list(x) for x in base.ap]
    return bass.AP(base.tensor, base.offset, [d[0], [0, mid], d[1]])


def p1_decl(nc, G=1, fused=False):
    io = {}
    io['x1'] = dram(nc, "x1", [8192, 1024], F32)
    io['ctx1'] = dram(nc, "ctx1", [256, 1024], F32)
    io['c_b'] = dram(nc, "c_b", [1024], F32)
    io['c_ctx'] = dram(nc, "c_ctx", [1024], F32)
    if fused:
        io['ada_w1'] = dram(nc, "ada_w", [1024, 6144], F32)
        io['ada_b1'] = dram(nc, "ada_b", [6144], F32)
    else:
        io['ada_w1'] = dram(nc, "ada_w1", [1024, 2048], F32)
        io['ada_b1'] = dram(nc, "ada_b1", [2048], F32)
    io['norm1_g'] = dram(nc, "norm1_g", [1024], F32)
    io['w_in1'] = dram(nc, "w_in1", [G, 1024, 1288], F32)
    io['l_cw'] = dram(nc, "l_cw", [G, 4, 256], F32)
    io['l_cb'] = dram(nc, "l_cb", [G, 256], F32)
    io['l_wbd'] = dram(nc, "l_wbd", [G, 8, 128, 128], F32)
    io['l_ba'] = dram(nc, "l_ba", [G, 2, 256], F32)
    io['l_bx'] = dram(nc, "l_bx", [G, 2, 256], F32)
    io['l_lam'] = dram(nc, "l_lam", [G, 2, 256], F32)
    io['s_cw'] = dram(nc, "s_cw", [G, 4, 512], F32)
    io['s_cb'] = dram(nc, "s_cb", [G, 512], F32)
    io['s_alog'] = dram(nc, "s_alog", [G, 8], F32)
    io['s_dtb'] = dram(nc, "s_dtb", [G, 8], F32)
    io['s_d'] = dram(nc, "s_d", [G, 4], F32)
    io['ident'] = dram(nc, "ident", [128, 128], F32)
    io['triu'] = dram(nc, "triu", [128, 128], F32)
    io['tril'] = dram(nc, "tril", [128, 128], F32)
    io['ones'] = dram(nc, "ones", [128, 128], F32)
    io['identzf'] = dram(nc, "identzf", [128, 128], F32)
    io['identzb'] = dram(nc, "identzb", [128, 128], F32)
    okind = "Internal" if fused else "ExternalOutput"
    io['ol'] = dram(nc, "ol", [G, 8192, 256], BF16, kind=okind)
    io['os'] = dram(nc, "os", [G, 8192, 256], BF16, kind=okind)
    io['ht_d'] = dram(nc, "ht_d", [len(TILES), 128, 8, 512], BF16, kind="Internal")
    io['lug'] = dram(nc, "lug", [G, 4, 128, TT], F32, kind=DBG_KIND)
    io['zx'] = dram(nc, "zx", [G, TT, 776], F32, kind=DBG_KIND)
    io['xc_d'] = dram(nc, "xc_d", [G, 2, 128, TT], F32, kind="Internal")
    io['hf_d'] = dram(nc, "hf_d", [G, 2, 128, 8192], F32, kind="Internal")
    io['sb_d'] = dram(nc, "sb_d", [G, 66, 128, 256], F32, kind="Internal")
    io['yp_d'] = dram(nc, "yp_d", [G, 64, 128, 256], F32, kind="Internal")
    io['ct_d'] = dram(nc, "ct_d", [G, 64, 128, 128], BF16, kind="Internal")
    io['ecs_d'] = dram(nc, "ecs_d", [G, 66, 128, 16], F32, kind="Internal")
    return io


def io_group(io, q):
    v = dict(io)
    for n in ('w_in1', 'l_cw', 'l_cb', 'l_wbd', 'l_ba', 'l_bx', 'l_lam', 's_cw', 's_cb', 's_alog', 's_dtb', 's_d',
              'ol', 'os', 'lug', 'zx', 'xc_d', 'hf_d', 'sb_d', 'yp_d', 'ct_d', 'ecs_d'):
        v[n] = io[n][q]
    return v


def p1_shared(k, es, io):
    W = {}
    identf = k.sb([128, 128], F32, es, "identf")
    identb = k.sb([128, 128], BF16, es, "identb")
    triu = k.sb([128, 128], F32, es, "triu")
    tril = k.sb([128, 128], F32, es, "tril")
    ones = k.sb([128, 128], F32, es, "ones")
    identzf = k.sb([128, 128], F32, es, "identzf")
    identzb = k.sb([128, 128], F32, es, "identzb")
    ident4 = k.sb([128, 512], F32, es, "ident4")
    for (b, n) in ((identf, 'ident'), (triu, 'triu'), (tril, 'tril'), (ones, 'ones'), (identzf, 'identzf'),
                   (identzb, 'identzb')):
        k.dma('sp', b[:, :], io[n], writes=[b])
    for h in range(4):
        k.dma('sp', ident4[:, h * 128:(h + 1) * 128], io['ident'], writes=[ident4])
    W.update(identzf=identzf, identzb=identzb, ident4=ident4)
    k.op('dve', lambda e: e.tensor_copy(out=identb[:, :], in_=identf[:, :]), [identf], [identb])
    W.update(identf=identf, identb=identb, triu=triu, tril=tril, ones=ones)

    sc = silu_vec(k, es, [io['c_b'], io['c_ctx']], 2, "c1")
    adab_p = k.sb([128, 16], F32, es, "adab_p")
    k.dma('sp', adab_p[:, :], io['ada_b1'][0:2048].rearrange("(m p) -> p m", p=128), writes=[adab_p])
    shsc = k.sb([128, 16, 2], F32, es, "shsc")
    matvec_part(k, es, io['ada_w1'][:, 0:2048], 2048, sc, 2, shsc, adab_p, "shsc")
    n1g = k.sb([128, 8], F32, es, "n1g")
    k.dma('sp', n1g[:, :], io['norm1_g'].rearrange("(m p) -> p m", p=128), writes=[n1g])
    gs = k.sb([128, 8, 2], F32, es, "gs1")
    sh = k.sb([128, 8, 2], F32, es, "sh1")
    for v in range(2):
        k.op('dve', lambda e: e.scalar_tensor_tensor(out=gs[:, :, v], in0=shsc[:, 8:16, v], scalar=1.0, in1=n1g[:, :],
                                                     op0=ALU.add, op1=ALU.mult), [shsc, n1g], [gs])
    k.op('dve', lambda e: e.tensor_copy(out=sh[:, :, :], in_=shsc[:, 0:8, :]), [shsc], [sh])
    W.update(sh=sh, gs=gs)
    return W


def p1_group(k, es, io, W0):
    W = dict(W0)
    identf, sh, gs = W['identf'], W['sh'], W['gs']
    bl = k.sb([128, 4, 2], F32, es, "bl")
    matvec_part(k, es, io['w_in1'][:, 0:512], 512, sh, 2, bl, None, "bl")
    bz = []
    for v in range(2):
        rep = make_rep(k, es, identf, sh, v, f"shrep{v}")
        o = k.sb([128, 776], F32, es, f"bz{v}")
        matvec_bcast(k, es, io['w_in1'][:, 512:1288], 776, (rep, lambda j, rep=rep: rep[:, j, :]), o, None, f"bz{v}")
        bz.append(o)
    W.update(bl=bl, bz=bz)
    Wl = k.sb([128, 8, 1288], BF16, es, "Wl")
    Wc = k.sb([128, 8, 1288], BF16, es, "Wc")
    with k.scope() as s:
        stg = ring(k, s, [128, 1288], F32, 2, "wstg")
        for j in range(8):
            st = stg[j % 2]
            k.dma('sp', st[:, :], io['w_in1'][j * 128:(j + 1) * 128, :], writes=[st])
            k.op('dve', lambda e: e.tensor_scalar(out=Wl[:, j, :], in0=st[:, :], scalar1=gs[:, j, 0:1], scalar2=None,
                                                  op0=ALU.mult), [st, gs], [Wl])
            k.act(Wc[:, j, :], st[:, :], AF.Copy, [st, gs], [Wc], scale=gs[:, j, 1:2])
    W.update(Wl=Wl, Wc=Wc)
    lcw = k.sb([128, 2, 4], F32, es, "lcw")
    lcb = k.sb([128, 2], F32, es, "lcb")
    for tp in range(4):
        k.dma('sp', lcw[:, :, tp], io['l_cw'][tp].rearrange("(m p) -> p m", p=128), writes=[lcw])
    k.dma('sp', lcb[:, :], io['l_cb'].rearrange("(m p) -> p m", p=128), writes=[lcb])
    wbd = k.sb([128, 8, 128], BF16, es, "wbd")
    with k.scope() as s:
        wf = k.sb([128, 8, 128], F32, s, "wbdf")
        k.dma('sp', wf[:, :, :], io['l_wbd'].rearrange("i p m -> p i m"), writes=[wf])
        k.op('dve', lambda e: e.tensor_copy(out=wbd[:, :, :], in_=wf[:, :, :]), [wf], [wbd])
    hba = k.sb([128, 2, 2], F32, es, "hba")
    hbx = k.sb([128, 2, 2], F32, es, "hbx")
    lam = k.sb([128, 2, 2], F32, es, "lam")
    for d in range(2):
        k.dma('sp', hba[:, d, :], io['l_ba'][d].rearrange("(m p) -> p m", p=128), writes=[hba])
        k.dma('sp', hbx[:, d, :], io['l_bx'][d].rearrange("(m p) -> p m", p=128), writes=[hbx])
        k.dma('sp', lam[:, d, :], io['l_lam'][d].rearrange("(m p) -> p m", p=128), writes=[lam])
    for b in (hba, hbx):
        k.op('dve', lambda e: e.tensor_scalar(out=b[:, :, :], in0=b[:, :, :], scalar1=0.5, scalar2=None, op0=ALU.mult),
             [b], [b])
    cr = k.sb([128, 2, 2], F32, es, "cr")
    hcr = k.sb([128, 2, 2], F32, es, "hcr")
    k.act(cr[:, :, :], lam[:, :, :], AF.Exp, [lam], [cr], scale=-1.0)
    k.act(cr[:, :, :], cr[:, :, :], AF.Ln, [cr], [cr], bias=1.0)
    k.op('dve', lambda e: e.tensor_scalar(out=hcr[:, :, :], in0=cr[:, :, :], scalar1=-4.0, scalar2=None, op0=ALU.mult),
         [cr], [hcr])
    k.op('dve', lambda e: e.tensor_scalar(out=cr[:, :, :], in0=cr[:, :, :], scalar1=-8.0, scalar2=None, op0=ALU.mult),
         [cr], [cr])
    W.update(lcw=lcw, lcb=lcb, wbd=wbd, hba=hba, hbx=hbx, cr=cr, hcr=hcr)
    scw = k.sb([128, 4, 512], F32, es, "scw")
    scb = k.sb([128, 512], F32, es, "scb")
    for tp in range(4):
        k.dma('sp', scw[:, tp, :], bcast_rows(io['s_cw'][tp], 512), writes=[scw])
    k.dma('sp', scb[:, :], bcast_rows(io['s_cb'], 512), writes=[scb])
    negA = k.sb([128, 8], F32, es, "negA")
    dtb = k.sb([128, 8], F32, es, "dtb")
    hD = k.sb([128, 4], F32, es, "hD")
    k.dma('sp', negA[:, :], bcast_rows(io['s_alog'], 8), writes=[negA])
    k.dma('sp', dtb[:, :], bcast_rows(io['s_dtb'], 8), writes=[dtb])
    k.dma('sp', hD[:, :], bcast_rows(io['s_d'], 4), writes=[hD])
    k.act(negA[:, :], negA[:, :], AF.Exp, [negA], [negA])
    k.op('dve', lambda e: e.tensor_scalar(out=negA[:, :], in0=negA[:, :], scalar1=-1.0, scalar2=None, op0=ALU.mult),
         [negA], [negA])
    k.op('dve', lambda e: e.tensor_scalar(out=hD[:, :], in0=hD[:, :], scalar1=0.5, scalar2=None, op0=ALU.mult),
         [hD], [hD])
    W.update(scw=scw, scb=scb, negA=negA, dtb=dtb, hD=hD)
    return W


def p1_prework(k, es, io, W0):
    identb = W0['identb']
    htd = [Buf(io['ht_d'][ti]) for ti in range(len(TILES))]
    with k.scope() as s:
        xt = ring(k, s, [128, 1024], F32, 4, "xt")
        xn = ring(k, s, [128, 1024], BF16, 3, "xn")
        junk = ring(k, s, [128, 1024], BF16, 2, "junk")
        st = ring(k, s, [128, 4], F32, 4, "st")
        hT = ring(k, s, [128, 8, 512], BF16, 3, "hTp")
        psH = [k.ps([128, 8, 128], BF16, s, f"psH{i}") for i in range(4)]
        nx = 0
        for ti, (kd, t0, n) in enumerate(TILES):
            src = io['ctx1'] if kd == 'c' else io['x1']
            h = hT[ti % 3]
            for sub in range(n // 128):
                x = xt[nx % 4]
                xb = xn[nx % 3]
                jk = junk[nx % 2]
                a = st[nx % 4]
                ph = psH[nx % 4]
                nx += 1
                r0 = t0 + sub * 128
                k.dma('sp', x[:, :], src[r0:r0 + 128, :], writes=[x])
                k.act(jk[:, :], x[:, :], AF.Square, [x], [jk, a], accum_out=a[:, 0:1])
                k.op('dve', lambda e: e.tensor_scalar(out=a[:, 1:2], in0=a[:, 0:1], scalar1=1.0 / 1024, scalar2=EPS,
                                                      op0=ALU.mult, op1=ALU.add), [a], [a])
                k.act(a[:, 1:2], a[:, 1:2], AF.Sqrt, [a], [a])
                k.op('dve', lambda e: e.reciprocal(out=a[:, 2:3], in_=a[:, 1:2]), [a], [a])
                k.op('dve', lambda e: e.tensor_scalar(out=xb[:, :], in0=x[:, :], scalar1=a[:, 2:3], scalar2=None,
                                                      op0=ALU.mult), [x, a], [xb])
                for j in range(8):
                    k.tr(ph[:, j, :], xb[:, j * 128:(j + 1) * 128], identb[:, :], [xb, identb], [ph], inc=(j == 7))
                k.op('dve' if sub % 2 else 'act',
                     (lambda e: e.tensor_copy(out=h[:, :, sub * 128:(sub + 1) * 128], in_=ph[:, :, :])) if sub % 2 else
                     (lambda e: e.activation(out=h[:, :, sub * 128:(sub + 1) * 128], in_=ph[:, :, :], func=AF.Copy)),
                     [ph], [h])
            k.dma('pool', htd[ti][:, :, 0:n], h[:, :, 0:n], reads=[h], writes=[htd[ti]])
    return htd


def p1_inproj(k, es, io, W, htd):
    lug = [[Buf(io['lug'][mb, :, (0 if ti == 0 else 256 + t0):(0 if ti == 0 else 256 + t0) + n])
            for ti, (kd, t0, n) in enumerate(TILES)] for mb in range(4)]
    zxb = {}
    with k.scope() as s:
        hT = ring(k, s, [128, 8, 512], BF16, 3, "hT")
        lo = ring(k, s, [128, 512], F32, 3, "lo")
        g1 = ring(k, s, [128, 512], F32, 2, "g1")
        g2 = ring(k, s, [128, 512], F32, 2, "g2")
        zo = ring(k, s, [128, 776], F32, 3, "zo")
        psL = [k.ps([128, 512], F32, s, f"psL{i}") for i in range(4)]
        psZ = [k.ps([128, 1024], F32, s, f"psZ{i}") for i in range(2)]
        nl = 0
        nz = 0

        def load(ti):
            kd, t0, n = TILES[ti]
            h = hT[ti % 3]
            k.dma('sp', h[:, :, 0:n], htd[ti][:, :, 0:n], reads=[htd[ti]], writes=[h])

        load(0)
        load(1)
        for ti, (kd, t0, n) in enumerate(TILES):
            if ti + 2 < len(TILES):
                load(ti + 2)
            v = 1 if kd == 'c' else 0
            Wt = W['Wc'] if kd == 'c' else W['Wl']
            base = 0 if kd == 'c' else 256
            h = hT[ti % 3]
            nsub = n // 128
            for mb in range(4):
                ps = psL[nl % 4]
                o = lo[nl % 3]
                ga = g1[nl % 2]
                gb = g2[nl % 2]
                nl += 1
                for j in range(8):
                    k.mm(ps[:, 0:n], Wt[:, j, mb * 128:(mb + 1) * 128], h[:, j, 0:n], start=(j == 0), stop=(j == 7),
                         reads=[Wt, h], writes=[ps])
                k.act(o[:, 0:n], ps[:, 0:n], AF.Identity, [ps, W['bl']], [o], bias=W['bl'][:, mb, v:v + 1])
                if mb >= 2:
                    k.act(ga[:, 0:n], o[:, 0:n], AF.Square, [o], [ga])
                    k.op('dve', lambda e: e.tensor_scalar(out=ga[:, 0:n], in0=ga[:, 0:n], scalar1=0.044715 * GC, scalar2=GC,
                                                          op0=ALU.mult, op1=ALU.add), [ga], [ga])
                    k.op('pool', lambda e: e.tensor_tensor(out=ga[:, 0:n], in0=ga[:, 0:n], in1=o[:, 0:n], op=ALU.mult),
                         [ga, o], [ga])
                    k.act(gb[:, 0:n], ga[:, 0:n], AF.Tanh, [ga], [gb])
                    k.act(ga[:, 0:n], o[:, 0:n], AF.Copy, [o], [ga], scale=0.25)
                    k.op('dve', lambda e: e.scalar_tensor_tensor(out=o[:, 0:n], in0=gb[:, 0:n], scalar=1.0, in1=ga[:, 0:n],
                                                                 op0=ALU.add, op1=ALU.mult), [gb, ga], [o])
                k.dma('pool', lug[mb][ti][:, :], o[:, 0:n], reads=[o], writes=[lug[mb][ti]])
            for sub in range(nsub):
                pz = psZ[nz % 2]
                z = zo[nz % 3]
                nz += 1
                for (c0, cw_) in ((0, 512), (512, 264)):
                    for j in range(8):
                        k.mm(pz[:, c0:c0 + cw_], h[:, j, sub * 128:(sub + 1) * 128], Wt[:, j, 512 + c0:512 + c0 + cw_],
                             start=(j == 0), stop=(j == 7), reads=[h, Wt], writes=[pz])
                k.op('dve', lambda e: e.tensor_tensor(out=z[:, :], in0=pz[:, 0:776], in1=W['bz'][v][:, :], op=ALU.add),
                     [pz, W['bz'][v]], [z])
                r0 = base + t0 + sub * 128
                zb = Buf(io['zx'][r0:r0 + 128, :])
                zxb[r0 // 128] = zb
                k.dma('pool', zb[:, :], z[:, :], reads=[z], writes=[zb])
    return lug, zxb


def lru_gates(k, W, d, blk, n, xcb, psR, psI, thr, thi, a, m, u, xc):
    wbd = W['wbd']
    k.mm(psR[:, 0:n], wbd[:, d * 4 + 0 + blk, :], xcb[:, 0:n], True, True, [wbd, xcb], [psR])
    k.mm(psI[:, 0:n], wbd[:, d * 4 + 2 + blk, :], xcb[:, 0:n], True, True, [wbd, xcb], [psI])
    yield
    k.act(thr[:, 0:n], psR[:, 0:n], AF.Tanh, [psR, W['hba']], [thr], scale=0.5, bias=W['hba'][:, d, blk:blk + 1])
    k.act(thi[:, 0:n], psI[:, 0:n], AF.Tanh, [psI, W['hbx']], [thi], scale=0.5, bias=W['hbx'][:, d, blk:blk + 1])
    yield
    k.act(a[:, 0:n], thr[:, 0:n], AF.Exp, [thr, W['hcr']], [a], scale=W['hcr'][:, d, blk:blk + 1],
          bias=W['hcr'][:, d, blk:blk + 1])
    k.act(m[:, 0:n], thr[:, 0:n], AF.Exp, [thr, W['cr']], [m], scale=W['cr'][:, d, blk:blk + 1],
          bias=W['cr'][:, d, blk:blk + 1])
    yield
    k.act(m[:, 0:n], m[:, 0:n], AF.Sqrt, [m], [m], scale=-1.0, bias=1.0)
    yield
    k.op('dve', lambda e: e.scalar_tensor_tensor(out=u[:, 0:n], in0=thi[:, 0:n], scalar=1.0, in1=xc[:, 0:n],
                                                 op0=ALU.add, op1=ALU.mult), [thi, xc], [u])
    yield
    k.op('dve', lambda e: e.tensor_tensor(out=u[:, 0:n], in0=u[:, 0:n], in1=m[:, 0:n], op=ALU.mult), [u, m], [u])


def p1_lru(k, es, io, W, lug):
    xcd = [[Buf(io['xc_d'][blk, :, (0 if ti == 0 else 256 + t0):(0 if ti == 0 else 256 + t0) + n])
            for ti, (kd, t0, n) in enumerate(TILES)] for blk in range(2)]
    hfd = [[None] + [Buf(io['hf_d'][blk, :, t0:t0 + n]) for (kd, t0, n) in TILES[1:]] for blk in range(2)]
    outs = []
    with k.scope() as s:
        lw = ring(k, s, [128, 515], F32, 6, "lw")
        xc = ring(k, s, [128, 512], F32, 4, "xc")
        xcb = ring(k, s, [128, 512], BF16, 4, "xcb")
        thr = ring(k, s, [128, 512], F32, 2, "thr")
        thi = ring(k, s, [128, 512], F32, 2, "thi")
        aa = ring(k, s, [128, 512], F32, 2, "aa")
        mm_ = ring(k, s, [128, 512], F32, 2, "mm")
        uu = ring(k, s, [128, 512], F32, 2, "uu")
        hf = [ring(k, s, [128, 512], F32, 2, f"hf{blk}") for blk in range(2)]
        psR = [k.ps([128, 512], F32, s, f"psR{i}") for i in range(2)]
        psI = [k.ps([128, 512], F32, s, f"psI{i}") for i in range(2)]
        prev = [None, None]

        def fwd_load(ti, blk, it):
            kd, t0, n = TILES[ti]
            L = 256 if kd == 'c' else 8192
            w = lw[it % 6]
            lo_, hi_ = t0 - 2, t0 + n + 1
            clo, chi = max(lo_, 0), min(hi_, L)
            if clo > lo_ or chi < hi_:
                k.op('pool', lambda e: e.memset(w[:, :], 0.0), [], [w])
            base = 0 if kd == 'c' else 256
            srcs = [lug[blk][tj] for tj, (kd2, t02, n2) in enumerate(TILES)
                    if kd2 == kd and t02 < chi and t02 + n2 > clo]
            k.dma('sp', w[:, clo - lo_:chi - lo_], io['lug'][blk, :, base + clo:base + chi], reads=srcs, writes=[w])

        def fwd_early(ti, blk, it):
            kd, t0, n = TILES[ti]
            w = lw[it % 6]
            c = xc[it % 4]
            xb_ = xcb[it % 4]
            yield
            lcw, lcb = W['lcw'], W['lcb']
            k.op('dve', lambda e: e.tensor_scalar(out=c[:, 0:n], in0=w[:, 0:n], scalar1=lcw[:, blk, 0:1],
                                                  scalar2=lcb[:, blk:blk + 1], op0=ALU.mult, op1=ALU.add),
                 [w, lcw, lcb], [c])
            yield
            for tp in (1, 2, 3):
                k.op('dve', lambda e: e.scalar_tensor_tensor(out=c[:, 0:n], in0=w[:, tp:tp + n],
                                                             scalar=lcw[:, blk, tp:tp + 1], in1=c[:, 0:n],
                                                             op0=ALU.mult, op1=ALU.add), [w, lcw, c], [c])
                yield
            k.act(xb_[:, 0:n], c[:, 0:n], AF.Copy, [c], [xb_])
            k.dma('pool', xcd[blk][ti][:, :], c[:, 0:n], reads=[c], writes=[xcd[blk][ti]])
            yield

        def fwd_late(ti, blk, it):
            kd, t0, n = TILES[ti]
            c = xc[it % 4]
            xb_ = xcb[it % 4]
            i2 = it % 2
            yield from lru_gates(k, W, 0, blk, n, xb_, psR[i2], psI[i2], thr[i2], thi[i2], aa[i2], mm_[i2], uu[i2], c)
            h = hf[blk][ti % 2]
            if prev[blk] is None:
                k.op('dve', lambda e: e.tensor_tensor_scan(out=h[:, 0:n], data0=aa[i2][:, 0:n], data1=uu[i2][:, 0:n],
                                                           initial=0.0, op0=ALU.mult, op1=ALU.add),
                     [aa[i2], uu[i2]], [h])
            else:
                pb, pn = prev[blk]
                k.op('dve', lambda e: e.tensor_tensor_scan(out=h[:, 0:n], data0=aa[i2][:, 0:n], data1=uu[i2][:, 0:n],
                                                           initial=pb[:, pn - 1:pn], op0=ALU.mult, op1=ALU.add),
                     [aa[i2], uu[i2], pb], [h])
            prev[blk] = (h, n)
            if kd == 'l':
                k.dma('pool', hfd[blk][ti][:, :], h[:, 0:n], reads=[h], writes=[hfd[blk][ti]])
            yield

        NTL = len(TILES)
        for tj in range(min(2, NTL)):
            for blk in range(2):
                fwd_load(tj, blk, 2 * tj + blk)
        round_robin([fwd_early(0, blk, blk) for blk in range(2)])
        for ti in range(NTL):
            if ti + 2 < NTL:
                for blk in range(2):
                    fwd_load(ti + 2, blk, 2 * (ti + 2) + blk)
            gens = [fwd_late(ti, blk, 2 * ti + blk) for blk in range(2)]
            if ti + 1 < NTL:
                gens += [fwd_early(ti + 1, blk, 2 * (ti + 1) + blk) for blk in range(2)]
            round_robin(gens)
    with k.scope() as s:
        xc = ring(k, s, [128, 512], F32, 4, "bxc")
        xcb = ring(k, s, [128, 512], BF16, 4, "bxcb")
        thr = ring(k, s, [128, 512], F32, 2, "bthr")
        thi = ring(k, s, [128, 512], F32, 2, "bthi")
        aa = ring(k, s, [128, 512], F32, 2, "baa")
        mm_ = ring(k, s, [128, 512], F32, 2, "bmm")
        uu = ring(k, s, [128, 512], F32, 2, "buu")
        hb = [ring(k, s, [128, 512], F32, 2, f"hb{blk}") for blk in range(2)]
        hfl = ring(k, s, [128, 512], F32, 4, "hfl")
        gl = ring(k, s, [128, 512], F32, 4, "gl")
        ob = ring(k, s, [128, 512], BF16, 2, "ob")
        otm = ring(k, s, [128, 4, 256], BF16, 2, "otm")
        psR = [k.ps([128, 512], F32, s, f"bpsR{i}") for i in range(2)]
        psI = [k.ps([128, 512], F32, s, f"bpsI{i}") for i in range(2)]
        psT = [k.ps([128, 4, 256], BF16, s, f"bpsT{i}") for i in range(2)]
        order = [0] + list(range(16, 0, -1))
        prev = [None, None]

        def bwd_early(oi, ti, blk, it):
            kd, t0, n = TILES[ti]
            c = xc[it % 4]
            hl = hfl[it % 4]
            g = gl[it % 4]
            k.dma('sp', c[:, 0:n], xcd[blk][ti][:, :], reads=[xcd[blk][ti]], writes=[c])
            if kd == 'l':
                k.dma('sp', hl[:, 0:n], hfd[blk][ti][:, :], reads=[hfd[blk][ti]], writes=[hl])
                k.dma('sp', g[:, 0:n], lug[2 + blk][ti][:, :], reads=[lug[2 + blk][ti]], writes=[g])
            yield
            yield
            yield
            k.act(xcb[it % 4][:, 0:n], c[:, 0:n], AF.Copy, [c], [xcb[it % 4]])
            yield

        def bwd_late(oi, ti, blk, it):
            kd, t0, n = TILES[ti]
            pt = psT[oi % 2]
            c = xc[it % 4]
            hl = hfl[it % 4]
            g = gl[it % 4]
            xb_ = xcb[it % 4]
            i2 = it % 2
            yield from lru_gates(k, W, 1, blk, n, xb_, psR[i2], psI[i2], thr[i2], thi[i2], aa[i2], mm_[i2], uu[i2], c)
            h = hb[blk][oi % 2]
            if prev[blk] is None:
                k.op('dve', lambda e: e.tensor_tensor_scan(out=h[:, 0:n][:, ::-1],
                                                           data0=aa[i2][:, 0:n][:, ::-1], data1=uu[i2][:, 0:n][:, ::-1],
                                                           initial=0.0, op0=ALU.mult, op1=ALU.add),
                     [aa[i2], uu[i2]], [h])
            else:
                pb = prev[blk]
                k.op('dve', lambda e: e.tensor_tensor_scan(out=h[:, 0:n][:, ::-1], data0=aa[i2][:, 0:n][:, ::-1],
                                                           data1=uu[i2][:, 0:n][:, ::-1], initial=pb[:, 0:1],
                                                           op0=ALU.mult, op1=ALU.add), [aa[i2], uu[i2], pb], [h])
            prev[blk] = h
            yield
            if kd == 'l':
                k.op('pool', lambda e: e.tensor_tensor(out=hl[:, 0:n], in0=hl[:, 0:n], in1=h[:, 0:n], op=ALU.add),
                     [hl, h], [hl])
                yield
                k.op('pool', lambda e: e.tensor_tensor(out=ob[i2][:, 0:n], in0=hl[:, 0:n], in1=g[:, 0:n], op=ALU.mult),
                     [hl, g], [ob[i2]])
                yield
                for sub in range(4):
                    k.tr(pt[:, sub, blk * 128:(blk + 1) * 128], ob[i2][:, sub * 128:(sub + 1) * 128], W['identb'][:, :],
                         [ob[i2], W['identb']], [pt], inc=(sub == 3))

        round_robin([bwd_early(0, order[0], blk, blk) for blk in range(2)])
        for oi, ti in enumerate(order):
            kd, t0, n = TILES[ti]
            pt = psT[oi % 2]
            gens = [bwd_late(oi, ti, blk, 2 * oi + blk) for blk in range(2)]
            if oi + 1 < len(order):
                gens += [bwd_early(oi + 1, order[oi + 1], blk, 2 * (oi + 1) + blk) for blk in range(2)]
            round_robin(gens)
            if kd == 'l':
                o = otm[oi % 2]
                k.act(o[:, :, :], pt[:, :, :], AF.Copy, [pt], [o])
                dst = Buf(io['ol'][t0:t0 + n, :])
                k.dma('pool', dst.t.rearrange("(s p) c -> p s c", p=128), o[:, :, :], reads=[o], writes=[dst])
                outs.append(dst)
    return outs


def zrows(io, ci, r_lo, r_hi, c0, c1):
    zx = io['zx']
    if ci < 2:
        b0 = ci * 128 + r_lo
        return zx[b0:b0 + (r_hi - r_lo), c0:c1]
    c = ci - 2
    start = 256 + r_lo * 64 + c
    return bass.AP(zx.tensor, zx.offset + start * 776 + c0, [[64 * 776, r_hi - r_lo], [1, c1 - c0]])


def p1_ssd(k, es, io, W, zxb):
    identf, identb, triu, tril, ones = W['identf'], W['identb'], W['triu'], W['tril'], W['ones']
    outs = []
    join = k.sb([128, 1], F32, es, "zjoin")
    k.op('pool', lambda e: e.memset(join[:, :], 0.0), list(zxb.values()), [join])
    NCH = 66
    sbd = [Buf(io['sb_d'][ci]) for ci in range(NCH)]
    ecd = [Buf(io['ecs_d'][ci]) for ci in range(NCH)]
    ypd = [Buf(io['yp_d'][c]) for c in range(64)]
    ctd = [Buf(io['ct_d'][c]) for c in range(64)]
    DTV = k.sb([128, NCH, 8], F32, es, "DTV")
    DA = k.sb([128, NCH, 8], F32, es, "DA")
    EA = k.sb([128, NCH, 8], F32, es, "EA")
    HDT = k.sb([128, NCH, 8], F32, es, "HDT")
    for ci in range(2):
        k.dma('sp', DTV[:, ci, :], zrows(io, ci, 0, 128, 768, 776), reads=[join], writes=[DTV])
    zx = io['zx']
    for g in range(4):
        src = bass.AP(zx.tensor, zx.offset + (256 + 16 * g) * 776 + 768, [[64 * 776, 128], [776, 16], [1, 8]])
        k.dma('sp', DTV[:, 2 + 16 * g:2 + 16 * (g + 1), :], src, reads=[join], writes=[DTV])

    def bc66(buf):
        b = buf[:, :]
        d = [list(x) for x in b.ap]
        return bass.AP(b.tensor, b.offset, [d[0], [0, NCH], d[1]])
    k.op('dve', lambda e: e.tensor_tensor(out=DTV[:, :, :], in0=DTV[:, :, :], in1=bc66(W['dtb']), op=ALU.add),
         [DTV, W['dtb']], [DTV])
    k.act(DTV[:, :, :], DTV[:, :, :], AF.Exp, [DTV], [DTV])
    k.act(DTV[:, :, :], DTV[:, :, :], AF.Ln, [DTV], [DTV], bias=1.0)
    k.op('dve', lambda e: e.tensor_tensor(out=DA[:, :, :], in0=DTV[:, :, :], in1=bc66(W['negA']), op=ALU.mult),
         [DTV, W['negA']], [DA])
    k.act(EA[:, :, :], DA[:, :, :], AF.Exp, [DA], [EA])
    k.op('dve', lambda e: e.tensor_scalar(out=HDT[:, :, :], in0=DTV[:, :, :], scalar1=0.5, scalar2=None, op0=ALU.mult),
         [DTV], [HDT])

    if SSD_LIMIT[0] < 1:
        return outs
    with k.scope() as s:
        tk = [ring(k, s, [128, 512], F32, 3, f"tk{kk}_") for kk in range(4)]
        pre = ring(k, s, [128, 512], F32, 2, "pre")
        th = ring(k, s, [128, 512], F32, 2, "sth")
        act_ = ring(k, s, [128, 512], F32, 2, "sact")
        xbf = ring(k, s, [128, 512], BF16, 2, "xbf")
        bct = ring(k, s, [128, 2, 128], BF16, 2, "bct")
        scT = ring(k, s, [128, 128], F32, 2, "scT")
        ecs = ring(k, s, [128, 16], F32, 2, "ecs")
        identz = [W['identzf'], W['identzb']]
        ident4 = W['ident4']
        LT8 = ring(k, s, [128, 8, 128], F32, 2, "LT8")
        MT = ring(k, s, [128, 8, 128], BF16, 2, "MT")
        xs = ring(k, s, [128, 2, 256], BF16, 2, "xs")
        xd = ring(k, s, [128, 2, 256], BF16, 2, "xd")
        yp = ring(k, s, [128, 256], F32, 2, "yp")
        tmp = ring(k, s, [128, 256], F32, 2, "ytmp")
        sbo = ring(k, s, [128, 256], F32, 2, "sbo")
        dst8 = ring(k, s, [128, 8], F32, 2, "dst8")
        hd8 = ring(k, s, [128, 8], F32, 2, "hd8")
        sfo = ring(k, s, [128, 256], F32, 2, "sfo")
        Hf = k.sb([128, 256], F32, s, "Hf")
        Hfb = k.sb([128, 256], BF16, s, "Hfb")
        Hs = k.sb([128, 256], F32, s, "Hs")
        k.op('pool', lambda e: e.memset(Hf[:, :], 0.0), [], [Hf])
        k.op('pool', lambda e: e.memset(Hfb[:, :], 0.0), [], [Hfb])
        psBC = k.ps([128, 2, 128], BF16, s, "psBC")
        psSc = k.ps([128, 128], F32, s, "psSc")
        psCS = k.ps([128, 16], F32, s, "psCS")
        psA = [k.ps([128, 512], F32, s, f"psA{i}") for i in range(2)]
        psY = k.ps([128, 256], F32, s, "psY")
        psYo = k.ps([128, 256], F32, s, "psYo")
        psS = k.ps([128, 512], F32, s, "psS")
        na_box = [0]

        pend = []

        def ssd_loads(cj):
            r3 = cj % 3
            has_prev = cj in (1,) or cj > 2
            has_next = cj == 0 or (2 <= cj < NCH - 1)
            for kk in range(4):
                sh = kk - 2
                t = tk[kk][r3]
                need_zero = (sh < 0 and not has_prev) or (sh > 0 and not has_next)
                if need_zero:
                    k.op('pool', lambda e: e.memset(t[:, :], 0.0), [], [t])
                a_, b_ = max(0, -sh), min(128, 128 - sh)
                k.dma('sp', t[a_:b_, :], zrows(io, cj, a_ + sh, b_ + sh, 256, 768), reads=[join], writes=[t])
                if sh < 0 and has_prev:
                    k.dma('sp', t[0:-sh, :], zrows(io, cj - 1, 128 + sh, 128, 256, 768), reads=[join], writes=[t], merge=True)
                if sh > 0 and has_next:
                    k.dma('sp', t[128 - sh:128, :], zrows(io, cj + 1, 0, sh, 256, 768), reads=[join], writes=[t], merge=True)

        def stage_a(ci):
            i2 = ci % 2
            lat = ci >= 2
            has_prev = ci in (1,) or ci > 2
            has_next = ci == 0 or (2 <= ci < NCH - 1)
            if ci + 1 < min(NCH, SSD_LIMIT[1]):
                ssd_loads(ci + 1)
            r3 = ci % 3
            yield
            scw, scb = W['scw'], W['scb']
            for kk in range(4):
                t = tk[kk][r3]
                k.op('pool', lambda e: e.tensor_tensor(out=t[:, :], in0=t[:, :], in1=scw[:, kk, :], op=ALU.mult),
                     [t, scw], [t])
                yield
            p = pre[i2]
            k.op('dve', lambda e: e.tensor_tensor(out=p[:, :], in0=tk[0][r3][:, :], in1=tk[1][r3][:, :], op=ALU.add),
                 [tk[0][r3], tk[1][r3]], [p])
            yield
            k.op('dve', lambda e: e.tensor_tensor(out=p[:, :], in0=p[:, :], in1=tk[2][r3][:, :], op=ALU.add),
                 [p, tk[2][r3]], [p])
            yield
            k.op('dve', lambda e: e.tensor_tensor(out=p[:, :], in0=p[:, :], in1=tk[3][r3][:, :], op=ALU.add),
                 [p, tk[3][r3]], [p])
            k.op('dve', lambda e: e.tensor_tensor(out=p[:, :], in0=p[:, :], in1=scb[:, :], op=ALU.add), [p, scb], [p])
            yield
            k.act(th[i2][:, :], p[:, :], AF.Tanh, [p], [th[i2]], scale=0.5)
            yield
            ac = act_[i2]
            k.op('dve', lambda e: e.scalar_tensor_tensor(out=ac[:, :], in0=th[i2][:, :], scalar=1.0, in1=p[:, :],
                                                         op0=ALU.add, op1=ALU.mult), [th[i2], p], [ac])
            yield
            xb = xbf[i2]
            k.act(xb[:, :], ac[:, :], AF.Copy, [ac], [xb], scale=0.5)
            yield
            k.tr(psBC[:, 0, :], xb[:, 256:384], identb[:, :], [xb, identb], [psBC], inc=False)
            k.tr(psBC[:, 1, :], xb[:, 384:512], identb[:, :], [xb, identb], [psBC], inc=True)
            yield
            bc = bct[i2]
            k.act(bc[:, :, :], psBC[:, :, :], AF.Copy, [psBC], [bc])
            yield
            if lat:
                k.mm(psSc[:, :], bc[:, 0, :], bc[:, 1, :], True, True, [bc], [psSc])
                k.act(scT[i2][:, :], psSc[:, :], AF.Copy, [psSc], [scT[i2]])
                pend.append(lambda ci=ci: k.dma('pool', ctd[ci - 2][:, :], bc[:, 1, :], reads=[bc], writes=[ctd[ci - 2]]))
            k.mm(psCS[:, 0:4], triu[:, :], DA[:, ci, 0:4], True, True, [triu, DA], [psCS])
            k.mm(psCS[:, 4:8], tril[:, :], DA[:, ci, 4:8], True, True, [tril, DA], [psCS])
            k.mm(psCS[:, 8:16], ones[:, :], DA[:, ci, 0:8], True, True, [ones, DA], [psCS])
            yield
            ec = ecs[i2]
            k.act(ec[:, :], psCS[:, :], AF.Exp, [psCS], [ec])
            pend.append(lambda ci=ci: k.dma('pool', ecd[ci][:, :], ec[:, :], reads=[ec], writes=[ecd[ci]]))

        gen_box = [None]

        def advance(n=1):
            for _ in range(n):
                if gen_box[0] is None:
                    return
                try:
                    next(gen_box[0])
                except StopIteration:
                    gen_box[0] = None

        def stage_b(ci):
            i2 = ci % 2
            lat = ci >= 2
            ac = act_[i2]
            mt = MT[i2]
            ltb = LT8[i2]
            if lat:
                for d in range(2):
                    for h in range(4):
                        col = d * 4 + h
                        k.act(xs[i2][:, d, h * 64:(h + 1) * 64], ac[:, h * 64:(h + 1) * 64], AF.Copy, [ac, HDT], [xs[i2]],
                              scale=HDT[:, ci, col:col + 1])
                    advance()
            for d in range(2):
                pa = psA[d]
                idz = identz[d]
                for h in range(4):
                    col = d * 4 + h
                    ecol = EA[:, ci, col:col + 1]
                    dd = [list(x) for x in ecol.ap]
                    lhsT = bass.AP(ecol.tensor, ecol.offset, [dd[0], [0, 128]])
                    k.mm(pa[:, h * 128:(h + 1) * 128], lhsT, idz[:, :], True, True, [EA, idz], [pa], inc=(h == 3))
                advance()
                lt4 = ltb[:, d * 4:(d + 1) * 4, :].rearrange("p h t -> p (h t)")
                if d == 0:
                    k.op('dve', lambda e: e.tensor_tensor_scan(out=lt4, data0=pa[:, :], data1=ident4[:, :],
                                                               initial=0.0, op0=ALU.mult, op1=ALU.add),
                         [pa, ident4], [ltb])
                else:
                    k.op('dve', lambda e: e.tensor_tensor_scan(out=lt4[:, ::-1], data0=pa[:, ::-1], data1=ident4[:, ::-1],
                                                               initial=0.0, op0=ALU.mult, op1=ALU.add),
                         [pa, ident4], [ltb])
                advance()
                if lat:
                    sc_ = scT[i2][:, :]
                    sd = [list(x) for x in sc_.ap]
                    scb4 = bass.AP(sc_.tensor, sc_.offset, [sd[0], [0, 4], sd[1]])
                    k.op('dve', lambda e: e.tensor_tensor(out=mt[:, d * 4:(d + 1) * 4, :], in0=ltb[:, d * 4:(d + 1) * 4, :],
                                                          in1=scb4, op=ALU.mult), [ltb, scT[i2]], [mt])
                k.op('pool', lambda e: e.tensor_tensor(out=hd8[i2][:, d * 4:d * 4 + 4], in0=HDT[:, ci, d * 4:d * 4 + 4],
                                                       in1=ltb[:, d * 4:d * 4 + 4, (127 if d == 0 else 0)], op=ALU.mult),
                     [HDT, ltb], [hd8[i2]])
                advance()
                for h in range(4):
                    col = d * 4 + h
                    k.act(xd[i2][:, d, h * 64:(h + 1) * 64], ac[:, h * 64:(h + 1) * 64], AF.Copy, [ac, hd8[i2]], [xd[i2]],
                          scale=hd8[i2][:, col:col + 1])
                advance()

        def stage_c(ci):
            i2 = ci % 2
            lat = ci >= 2
            ac = act_[i2]
            xb = xbf[i2]
            bc = bct[i2]
            ec = ecs[i2]
            mt = MT[i2]
            k.mm(psS[:, 0:256], xb[:, 256:384], xd[i2][:, 0, :], True, True, [xb, xd[i2]], [psS])
            k.mm(psS[:, 256:512], xb[:, 256:384], xd[i2][:, 1, :], True, True, [xb, xd[i2]], [psS])
            advance()
            so = sbo[i2]
            k.act(so[:, :], psS[:, 256:512], AF.Copy, [psS], [so])
            pend.append(lambda ci=ci: k.dma('pool', sbd[ci][:, :], so[:, :], reads=[so], writes=[sbd[ci]]))
            if lat:
                for h in range(4):
                    k.mm(psY[:, h * 64:(h + 1) * 64], mt[:, h, :], xs[i2][:, 0, h * 64:(h + 1) * 64], True, False,
                         [mt, xs[i2]], [psY], inc=False)
                    k.mm(psY[:, h * 64:(h + 1) * 64], mt[:, 4 + h, :], xs[i2][:, 1, h * 64:(h + 1) * 64], False, True,
                         [mt, xs[i2]], [psY], inc=(h == 3))
                k.mm(psYo[:, :], bc[:, 1, :], Hfb[:, :], True, True, [bc, Hfb], [psYo])
                y = yp[i2]
                tm = tmp[i2]
                k.act(y[:, :], psY[:, :], AF.Copy, [psY], [y])
                k.op('dve', lambda e: e.tensor_tensor(out=tm[:, :].rearrange("p (h c) -> p c h", c=64),
                                                      in0=psYo[:, :].rearrange("p (h c) -> p c h", c=64),
                                                      in1=ap3(ec[:, 0:4], 64), op=ALU.mult), [psYo, ec], [tm])
                k.op('pool', lambda e: e.tensor_tensor(out=y[:, :], in0=y[:, :], in1=tm[:, :], op=ALU.add), [y, tm], [y])
                k.op('dve', lambda e: e.tensor_tensor(out=tm[:, :].rearrange("p (h c) -> p c h", c=64),
                                                      in0=ac[:, 0:256].rearrange("p (h c) -> p c h", c=64),
                                                      in1=ap3(W['hD'][:, 0:4], 64), op=ALU.mult), [ac, W['hD']], [tm])
                k.op('pool', lambda e: e.tensor_tensor(out=y[:, :], in0=y[:, :], in1=tm[:, :], op=ALU.add), [y, tm], [y])
                pend.append(lambda ci=ci: k.dma('pool', ypd[ci - 2][:, :], y[:, :], reads=[y], writes=[ypd[ci - 2]]))
            advance()
            k.op('dve', lambda e: e.tensor_tensor(out=Hs[:, :].rearrange("p (h c) -> p c h", c=64),
                                                  in0=Hf[:, :].rearrange("p (h c) -> p c h", c=64),
                                                  in1=ap3(ec[:, 8:12], 64), op=ALU.mult), [Hf, ec], [Hs])
            k.act(sfo[i2][:, :], psS[:, 0:256], AF.Copy, [psS], [sfo[i2]])
            k.op('dve', lambda e: e.tensor_tensor(out=Hf[:, :], in0=sfo[i2][:, :], in1=Hs[:, :], op=ALU.add),
                 [Hs, sfo[i2]], [Hf])
            k.act(Hfb[:, :], Hf[:, :], AF.Copy, [Hf], [Hfb])


        NC1 = min(NCH, SSD_LIMIT[1])
        ssd_loads(0)
        for _ in stage_a(0):
            pass
        for f in pend:
            f()
        pend.clear()
        for ci in range(NC1):
            gen_box[0] = stage_a(ci + 1) if ci + 1 < NC1 else None
            stage_b(ci)
            stage_c(ci)
            while gen_box[0] is not None:
                advance()
            for f in pend:
                f()
            pend.clear()
    if SSD_LIMIT[0] < 2:
        return outs
    with k.scope() as s:
        Hb = k.sb([128, 256], F32, s, "Hb")
        Hbb = k.sb([128, 256], BF16, s, "Hbb")
        k.op('pool', lambda e: e.memset(Hb[:, :], 0.0), [], [Hb])
        k.op('pool', lambda e: e.memset(Hbb[:, :], 0.0), [], [Hbb])
        NR = 4
        sbi = ring(k, s, [128, 256], F32, NR, "sbi")
        eci = ring(k, s, [128, 16], F32, NR, "eci")
        ypi = ring(k, s, [128, 256], F32, NR, "ypi")
        cti = ring(k, s, [128, 128], BF16, NR, "cti")
        zi = ring(k, s, [128, 256], F32, NR, "zi")
        zth = ring(k, s, [128, 256], F32, 2, "zth")
        tm2 = ring(k, s, [128, 256], F32, 2, "tm2")
        vo = ring(k, s, [128, 256], BF16, 2, "vo")
        Hs2 = k.sb([128, 256], F32, s, "Hs2")
        psB = [k.ps([128, 256], F32, s, f"psB{i}") for i in range(2)]
        order = [1, 0] + list(range(NCH - 1, 1, -1))

        def d2_load(oi):
            ci = order[oi]
            r = oi % NR
            k.dma('sp', sbi[r][:, :], sbd[ci][:, :], reads=[sbd[ci]], writes=[sbi[r]])
            k.dma('sp', eci[r][:, :], ecd[ci][:, :], reads=[ecd[ci]], writes=[eci[r]])
            if ci >= 2:
                c = ci - 2
                k.dma('sp', ypi[r][:, :], ypd[c][:, :], reads=[ypd[c]], writes=[ypi[r]])
                k.dma('sp', cti[r][:, :], ctd[c][:, :], reads=[ctd[c]], writes=[cti[r]])
                k.dma('sp', zi[r][:, :], zrows(io, ci, 0, 128, 0, 256), reads=[join], writes=[zi[r]])

        d2_load(0)
        d2_load(1)
        for oi, ci in enumerate(order):
            if oi + 2 < len(order):
                d2_load(oi + 2)
            i2 = oi % 2
            r = oi % NR
            lat = ci >= 2
            sb_ = sbi[r]
            ec = eci[r]
            if lat:
                c = ci - 2
                y = ypi[r]
                ct = cti[r]
                z = zi[r]
                pb = psB[i2]
                k.mm(pb[:, :], ct[:, :], Hbb[:, :], True, True, [ct, Hbb], [pb])
            k.op('dve', lambda e: e.tensor_tensor(out=Hs2[:, :].rearrange("p (h c) -> p c h", c=64),
                                                  in0=Hb[:, :].rearrange("p (h c) -> p c h", c=64),
                                                  in1=ap3(ec[:, 12:16], 64), op=ALU.mult), [Hb, ec], [Hs2])
            k.op('dve', lambda e: e.tensor_tensor(out=Hbb[:, :], in0=Hs2[:, :], in1=sb_[:, :], op=ALU.add),
                 [Hs2, sb_], [Hbb])
            k.op('pool', lambda e: e.tensor_tensor(out=Hb[:, :], in0=Hs2[:, :], in1=sb_[:, :], op=ALU.add),
                 [Hs2, sb_], [Hb])
            if lat:
                tm = tm2[i2]
                k.op('dve', lambda e: e.tensor_tensor(out=tm[:, :].rearrange("p (h c) -> p c h", c=64),
                                                      in0=pb[:, :].rearrange("p (h c) -> p c h", c=64),
                                                      in1=ap3(ec[:, 4:8], 64), op=ALU.mult), [pb, ec], [tm])
                k.op('pool', lambda e: e.tensor_tensor(out=y[:, :], in0=y[:, :], in1=tm[:, :], op=ALU.add), [y, tm], [y])
                k.act(zth[i2][:, :], z[:, :], AF.Tanh, [z], [zth[i2]], scale=0.5)
                k.op('dve', lambda e: e.scalar_tensor_tensor(out=z[:, :], in0=zth[i2][:, :], scalar=1.0, in1=z[:, :],
                                                             op0=ALU.add, op1=ALU.mult), [zth[i2], z], [z])
                v = vo[i2]
                k.op('dve', lambda e: e.scalar_tensor_tensor(out=v[:, :], in0=y[:, :], scalar=0.5, in1=z[:, :],
                                                             op0=ALU.mult, op1=ALU.mult), [y, z], [v])
                dst = Buf(bass.AP(io['os'].tensor, io['os'].offset + c * 256, [[64 * 256, 128], [1, 256]]))
                k.dma('pool', dst.t, v[:, :], reads=[v], writes=[dst])
                outs.append(dst)
    return outs


def build_phase1(nc, k, es, io, G, stages=3):
    W0 = p1_shared(k, es, io)
    htd = p1_prework(k, es, io, W0)
    outs = []
    for q in range(G):
        k.sfx = f"_g{q}"
        iog = io_group(io, q)
        with k.scope() as sg:
            W = p1_group(k, sg, iog, W0)
            lug, zxb = p1_inproj(k, sg, iog, W, htd)
            if stages < 3:
                outs += [b_ for r in lug for b_ in r] + list(zxb.values())
            if stages >= 2:
                outs += p1_lru(k, sg, iog, W, lug)
            if stages >= 3:
                outs += p1_ssd(k, sg, iog, W, zxb)
    k.sfx = ""
    return outs


def make_phase1_nc(stages=3):
    nc = bass.Bass("TRN2", target_bir_lowering=False)
    io = p1_decl(nc, 1, False)
    with ExitStack() as es:
        es.enter_context(nc.allow_non_contiguous_dma("small strided parameter loads"))
        k = KB(nc, es)
        outs = build_phase1(nc, k, es, io, 1, stages)
        k.finish(outs)
    return nc


def p1_group_arrays(inp, q):
    w_in = inp['w_in'][0]
    g = q // 2
    cs = slice(q * 256, (q + 1) * 256)
    cols = np.concatenate([np.arange(q * 256, (q + 1) * 256), 1024 + np.arange(q * 256, (q + 1) * 256),
                           2048 + np.arange(q * 256, (q + 1) * 256), 3072 + np.arange(q * 256, (q + 1) * 256),
                           4096 + g * 128 + np.arange(128), 4352 + g * 128 + np.arange(128),
                           4608 + 4 * q + np.arange(4), 4624 + 4 * q + np.arange(4)])
    wbd = np.zeros((8, 128, 128), np.float32)
    for d in range(2):
        for gi, nm in enumerate(('lru_wa', 'lru_wx')):
            for blk in range(2):
                for hh in range(2):
                    head = 4 * q + 2 * blk + hh
                    wbd[d * 4 + gi * 2 + blk, hh * 64:(hh + 1) * 64, hh * 64:(hh + 1) * 64] = inp[nm][0, d, head]
    ccols = np.concatenate([np.arange(q * 256, (q + 1) * 256), 1024 + g * 128 + np.arange(128),
                            1280 + g * 128 + np.arange(128)])
    hs = slice(4 * q, 4 * q + 4)
    return {
        'w_in1': w_in[:, cols],
        'l_cw': inp['lru_conv_w'][0][:, cs], 'l_cb': inp['lru_conv_b'][0][cs],
        'l_wbd': wbd,
        'l_ba': inp['lru_ba'][0][:, cs], 'l_bx': inp['lru_bx'][0][:, cs], 'l_lam': inp['lru_lambda'][0][:, cs],
        's_cw': inp['ssd_conv_w'][0][:, ccols], 's_cb': inp['ssd_conv_b'][0][ccols],
        's_alog': inp['ssd_a_log'][0][:, hs].reshape(8), 's_dtb': inp['ssd_dt_bias'][0][:, hs].reshape(8),
        's_d': inp['ssd_d'][0][hs],
    }


def p1_common_arrays(inp, b, fused):
    d = {
        'x1': np.ascontiguousarray(inp['x'][b]), 'ctx1': np.ascontiguousarray(inp['ctx'][b]),
        'c_b': np.ascontiguousarray(inp['c'][b]), 'c_ctx': np.ascontiguousarray(inp['c_ctx']),
        'norm1_g': np.ascontiguousarray(inp['norm1_g'][0]),
        'ident': np.eye(128, dtype=np.float32),
        'triu': np.triu(np.ones((128, 128), np.float32)),
        'tril': np.tril(np.ones((128, 128), np.float32)),
        'ones': np.ones((128, 128), np.float32),
        'identzf': np.diag(np.r_[0.0, np.ones(127)]).astype(np.float32),
        'identzb': np.diag(np.r_[np.ones(127), 0.0]).astype(np.float32),
    }
    if fused:
        d['ada_w'] = np.ascontiguousarray(inp['ada_w'][0])
        d['ada_b'] = np.ascontiguousarray(inp['ada_b'][0])
    else:
        d['ada_w1'] = np.ascontiguousarray(inp['ada_w'][0][:, 0:2048])
        d['ada_b1'] = np.ascontiguousarray(inp['ada_b'][0][0:2048])
    return d


def p1_inputs(inp):
    maps = []
    for b in range(2):
        for q in range(4):
            m = p1_common_arrays(inp, b, False)
            for n, v in p1_group_arrays(inp, q).items():
                m[n] = np.ascontiguousarray(v[None])
            maps.append(m)
    return maps


def fused_inputs(inp):
    groups = [p1_group_arrays(inp, q) for q in range(4)]
    stacked = {n: np.ascontiguousarray(np.stack([g[n] for g in groups], 0)) for n in groups[0]}
    maps = []
    for b in range(2):
        for kq in range(4):
            m = p1_common_arrays(inp, b, True)
            m.update(stacked)
            t0 = kq * NT2
            rows = list(range(t0, t0 + NT2)) + [max(t0 - 1, 0), min(t0 + NT2, 8191)]
            hm = np.ones((128, 2), np.float32)
            if kq == 0:
                hm[:, 0] = 0.0
            if kq == 3:
                hm[:, 1] = 0.0
            sel = np.zeros((128, 4), np.float32)
            sel[:, kq] = 1.0
            m.update({
                'x2': np.ascontiguousarray(inp['x'][b][rows]), 'hmask': hm, 'sel': sel,
                'final_g': np.ascontiguousarray(inp['final_norm_g']),
                'norm2_g': np.ascontiguousarray(inp['norm2_g'][0]),
                'ssd_norm_g': np.ascontiguousarray(inp['ssd_norm_g'][0]),
                'w_out': np.ascontiguousarray(inp['w_out'][0]),
                'w_up': np.ascontiguousarray(inp['ffn_w_up'][0]),
                'w_down': np.ascontiguousarray(inp['ffn_w_down'][0]),
                'ffn_cw': np.ascontiguousarray(inp['ffn_conv_w'][0]),
                'ffn_cb': np.ascontiguousarray(inp['ffn_conv_b'][0]),
            })
            maps.append(m)
    return maps


def make_fused_nc():
    nc = bass.Bass("TRN2", target_bir_lowering=False)
    io1 = p1_decl(nc, 4, True)
    io2 = p2_decl(nc, io1)
    with ExitStack() as es:
        es.enter_context(nc.allow_non_contiguous_dma("small strided parameter loads"))
        k = KB(nc, es)
        with k.scope() as s1:
            outs1 = build_phase1(nc, k, s1, io1, 4)
        k.sfx = "_p2"
        pjoin = k.sb([128, 1], F32, es, "pjoin")
        k.op('pool', lambda e: e.memset(pjoin[:, :], 0.0), outs1, [pjoin])
        sel = k.sb([128, 4], F32, es, "sel")
        k.dma('sp', sel[:, :], io2['sel'], writes=[sel])
        outs = build_phase2(nc, k, es, io2, fz=(io1['ol'], io1['os'], pjoin, sel))
        k.finish(outs)
    return nc


def kernel_2launch(inp):
    nc1 = make_phase1_nc()
    res1 = run_bass_kernel_spmd(nc1, p1_inputs(inp), core_ids=list(range(8)))
    r0 = res1.results[0]['ol']
    mixl = np.zeros((2, 8192, 1024), r0.dtype)
    mixs = np.zeros((2, 8192, 1024), r0.dtype)
    for b in range(2):
        for q in range(4):
            r = res1.results[b * 4 + q]
            mixl[b][:, q * 256:(q + 1) * 256] = r['ol'][0]
            mixs[b][:, q * 256:(q + 1) * 256] = r['os'][0]
    nc2 = make_phase2_nc()
    res2 = run_bass_kernel_spmd(nc2, p2_inputs(inp, mixl, mixs), core_ids=list(range(8)))
    return np.stack([np.concatenate([res2.results[b * 4 + kq]['out2'] for kq in range(4)], 0) for b in range(2)])


def kernel(**inputs):
    inp = {k_: np.asarray(v) for k_, v in inputs.items()}
    nc = make_fused_nc()
    res = run_bass_kernel_spmd(nc, fused_inputs(inp), core_ids=list(range(8)))
    out = np.stack([np.concatenate([res.results[b * 4 + kq]['out2'] for kq in range(4)], 0) for b in range(2)])
    return out.astype(np.float32)
```
